# Optimizing a Trainium2 kernel written in Bass

```python
import math
import jax, jax.numpy as jnp
from jax import lax
import numpy as np

D_MODEL = 1024
BATCH = 4
SEQ = 4096
DEPTH = 1

POOL_WIDTH = D_MODEL // 2
POOL_GROUPS = 4
POOL_GROUP_DIM = POOL_WIDTH // POOL_GROUPS
POOL_WINDOWS = (2, 4, 8, 16)
ATTN_WIDTH = D_MODEL - POOL_WIDTH
N_DIFF_HEADS = 4
DIFF_HEAD_DIM = ATTN_WIDTH // (2 * N_DIFF_HEADS)
DIFF_V_DIM = 2 * DIFF_HEAD_DIM
IN_PROJ_WIDTH = POOL_WIDTH + 3 * ATTN_WIDTH
Q_BLOCK = 128
NUM_BUCKETS = 32
MAX_DISTANCE = 128
D_FF = 2816
CONV_WIDTH = 3
NORM_EPS = 1e-6
SUBLN_EPS = 1e-5
NEG_INF = -1e30

kernel_name = "hybrid_pool_diffattn_convffn"


def rms_norm(x, g, eps):
    xf = x.astype(jnp.float32)
    y = xf * lax.rsqrt(jnp.mean(xf * xf, axis=-1, keepdims=True) + eps)
    return (y * g.astype(jnp.float32)).astype(x.dtype)


def lambda_init_fn(layer_idx):
    return 0.8 - 0.6 * math.exp(-0.3 * layer_idx)


def t5_causal_bucket(q_pos, k_pos):
    n = jnp.maximum(q_pos[:, None] - k_pos[None, :], 0)
    max_exact = NUM_BUCKETS // 2
    nf = jnp.maximum(n, 1).astype(jnp.float32)
    large = max_exact + (jnp.log(nf / max_exact) / math.log(MAX_DISTANCE / max_exact)
                         * (NUM_BUCKETS - max_exact)).astype(jnp.int32)
    large = jnp.minimum(large, NUM_BUCKETS - 1)
    return jnp.where(n < max_exact, n, large)


def causal_window_mean(u, w):
    S = u.shape[1]
    cs = jnp.cumsum(u.astype(jnp.float32), axis=1)
    prev = jnp.pad(cs, ((0, 0), (w, 0), (0, 0)))[:, :S]
    cnt = jnp.minimum(jnp.arange(1, S + 1), w).astype(jnp.float32)[None, :, None]
    return ((cs - prev) / cnt).astype(u.dtype)


def pooling_mixer(zp, pool_w, pool_scale):
    B, S, _ = zp.shape
    zg = zp.reshape(B, S, POOL_GROUPS, POOL_GROUP_DIM)
    pooled = jnp.stack(
        [causal_window_mean(zg[:, :, gi], w) - zg[:, :, gi] for gi, w in enumerate(POOL_WINDOWS)],
        axis=2)
    y = jnp.einsum('bsgc,gcd->bsgd', pooled, pool_w.astype(zp.dtype)).reshape(B, S, POOL_WIDTH)
    return y * pool_scale.astype(zp.dtype)


def diff_attention(q, k, v, lam, rel_bias):
    B, S = q.shape[0], q.shape[1]
    nb = S // Q_BLOCK
    scale = DIFF_HEAD_DIM ** -0.5
    qb = q.reshape(B, nb, Q_BLOCK, N_DIFF_HEADS, 2, DIFF_HEAD_DIM).transpose(1, 0, 2, 3, 4, 5)
    k_pos = jnp.arange(S)

    def one_block(args):
        i, qi = args
        q_pos = i * Q_BLOCK + jnp.arange(Q_BLOCK)
        bias = jnp.transpose(rel_bias[t5_causal_bucket(q_pos, k_pos)], (2, 0, 1)).astype(jnp.float32)
        s = jnp.einsum('bqhcd,bkhcd->bhcqk', qi, k).astype(jnp.float32) * scale + bias[None, :, None]
        mask = k_pos[None, :] <= q_pos[:, None]
        s = jnp.where(mask, s, NEG_INF)
        p = jax.nn.softmax(s, axis=-1)
        a = (p[:, :, 0] - lam * p[:, :, 1]).astype(v.dtype)
        return jnp.einsum('bhqk,bkhe->bqhe', a, v)

    out = lax.map(one_block, (jnp.arange(nb), qb))
    return out.transpose(1, 0, 2, 3, 4).reshape(B, S, N_DIFF_HEADS, DIFF_V_DIM)


def causal_depthwise_conv(g, w, b):
    C = g.shape[-1]
    y = lax.conv_general_dilated(
        g, w.astype(g.dtype)[:, None, :], window_strides=(1,), padding=[(CONV_WIDTH - 1, 0)],
        dimension_numbers=('NWC', 'WIO', 'NWC'), feature_group_count=C)
    return y + b.astype(g.dtype)


def setup_inputs(seed: int = 0) -> dict:
    key = jax.random.key(seed)
    ks = jax.random.split(key, 20)
    f32 = jnp.float32
    nrm = lambda k, shape, s: jax.random.normal(k, shape, f32) * s
    return {
        "x": jax.random.normal(ks[0], (BATCH, SEQ, D_MODEL), f32),
        "norm_mix_g": 1.0 + nrm(ks[1], (DEPTH, D_MODEL), 0.02),
        "w_in": nrm(ks[2], (DEPTH, D_MODEL, IN_PROJ_WIDTH), D_MODEL ** -0.5),
        "pool_w": nrm(ks[3], (DEPTH, POOL_GROUPS, POOL_GROUP_DIM, POOL_GROUP_DIM), POOL_GROUP_DIM ** -0.5),
        "pool_scale": 1.0 + nrm(ks[4], (DEPTH, POOL_WIDTH), 0.1),
        "lambda_q1": nrm(ks[5], (DEPTH, DIFF_HEAD_DIM), 0.1),
        "lambda_k1": nrm(ks[6], (DEPTH, DIFF_HEAD_DIM), 0.1),
        "lambda_q2": nrm(ks[7], (DEPTH, DIFF_HEAD_DIM), 0.1),
        "lambda_k2": nrm(ks[8], (DEPTH, DIFF_HEAD_DIM), 0.1),
        "subln_g": 1.0 + nrm(ks[9], (DEPTH, DIFF_V_DIM), 0.02),
        "rel_bias": nrm(ks[10], (NUM_BUCKETS, N_DIFF_HEADS), 0.5),
        "w_out": nrm(ks[11], (DEPTH, D_MODEL, D_MODEL), D_MODEL ** -0.5),
        "norm_ffn_g": 1.0 + nrm(ks[12], (DEPTH, D_MODEL), 0.02),
        "ffn_w_in": nrm(ks[13], (DEPTH, D_MODEL, 2 * D_FF), D_MODEL ** -0.5),
        "ffn_conv_w": nrm(ks[14], (DEPTH, CONV_WIDTH, D_FF), CONV_WIDTH ** -0.5),
        "ffn_conv_b": nrm(ks[15], (DEPTH, D_FF), 0.01),
        "ffn_w_out": nrm(ks[16], (DEPTH, D_FF, D_MODEL), D_FF ** -0.5),
        "norm_final_g": 1.0 + nrm(ks[17], (D_MODEL,), 0.02),
    }


def reference(x, norm_mix_g, w_in, pool_w, pool_scale, lambda_q1, lambda_k1, lambda_q2,
              lambda_k2, subln_g, rel_bias, w_out, norm_ffn_g, ffn_w_in, ffn_conv_w,
              ffn_conv_b, ffn_w_out, norm_final_g):
    B, S, _ = x.shape
    f32 = jnp.float32
    for l in range(DEPTH):
        lam_init = lambda_init_fn(l)
        h = rms_norm(x, norm_mix_g[l], NORM_EPS)
        z = h @ w_in[l].astype(h.dtype)
        zp = z[..., :POOL_WIDTH]
        q = z[..., POOL_WIDTH:POOL_WIDTH + ATTN_WIDTH].reshape(B, S, N_DIFF_HEADS, 2, DIFF_HEAD_DIM)
        k = z[..., POOL_WIDTH + ATTN_WIDTH:POOL_WIDTH + 2 * ATTN_WIDTH].reshape(B, S, N_DIFF_HEADS, 2, DIFF_HEAD_DIM)
        v = z[..., POOL_WIDTH + 2 * ATTN_WIDTH:].reshape(B, S, N_DIFF_HEADS, DIFF_V_DIM)

        y_pool = pooling_mixer(zp, pool_w[l], pool_scale[l])

        lam = (jnp.exp(jnp.sum(lambda_q1[l].astype(f32) * lambda_k1[l].astype(f32)))
               - jnp.exp(jnp.sum(lambda_q2[l].astype(f32) * lambda_k2[l].astype(f32)))
               + lam_init)
        o = diff_attention(q, k, v, lam, rel_bias)
        o = rms_norm(o, subln_g[l], SUBLN_EPS) * (1.0 - lam_init)
        y_attn = o.reshape(B, S, ATTN_WIDTH)

        y = jnp.concatenate([y_pool, y_attn], axis=-1)
        x = x + y @ w_out[l].astype(y.dtype)

        h = rms_norm(x, norm_ffn_g[l], NORM_EPS)
        gu = h @ ffn_w_in[l].astype(h.dtype)
        g = causal_depthwise_conv(gu[..., :D_FF], ffn_conv_w[l], ffn_conv_b[l])
        u = gu[..., D_FF:]
        x = x + (jax.nn.silu(g) * u) @ ffn_w_out[l].astype(h.dtype)
    return rms_norm(x, norm_final_g, NORM_EPS)
```

```python
import contextlib
import math
import numpy as np
import concourse.bass as bass
import concourse.mybir as mybir
from concourse.bass_utils import run_bass_kernel_spmd

F32 = mybir.dt.float32
BF16 = mybir.dt.bfloat16
ALU = mybir.AluOpType
AF = mybir.ActivationFunctionType

D = 1024
S = 4096
NB = 4
DFF = 2816
NJ = DFF // 128
NEGM = -30000.0
LAM_INIT = 0.8 - 0.6 * math.exp(0.0)


class Prog:
    ENGS = ['pe', 'act', 'dve', 'pool', 'sp']

    def __init__(self, nc):
        self.nc = nc
        self.streams = {e: [] for e in self.ENGS}
        self.count = {e: 0 for e in self.ENGS}
        self.seen = {e: {} for e in self.ENGS}
        self.lastw = {}
        self.readers = {}
        self.dma_count = {}

    def _collect(self, eng, reads, writes):
        need = {}

        def add(tok, kind):
            if tok is None:
                return
            k, v = tok
            if k == eng and eng == 'pe':
                return
            if need.get(k, 0) < v:
                need[k] = v
        for b in reads:
            add(self.lastw.get(b), 'raw')
        for b in writes:
            add(self.lastw.get(b), 'waw')
            for k, v in self.readers.get(b, {}).items():
                add((k, v), 'war')
        waits = []
        for k, v in need.items():
            if self.seen[eng].get(k, 0) >= v:
                continue
            self.seen[eng][k] = v
            waits.append((k, v))
        return waits

    def _update(self, tok, reads, writes):
        k, v = tok
        for b in reads:
            r = self.readers.setdefault(b, {})
            if r.get(k, 0) < v:
                r[k] = v
        for b in writes:
            self.lastw[b] = tok
            self.readers[b] = {}

    def op(self, eng, fn, reads=(), writes=()):
        waits = self._collect(eng, reads, writes)
        self.count[eng] += 1
        tok = (eng, self.count[eng])
        self.streams[eng].append((waits, fn, (eng, 1)))
        self._update(tok, reads, writes)

    def dma(self, q, key, fn, reads=(), writes=()):
        waits = self._collect(q, reads, writes)
        dk = 'dma_' + key
        self.dma_count[dk] = self.dma_count.get(dk, 0) + 16
        tok = (dk, self.dma_count[dk])
        self.streams[q].append((waits, fn, (dk, 16)))
        self._update(tok, reads, writes)

    def alias(self, new_keys, old_prefixes):
        olds = [k for k in set(list(self.lastw.keys()) + list(self.readers.keys()))
                if any(k == p or k.startswith(p + ':') for p in old_prefixes)]
        for new in new_keys:
            r = self.readers.setdefault(new, {})
            for o in olds:
                w = self.lastw.get(o)
                if w is not None and r.get(w[0], 0) < w[1]:
                    r[w[0]] = w[1]
                for k, v in self.readers.get(o, {}).items():
                    if r.get(k, 0) < v:
                        r[k] = v

    def finish(self, eng, bufs):
        waits = self._collect(eng, bufs, ())
        self.streams[eng].append((waits, None, None))

    def emit(self, ctx):
        nc = self.nc
        keys = list(self.ENGS) + sorted(self.dma_count.keys())
        sems = {k: ctx.enter_context(nc.semaphore('s_' + k)) for k in keys}
        block = ctx.enter_context(nc.Block())
        streams = self.streams

        def run(engname, e):
            for waits, fn, inc in streams[engname]:
                for k, v in waits:
                    e.wait_ge(sems[k], v)
                if fn is not None:
                    fn(e).then_inc(sems[inc[0]], inc[1])

        @block.tensor
        def _(e):
            run('pe', e)

        @block.scalar
        def _(e):
            run('act', e)

        @block.vector
        def _(e):
            run('dve', e)

        @block.gpsimd
        def _(e):
            run('pool', e)

        @block.sync
        def _(e):
            run('sp', e)


def build_nc():
    nc = bass.Bass("TRN2", target_bir_lowering=False)

    def din(name, shape):
        return nc.dram_tensor(name, list(shape), F32, kind="ExternalInput").ap()

    xin = din("xin", [S, D])
    w_in = din("w_in", [D, 2048])
    w_out = din("w_out", [D, D])
    wgu = din("wgu", [NJ, 128, 8 * 256])
    w_fo = din("w_fo", [DFF, D])
    pool_w = din("pool_w", [4, 128, 128])
    g1t = din("g1t", [128, 8])
    g2 = din("g2", [1, D])
    gf = din("gf", [1, D])
    subg = din("subg", [1, 128])
    lamv = din("lamv", [1, 256])
    pscale = din("pscale", [128, 4])
    convw = din("convw", [128, NJ * 3])
    convb = din("convb", [128, NJ])
    nearb = din("nearb", [128, 4 * 128])
    diagb = din("diagb", [128, 4 * 128])
    maskd = din("maskd", [128, 128])
    b31 = din("b31", [1, 4])
    cflag = din("cflag", [128, 1])
    oflag = din("oflag", [128, 1])
    corr = din("corr", [128, 4 * 16])
    identd = din("identd", [128, 128])
    yout = nc.dram_tensor("yout", [2048, D], F32, kind="ExternalOutput").ap()

    base = (nc._sbuf_addr_for_side(None) + 63) // 64 * 64
    top = nc._sbuf_addr_for_side('right')
    cur = [base]
    names = [0]
    offs = {}

    def alloc(shape, dt, name=None):
        nbytes = int(np.prod(shape[1:])) * (4 if dt == F32 else 2)
        names[0] += 1
        offs[name] = cur[0]
        t = nc.alloc_sbuf_tensor_at((name or 't') + str(names[0]), list(shape), dt, offset=cur[0])
        cur[0] += (nbytes + 63) // 64 * 64
        assert cur[0] <= top, ("SBUF overflow", cur[0], top)
        return t

    P = Prog(nc)
    ev = [0]
    force_act = [0]

    def evac(out, in_, reads, writes, scale=None):
        ev[0] += 1
        use_act = (ev[0] % 2 == 0)
        if force_act[0] > 0:
            force_act[0] -= 1
            use_act = True
        if use_act:
            if scale is None:
                P.op('act', lambda e: e.activation(out=out, in_=in_, func=AF.Copy), reads, writes)
            else:
                P.op('act', lambda e: e.activation(out=out, in_=in_, func=AF.Copy, scale=scale), reads, writes)
        else:
            if scale is None:
                P.op('dve', lambda e: e.tensor_copy(out=out, in_=in_), reads, writes)
            else:
                P.op('dve', lambda e: e.tensor_scalar(out=out, in0=in_, scalar1=scale, scalar2=None, op0=ALU.mult), reads, writes)

    def mm(out, lhsT, rhs, start, stop, reads, writes, skip=False):
        if skip:
            P.op('pe', lambda e: e.matmul(out, lhsT=lhsT, rhs=rhs, start=start, stop=stop, skip_group_check=True), reads, writes)
        else:
            P.op('pe', lambda e: e.matmul(out, lhsT=lhsT, rhs=rhs, start=start, stop=stop), reads, writes)

    with contextlib.ExitStack() as ctx:
        ps = ctx.enter_context(nc.psum_tensor("ps", [128, 7, 512], F32))
        pst = ctx.enter_context(nc.psum_tensor("pst", [128, 1024], BF16))

        ident = alloc([128, 128], BF16, 'ident')
        identf = alloc([128, 128], F32, 'identf')
        subgbc = alloc([128, 128], F32, 'subgbc')
        lamt = alloc([128, 256], F32, 'lamt')
        lams = alloc([128, 8], F32, 'lams')
        psc = alloc([128, 4], F32, 'psc')
        cw = alloc([128, NJ * 3], F32, 'cw')
        cb = alloc([128, NJ], F32, 'cb')
        spec = alloc([128, 4, 4, 128], F32, 'spec')
        mskd = alloc([128, 128], F32, 'mskd')
        b31t = alloc([128, 4], F32, 'b31t')
        kbias = alloc([128, 8], F32, 'kbias')
        cfl = alloc([128, 1], F32, 'cfl')
        ofl = alloc([128, 1], F32, 'ofl')
        corrt = alloc([128, 4, 16], F32, 'corrt')
        epsn = alloc([128, 1], F32, 'epsn')
        epss = alloc([128, 1], F32, 'epss')
        stat = alloc([128, 16], F32, 'stat')
        junk = alloc([128, D], BF16, 'junk')
        poolwt = alloc([128, 4, 128], BF16, 'poolwt')
        hist = alloc([128, NJ, 2], F32, 'hist')
        yT = alloc([128, 8, 17 * 128], BF16, 'yT')

        def ld(q, key, out, in_, wr):
            P.dma(q, key, lambda e: e.dma_start(out=out, in_=in_), (), wr)

        ld('sp', 'c0', identf[:], identd, ['identf'])
        P.op('dve', lambda e: e.tensor_copy(out=ident[:], in_=identf[:]), ['identf'], ['ident'])
        P.op('dve', lambda e: e.memset(epsn[:], 1e-6), (), ['epsn'])
        P.op('dve', lambda e: e.memset(epss[:], 1e-5), (), ['epss'])
        P.op('dve', lambda e: e.memset(hist[:].rearrange("p j t -> p (j t)"), 0.0), (), ['hist'])
        P.op('dve', lambda e: e.memset(stat[:], 0.0), (), ['ss0', 'ss1', 'ss2', 'ss3', 'rstd', 'cfl8', 'fss', 'frs'])
        g1T = alloc([128, 8], F32, 'g1T')
        ld('sp', 'c1', g1T[:], g1t, ['g1T'])
        persist_end = cur[0]
        def setup_late():
            ld('sp', 'c3', subgbc[:], subg.broadcast_to([128, 128]), ['subgbc'])
            ld('sp', 'c4', lamt[:], lamv.broadcast_to([128, 256]), ['lamt'])
            ld('sp', 'c5', psc[:], pscale, ['psc'])
            ld('sp', 'c6', cw[:], convw, ['cw'])
            ld('sp', 'c7', cb[:], convb, ['cb'])
            ld('sp', 'c8', spec[:, 0, :, :].rearrange("p h q -> p (h q)"), nearb, ['spec0'])
            ld('sp', 'c9', spec[:, 1, :, :].rearrange("p h q -> p (h q)"), diagb, ['spec1'])
            ld('sp', 'c10', mskd[:], maskd, ['mskd'])
            ld('sp', 'c11', b31t[:], b31.broadcast_to([128, 4]), ['b31t'])
            ld('sp', 'c12', cfl[:], cflag, ['cfl'])
            ld('sp', 'c13', ofl[:], oflag, ['ofl'])
            ld('sp', 'c14', corrt[:].rearrange("p g t -> p (g t)"), corr, ['corrt'])
            ld('pool', 'c15', poolwt[:], pool_w.rearrange("g c d -> c g d"), ['poolwt'])

            P.op('dve', lambda e: e.tensor_scalar(out=subgbc[:], in0=subgbc[:], scalar1=1.0 - LAM_INIT, scalar2=None, op0=ALU.mult), ['subgbc'], ['subgbc'])
            P.op('dve', lambda e: e.tensor_tensor(out=lamt[:, 0:64], in0=lamt[:, 0:64], in1=lamt[:, 64:128], op=ALU.mult), ['lamt'], ['lamt'])
            P.op('dve', lambda e: e.tensor_tensor(out=lamt[:, 128:192], in0=lamt[:, 128:192], in1=lamt[:, 192:256], op=ALU.mult), ['lamt'], ['lamt'])
            P.op('dve', lambda e: e.reduce_sum(out=lams[:, 0:1], in_=lamt[:, 0:64], axis=mybir.AxisListType.X), ['lamt'], ['lams'])
            P.op('dve', lambda e: e.reduce_sum(out=lams[:, 1:2], in_=lamt[:, 128:192], axis=mybir.AxisListType.X), ['lamt'], ['lams'])
            P.op('act', lambda e: e.activation(out=lams[:, 2:4], in_=lams[:, 0:2], func=AF.Exp), ['lams'], ['lams'])
            P.op('dve', lambda e: e.tensor_tensor(out=lams[:, 4:5], in0=lams[:, 3:4], in1=lams[:, 2:3], op=ALU.subtract), ['lams'], ['lams'])
            P.op('dve', lambda e: e.tensor_scalar(out=lams[:, 4:5], in0=lams[:, 4:5], scalar1=-LAM_INIT, scalar2=None, op0=ALU.add), ['lams'], ['lams'])
            P.op('dve', lambda e: e.tensor_copy(out=kbias[:, 0:4], in_=b31t[:]), ['b31t'], ['kbias'])
            P.op('dve', lambda e: e.tensor_scalar(out=kbias[:, 4:8], in0=b31t[:], scalar1=cfl[:, 0:1], scalar2=None, op0=ALU.add), ['b31t', 'cfl', 'kbias'], ['kbias'])
            for h in range(NB):
                P.op('dve', lambda e, h=h: e.tensor_scalar(out=spec[:, 0, h, :], in0=spec[:, 0, h, :], scalar1=b31t[:, h:h + 1], scalar2=8.0, op0=ALU.subtract, op1=ALU.mult), ['spec0', 'b31t'], ['spec0'])
                P.op('dve', lambda e, h=h: e.tensor_scalar(out=spec[:, 1, h, :], in0=spec[:, 1, h, :], scalar1=b31t[:, h:h + 1], scalar2=None, op0=ALU.subtract), ['spec1', 'b31t'], ['spec1'])
                P.op('dve', lambda e, h=h: e.tensor_tensor(out=spec[:, 1, h, :], in0=spec[:, 1, h, :], in1=mskd[:], op=ALU.add), ['spec1', 'mskd'], ['spec1'])
                P.op('dve', lambda e, h=h: e.tensor_scalar(out=spec[:, 1, h, :], in0=spec[:, 1, h, :], scalar1=8.0, scalar2=None, op0=ALU.mult), ['spec1'], ['spec1'])
            P.op('dve', lambda e: e.tensor_scalar(out=stat[:, 8:9], in0=cfl[:, 0:1], scalar1=8.0, scalar2=None, op0=ALU.mult), ['cfl'], ['cfl8'])
            for ty in range(2):
                P.op('dve', lambda e, ty=ty: e.tensor_scalar(out=spec[:, 2 + ty, :, :].rearrange("p h q -> p (h q)"), in0=spec[:, ty, :, :].rearrange("p h q -> p (h q)"), scalar1=stat[:, 8:9], scalar2=None, op0=ALU.add), ['spec%d' % ty, 'cfl8'], ['spec%d' % (2 + ty)])


            return lams[:, 4:5]

        KT = alloc([128, NB, S], BF16, 'KT')
        V = alloc([128, 32, NB, 130], BF16, 'V')
        QT = alloc([128, NB, 17 * 128], BF16, 'QT')
        ab_end = cur[0]
        w_in_t = alloc([128, 8, 2048], BF16, 'w_in_t')
        xt = alloc([128, 2, D], F32, 'xt')
        hb = alloc([128, 2, D], BF16, 'hb')
        hTs = [alloc([128, 8, 512], BF16, 'hT%d' % i) for i in range(2)]
        zt = alloc([128, 4, 528], F32, 'zt')
        zs = alloc([128, 2, 528], F32, 'zs')
        pooled = alloc([128, 4, 512], BF16, 'pooled')
        print('phaseA end', cur[0], 'top', top)
        a_end = cur[0]

        def load_w_in(c4):
            ld('pool', 'win%d' % c4, w_in_t[:, :, c4 * 512:(c4 + 1) * 512],
               w_in.rearrange("(k p) n -> p k n", p=128)[:, :, c4 * 512:(c4 + 1) * 512], ['w_in:%d' % c4])

        load_w_in(2)
        load_w_in(3)
        P.op('pool', lambda e: e.memset(V[:].rearrange("p a b c -> p (a b c)"), 1.0), (), ['V'])
        P.op('pool', lambda e: e.memset(zt[:].rearrange("p a b -> p (a b)"), 0.0), (), ['zt:%d' % g for g in range(4)] + ['zth:%d' % g for g in range(4)])

        bank = [0]

        def nextbank(nbanks=7):
            b = bank[0] % nbanks
            bank[0] += 1
            return b

        def norm_rows(src, ns, gain_key, gain, out_bf, src_keys, out_keys, eps_t, dim):
            for s in range(ns):
                P.op('act', lambda e, s=s: e.activation(out=junk[:, 0:dim], in_=src(s), func=AF.Square, accum_out=stat[:, s:s + 1]),
                     src_keys(s), ['ss%d' % s, 'junk'])
            P.op('act', lambda e: e.activation(out=stat[:, 4:4 + ns], in_=stat[:, 0:ns], func=AF.Ln, scale=1.0 / dim, bias=eps_t[:]),
                 ['ss%d' % s for s in range(ns)] + ['epsn', 'epss'], ['rstd'])
            P.op('act', lambda e: e.activation(out=stat[:, 4:4 + ns], in_=stat[:, 4:4 + ns], func=AF.Exp, scale=-0.5), ['rstd'], ['rstd'])
            for s in range(ns):
                if gain is None:
                    P.op('dve', lambda e, s=s: e.tensor_scalar(out=out_bf(s), in0=src(s), scalar1=stat[:, 4 + s:5 + s], scalar2=None, op0=ALU.mult),
                         src_keys(s) + ['rstd'], out_keys(s))
                else:
                    P.op('dve', lambda e, s=s: e.scalar_tensor_tensor(out=out_bf(s), in0=src(s), scalar=stat[:, 4 + s:5 + s], in1=gain, op0=ALU.mult, op1=ALU.mult),
                         src_keys(s) + ['rstd', gain_key], out_keys(s))
            P.op('dve', lambda e: e.memset(stat[:, 0:4], 0.0), (), ['ss%d' % s for s in range(4)])

        def tr_round(src_bf, ns, dstT, col0, kp, src_keys, dst_key, gscale=None):
            for kk in range(2):
                k = kp * 2 + kk
                for s in range(ns):
                    P.op('pe', lambda e, k=k, kk=kk, s=s: e.transpose(pst[:, kk * 512 + s * 128: kk * 512 + (s + 1) * 128], src_bf(s)[:, k * 128:(k + 1) * 128], ident[:]),
                         src_keys(s) + ['ident'], ['pst'])
            if gscale is None:
                evac(dstT[:, 2 * kp:2 * kp + 2, col0:col0 + ns * 128],
                     pst[:].rearrange("p (a b) -> p a b", a=2)[:, :, 0:ns * 128], ['pst'], [dst_key])
            else:
                for kk in range(2):
                    k = kp * 2 + kk
                    evac(dstT[:, k, col0:col0 + ns * 128], pst[:, kk * 512:kk * 512 + ns * 128], ['pst', 'g1T'], [dst_key],
                         scale=gscale[:, k:k + 1])

        def a_norm(t, half):
            r0 = t * 512 + half * 256
            P.dma('sp', 'xt', lambda e: e.dma_start(out=xt[:], in_=xin[r0:r0 + 256, :].rearrange("(s p) d -> p s d", p=128)),
                  (), ['xt:%d' % s for s in range(2)])
            norm_rows(lambda s: xt[:, s, :], 2, None, None, lambda s: hb[:, s, :],
                      lambda s: ['xt:%d' % s], lambda s: ['hb:%d' % s], epsn, D)

        def a_round(t, r):
            half, kp = r // 4, r % 4
            tr_round(lambda s: hb[:, s, :], 2, hTs[t % 2], half * 256, kp, lambda s: ['hb:%d' % s], 'hT%d' % (t % 2), gscale=g1T)

        def a_groups(t):
            own = t >= 4
            full = t >= 3
            hT = hTs[t % 2]
            hk = 'hT%d' % (t % 2)
            grps = []

            def chunk(m):
                b = nextbank()
                n0 = 0 if (own or m >= 8) else (256 if m < 4 else 384)
                for k in range(8):
                    mm(ps[:, b, n0:512], w_in_t[:, k, m * 128:(m + 1) * 128], hT[:, k, n0:512], k == 0, k == 7,
                       ['w_in:%d' % (m // 4), hk], ['ps%d' % b])
                if m < 4:
                    P.op('act', lambda e: e.activation(out=zt[:, m, 16 + n0:528], in_=ps[:, b, n0:512], func=AF.Copy), ['ps%d' % b], ['zt:%d' % m])
                elif m < 8:
                    h = m - 4
                    if own:
                        c0 = (16 + 4 * (t - 4) - 15) * 128
                        evac(QT[:, h, c0:c0 + 512], ps[:, b, :], ['ps%d' % b], ['QT'])
                    else:
                        evac(QT[:, h, 0:128], ps[:, b, 384:512], ['ps%d' % b], ['QT'])
                else:
                    h = m - 8
                    evac(KT[:, h, t * 512:(t + 1) * 512], ps[:, b, :], ['ps%d' % b], ['KT'])

            def vgrp(s):
                b = nextbank()
                for k in range(8):
                    mm(ps[:, b, :], hT[:, k, s * 128:(s + 1) * 128], w_in_t[:, k, 1536:2048], k == 0, k == 7,
                       ['w_in:3', hk], ['ps%d' % b])
                evac(V[:, t * 4 + s, :, 0:128], ps[:, b, :].rearrange("p (h e) -> p h e", h=NB), ['ps%d' % b], ['V'])

            if full:
                for m in range(0, 4):
                    grps.append(lambda m=m: chunk(m))
            for m in range(8, 12):
                grps.append(lambda m=m: chunk(m))
            for s in range(4):
                grps.append(lambda s=s: vgrp(s))
            if full:
                for m in range(4, 8):
                    grps.append(lambda m=m: chunk(m))
            return grps

        def a_pool(t):
            own = t >= 4
            if t == 4:
                P.op('dve', lambda e: e.tensor_scalar(out=zt[:, :, 0:16], in0=zt[:, :, 0:16], scalar1=ofl[:, 0:1], scalar2=None, op0=ALU.mult),
                     ['zth:%d' % g for g in range(4)] + ['ofl'], ['zth:%d' % g for g in range(4)])
            for g in range(4):
                zk = 'zt:%d' % g
                z = zt[:, g, :]
                P.op('dve', lambda e, z=z: e.tensor_tensor(out=zs[:, 0, 1:528], in0=z[:, 1:528], in1=z[:, 0:527], op=ALU.add), [zk, 'zth:%d' % g, 'zs0'], ['zs0'])
                curi = 0
                sh = 2
                for step in range(g):
                    nxt = 1 - curi
                    P.op('dve', lambda e, curi=curi, nxt=nxt, sh=sh: e.tensor_tensor(out=zs[:, nxt, 2 * sh - 1:528], in0=zs[:, curi, 2 * sh - 1:528], in1=zs[:, curi, sh - 1:528 - sh], op=ALU.add),
                         ['zs%d' % curi, 'zs%d' % nxt], ['zs%d' % nxt])
                    curi = nxt
                    sh *= 2
                w = 2 ** (g + 1)
                if t == 4:
                    P.op('dve', lambda e, curi=curi, g=g: e.tensor_tensor(out=zs[:, curi, 16:32], in0=zs[:, curi, 16:32], in1=corrt[:, g, :], op=ALU.mult),
                         ['zs%d' % curi, 'corrt'], ['zs%d' % curi])
                pg_ = g
                pk_ = 'pooled:%d' % pg_
                P.op('dve', lambda e, curi=curi, z=z, w=w, pg_=pg_: e.scalar_tensor_tensor(out=pooled[:, pg_, :], in0=zs[:, curi, 16:528], scalar=1.0 / w, in1=z[:, 16:528], op0=ALU.mult, op1=ALU.subtract),
                     ['zs%d' % curi, zk, pk_], [pk_])
                P.op('dve', lambda e, z=z: e.tensor_copy(out=z[:, 0:16], in_=z[:, 512:528]), [zk], ['zth:%d' % g])
            force_act[0] = 6

        def a_pool_mm(t):
            own = t >= 4
            for g in range(4):
                pk_ = 'pooled:%d' % g
                b = nextbank()
                if own:
                    c0 = (16 + 4 * (t - 4) - 15) * 128
                    mm(ps[:, b, :], poolwt[:, g, :], pooled[:, g, :], True, True, ['poolwt', pk_], ['ps%d' % b])
                    evac(yT[:, g, c0:c0 + 512], ps[:, b, :], ['ps%d' % b, 'psc'], ['yT:%d' % g], scale=psc[:, g:g + 1])
                else:
                    mm(ps[:, b, 0:128], poolwt[:, g, :], pooled[:, g, 384:512], True, True, ['poolwt', pk_], ['ps%d' % b])
                    evac(yT[:, g, 0:128], ps[:, b, 0:128], ['ps%d' % b, 'psc'], ['yT:%d' % g], scale=psc[:, g:g + 1])

        a_norm(0, 0)
        for r in range(4):
            a_round(0, r)
        a_norm(0, 1)
        for r in range(4, 8):
            a_round(0, r)
        neglam = setup_late()
        for t in range(8):
            grps = a_groups(t)
            nxt = t + 1 < 8
            if nxt:
                a_norm(t + 1, 0)
            for i, gfn in enumerate(grps):
                gfn()
                if nxt:
                    if i == 0:
                        a_round(t + 1, 0)
                        a_round(t + 1, 1)
                    elif i == 1:
                        a_round(t + 1, 2)
                        a_round(t + 1, 3)
                        a_norm(t + 1, 1)
                    elif 4 <= i < 8:
                        a_round(t + 1, i)
                if t >= 3 and i == 7:
                    a_pool(t)
            if t == 0:
                load_w_in(0)
                load_w_in(1)
            if t >= 3:
                a_pool_mm(t)

        cur[0] = ab_end
        w_out_t = alloc([128, 8, D], BF16, 'w_out_t')
        ring = [alloc([128, 8, 256], BF16, 'ring%d' % i) for i in range(4)]
        bc_start = cur[0]
        Pt = [alloc([128, 2, 512], BF16, 'Pt%d' % i) for i in range(2)]
        ya = alloc([128, 4, 512], BF16, 'ya')
        accs = alloc([128, 3, 512], F32, 'accs')
        oall = alloc([128, 4, 128], F32, 'oall')
        sqall = alloc([128, 4, 128], F32, 'sqall')
        stb = alloc([128, 16], F32, 'stb')
        phaseA_bufs = ['w_in', 'xt', 'hb', 'hT0', 'hT1', 'zt', 'zth', 'zs0', 'zs1', 'pooled']
        phaseB_bufs = ['Pt0', 'Pt1', 'ya', 'accs', 'oall:0', 'oall:1', 'oall:2', 'oall:3', 'sqall:0', 'sqall:1', 'sqall:2', 'sqall:3', 'rs', 'sss', 'srs']
        P.alias(['w_out', 'Pt0', 'Pt1', 'ya:0', 'ya:1', 'ya:2', 'ya:3', 'accs', 'oall:0', 'oall:1', 'oall:2', 'oall:3', 'sqall:0', 'sqall:1', 'sqall:2', 'sqall:3', 'rs', 'sss', 'srs'] + ['ring%d' % i for i in range(4)], phaseA_bufs)
        ld('pool', 'wout', w_out_t[:], w_out.rearrange("(k p) n -> p k n", p=128), ['w_out'])

        tiles = [(15, 1)] + [(16 + 4 * i, 4) for i in range(4)]
        ring_seq = [(ti, j) for ti in range(1, len(tiles)) for j in range(NJ)]

        def issue_ring(n):
            if n >= len(ring_seq):
                return
            ti, j = ring_seq[n]
            ri = n % 4
            rk = 'ring%d' % ri
            P.dma('pool', rk, lambda e: e.dma_start(out=ring[ri][:].rearrange("p k n -> p (k n)"), in_=wgu[j]), (), [rk])

        for n in range(4):
            issue_ring(n)

        groups = [(15, 1)] + [(16 + 4 * i, 4) for i in range(4)]
        ACC_BANK0 = 4
        HQ = 32
        for i_ in range(2):
            P.op('dve', lambda e, i_=i_: e.memset(Pt[i_][:].rearrange("p a b -> p (a b)"), 0.0), (), ['Pt%d' % i_])
        iters = []
        for (g0, nq) in groups:
            for h in range(NB):
                for kb in range(g0 + nq):
                    iters.append((g0, nq, h, kb))

        def emit_qk(idx):
            g0, nq, h, kb = iters[idx]
            qc0 = (g0 - 15) * 128
            a = max(0, kb - g0)
            n0, n1 = a * 128, nq * 128
            if nq == 1:
                n0 = 128 - HQ
            sb = 2 * (idx % 2)
            skeys = ['ps%d' % sb, 'ps%d' % (sb + 1)]
            for c in range(2):
                mm(ps[:, sb + c, n0:n1], KT[c * 64:(c + 1) * 64, h, kb * 128:(kb + 1) * 128],
                   QT[c * 64:(c + 1) * 64, h, qc0 + n0:qc0 + n1], True, True, ['KT', 'QT'], [skeys[c]])
            ctxk = kb < 16
            for qi in range(a, nq):
                d = g0 + qi - kb
                if d > 1:
                    continue
                ty = (0 if d == 1 else 1) + (2 if ctxk else 0)
                q0 = n0 if nq == 1 else qi * 128
                for c in range(2):
                    P.op('dve', lambda e, c=c, qi=qi, ty=ty, q0=q0: e.tensor_tensor(
                        out=ps[:, sb + c, q0:(qi + 1) * 128], in0=ps[:, sb + c, q0:(qi + 1) * 128],
                        in1=spec[:, ty, h, q0 - qi * 128:128], op=ALU.add), [skeys[c], 'spec%d' % ty], [skeys[c]])

        def emit_exp(idx):
            g0, nq, h, kb = iters[idx]
            a = max(0, kb - g0)
            n0, n1 = a * 128, nq * 128
            if nq == 1:
                n0 = 128 - HQ
            si = idx % 2
            sb = 2 * si
            skeys = ['ps%d' % sb, 'ps%d' % (sb + 1)]
            ctxk = kb < 16
            bc = 4 + h if ctxk else h
            bcol = kbias[:, bc:bc + 1]
            pk = 'Pt%d' % si
            P.op('act', lambda e: e.activation(
                out=Pt[si][:, :, n0:n1], in_=ps[:, sb:sb + 2, n0:n1], func=AF.Exp, bias=bcol, scale=0.125),
                skeys + ['kbias'], [pk])

        def emit_av(idx):
            g0, nq, h, kb = iters[idx]
            a = max(0, kb - g0)
            si = idx % 2
            pk = 'Pt%d' % si
            for qi in range(a, nq):
                for c in range(2):
                    ai = qi * 2 + c
                    ab = ACC_BANK0 + ai // 3
                    ac = (ai % 3) * 160
                    first_in_bank = (ai % 3 == 0)
                    mm(ps[:, ab, ac:ac + 129], Pt[si][:, c, qi * 128:(qi + 1) * 128], V[:, kb, h, 0:129],
                       (kb == 0 and first_in_bank), kb == g0 + qi, [pk, 'V'], ['ps%d' % ab], skip=True)

        def emit_head_end(g0, nq, h):
            nbk = (2 * nq + 2) // 3
            for bk in range(nbk):
                P.op('dve', lambda e, bk=bk: e.tensor_copy(out=accs[:, bk, 0:480], in_=ps[:, ACC_BANK0 + bk, 0:480]),
                     ['ps%d' % (ACC_BANK0 + bk), 'accs'], ['accs'])

            def acc(ai):
                return accs[:, ai // 3, (ai % 3) * 160:(ai % 3) * 160 + 129]
            for ai in range(2 * nq):
                P.op('dve', lambda e, ai=ai: e.tensor_scalar(out=stb[:, ai:ai + 1], in0=acc(ai)[:, 128:129], scalar1=1e-30, scalar2=None, op0=ALU.max),
                     ['accs', 'rs'], ['rs'])
            P.op('dve', lambda e: e.reciprocal(out=stb[:, 0:2 * nq], in_=stb[:, 0:2 * nq]), ['rs'], ['rs'])
            for qi in range(nq):
                P.op('dve', lambda e, qi=qi: e.tensor_tensor(out=stb[:, 2 * qi + 1:2 * qi + 2], in0=stb[:, 2 * qi + 1:2 * qi + 2], in1=neglam, op=ALU.mult), ['rs', 'lams'], ['rs'])
            for qi in range(nq):
                P.op('dve', lambda e, qi=qi: e.tensor_scalar(out=sqall[:, qi, :], in0=acc(2 * qi + 1)[:, 0:128], scalar1=stb[:, 2 * qi + 1:2 * qi + 2], scalar2=None, op0=ALU.mult),
                     ['accs', 'rs', 'sqall:%d' % qi], ['sqall:%d' % qi])
                P.op('dve', lambda e, qi=qi: e.scalar_tensor_tensor(out=oall[:, qi, :], in0=acc(2 * qi)[:, 0:128], scalar=stb[:, 2 * qi:2 * qi + 1], in1=sqall[:, qi, :], op0=ALU.mult, op1=ALU.add),
                     ['accs', 'rs', 'sqall:%d' % qi, 'oall:%d' % qi], ['oall:%d' % qi])
            sqk = ['sqall:%d' % qi for qi in range(nq)]
            oak = ['oall:%d' % qi for qi in range(nq)]
            P.op('dve', lambda e: e.tensor_tensor(out=sqall[:, 0:nq, :], in0=oall[:, 0:nq, :], in1=oall[:, 0:nq, :], op=ALU.mult), oak + sqk, sqk)
            P.op('dve', lambda e: e.reduce_sum(out=stb[:, 8:8 + nq], in_=sqall[:, 0:nq, :], axis=mybir.AxisListType.X), sqk, ['sss'])

        def emit_head_end2(g0, nq, h):
            P.op('act', lambda e: e.activation(out=stb[:, 12:12 + nq], in_=stb[:, 8:8 + nq], func=AF.Ln, scale=1.0 / 128, bias=epss[:]), ['sss', 'epss'], ['srs'])
            P.op('act', lambda e: e.activation(out=stb[:, 12:12 + nq], in_=stb[:, 12:12 + nq], func=AF.Exp, scale=-0.5), ['srs'], ['srs'])
            for qi in range(nq):
                P.op('dve', lambda e, qi=qi: e.scalar_tensor_tensor(out=ya[:, qi, h * 128:(h + 1) * 128], in0=oall[:, qi, :], scalar=stb[:, 12 + qi:13 + qi], in1=subgbc[:], op0=ALU.mult, op1=ALU.mult),
                     ['oall:%d' % qi, 'srs', 'subgbc'], ['ya:%d' % qi])

        def emit_group_end(g0, nq):
            qc0 = (g0 - 15) * 128
            for qi in range(nq):
                for h in range(NB):
                    P.op('pe', lambda e, qi=qi, h=h: e.transpose(pst[:, h * 128:(h + 1) * 128], ya[:, qi, h * 128:(h + 1) * 128], ident[:]),
                         ['ya:%d' % qi, 'ident'], ['pst'])
                cc = qc0 + qi * 128
                evac(yT[:, 4:8, cc:cc + 128], pst[:, 0:512].rearrange("p (h q) -> p h q", h=NB), ['pst'], ['yT:att'])

        emit_qk(0)
        emit_qk(1)
        pending = []
        for idx in range(len(iters)):
            emit_exp(idx)
            if idx + 2 < len(iters):
                emit_qk(idx + 2)
            emit_av(idx)
            while pending and pending[0][0] <= idx:
                pending.pop(0)[1]()
            g0, nq, h, kb = iters[idx]
            if kb == g0 + nq - 1:
                emit_head_end(g0, nq, h)

                def part2(g0=g0, nq=nq, h=h):
                    emit_head_end2(g0, nq, h)
                    if h == NB - 1:
                        emit_group_end(g0, nq)
                pending.append((idx + 10, part2))
        while pending:
            pending.pop(0)[1]()

        cur[0] = persist_end
        w_fo_t = alloc([128, NJ, D], BF16, 'w_fo_t')
        actT = alloc([128, NJ, 512], BF16, 'actT')
        h2T = alloc([128, 8, 512], BF16, 'h2T')
        h2Th = alloc([128, 8, 128], BF16, 'h2Th')
        cbuf = [alloc([128, 512], F32, 'cbuf%d' % i) for i in range(2)]
        assert cur[0] <= ab_end, (cur[0], ab_end)
        cur[0] = bc_start
        xt2s = [alloc([128, 4, D], F32, 'xt2_%d' % i) for i in range(2)]
        h2 = alloc([128, 2, D], BF16, 'h2')
        sgb = [alloc([128, 512], F32, 'sgb%d' % i) for i in range(2)]
        save2 = cur[0]
        cur[0] = offs['spec']
        g2bc = alloc([128, D], F32, 'g2bc')
        gfbc = alloc([128, D], F32, 'gfbc')
        assert cur[0] <= offs['mskd'], (cur[0], offs['mskd'])
        cur[0] = save2
        newC = (['xt2_%d:%d' % (b, i) for b in range(2) for i in range(4)] + ['h2:0', 'h2:1', 'g2bc', 'gfbc', 'w_fo:0', 'w_fo:1', 'w_fo:2', 'w_fo:3', 'actT', 'h2T', 'h2Th',
                'cbuf0', 'cbuf1', 'sgb0', 'sgb1'])
        P.alias(newC, phaseA_bufs + phaseB_bufs + ['KT', 'V', 'QT', 'spec0', 'spec1', 'spec2', 'spec3'])
        ld('sp', 'g2', g2bc[:], g2.broadcast_to([128, D]), ['g2bc'])
        ld('sp', 'c2', gfbc[:], gf.broadcast_to([128, D]), ['gfbc'])

        def c_load(ti):
            b0, nb = tiles[ti]
            xt2 = xt2s[ti % 2]
            xk = 'xt2_%d' % (ti % 2)
            P.dma('sp', xk, lambda e: e.dma_start(out=xt2[:, 0:nb, :], in_=xin[b0 * 128:(b0 + nb) * 128, :].rearrange("(s p) d -> p s d", p=128)),
                  (), [xk + ':%d' % s for s in range(nb)])

        def c_outproj(ti):
            b0, nb = tiles[ti]
            xt2 = xt2s[ti % 2]
            xk = 'xt2_%d' % (ti % 2)
            yc0 = (b0 - 15) * 128
            for s in range(nb):
                for n2 in range(2):
                    b = nextbank()
                    for k in range(8):
                        yk = 'yT:%d' % k if k < 4 else 'yT:att'
                        mm(ps[:, b, :], yT[:, k, yc0 + s * 128: yc0 + (s + 1) * 128], w_out_t[:, k, n2 * 512:(n2 + 1) * 512], k == 0, k == 7,
                           [yk, 'w_out'], ['ps%d' % b])
                    P.op('dve', lambda e, s=s, n2=n2, b=b: e.tensor_tensor(out=xt2[:, s, n2 * 512:(n2 + 1) * 512], in0=ps[:, b, :], in1=xt2[:, s, n2 * 512:(n2 + 1) * 512], op=ALU.add),
                         ['ps%d' % b, xk + ':%d' % s], [xk + ':%d' % s])

        def c_norm(ti, half):
            b0, nb = tiles[ti]
            xt2 = xt2s[ti % 2]
            xk = 'xt2_%d' % (ti % 2)
            ns = min(2, nb - 2 * half)
            if ns <= 0:
                return
            norm_rows(lambda s: xt2[:, 2 * half + s, :], ns, 'g2bc', g2bc[:], lambda s: h2[:, s, :],
                      lambda s: [xk + ':%d' % (2 * half + s)], lambda s: ['h2:%d' % s], epsn, D)

        def c_round(ti, r):
            b0, nb = tiles[ti]
            half, kp = r // 4, r % 4
            ns = min(2, nb - 2 * half)
            if ns <= 0:
                return
            if ti == 0:
                tr_round(lambda s: h2[:, s, :], ns, h2Th, 0, kp, lambda s: ['h2:%d' % s], 'h2Th')
            else:
                tr_round(lambda s: h2[:, s, :], ns, h2T, half * 256, kp, lambda s: ['h2:%d' % s], 'h2T')

        ring_n = [0]

        WFO_PIECES = [(0, 6), (6, 12), (12, 17), (17, 22)]

        def wfo_piece(j):
            for i, (a_, b_) in enumerate(WFO_PIECES):
                if a_ <= j < b_:
                    return i

        def c_ffn_in(ti):
            b0, nb = tiles[ti]
            N = nb * 128
            for j in range(NJ):
                if ti == 1 and j in (1, 6, 11, 16):
                    pc = (1, 6, 11, 16).index(j)
                    a_, b_ = WFO_PIECES[pc]
                    ld('pool', 'wfo%d' % pc, w_fo_t[:, a_:b_, :],
                       w_fo.rearrange("(j p) n -> p j n", p=128)[:, a_:b_, :], ['w_fo:%d' % pc])
                n = ring_n[0]
                ring_n[0] += 1
                ri = n % 4
                rk = 'ring%d' % ri
                if ti == 1:
                    bh = nextbank()
                    for k in range(8):
                        mm(ps[:, bh, 0:2], ring[ri][:, k, 0:128], h2Th[:, k, 126:128], k == 0, k == 7, [rk, 'h2Th'], ['ps%d' % bh])
                    P.op('act', lambda e, bh=bh, j=j: e.activation(out=hist[:, j, :], in_=ps[:, bh, 0:2], func=AF.Copy, scale=ofl[:, 0:1]),
                         ['ps%d' % bh, 'hist', 'ofl'], ['hist'])
                bg = nextbank()
                for k in range(8):
                    mm(ps[:, bg, 0:N], ring[ri][:, k, 0:128], h2T[:, k, 0:N], k == 0, k == 7, [rk, 'h2T'], ['ps%d' % bg])
                bu = nextbank()
                for k in range(8):
                    mm(ps[:, bu, 0:N], ring[ri][:, k, 128:256], h2T[:, k, 0:N], k == 0, k == 7, [rk, 'h2T'], ['ps%d' % bu])
                issue_ring(n + 4)
                ci = j % 2
                ck = 'cbuf%d' % ci
                cbt = cbuf[ci]
                pg = ps[:, bg, :]
                P.op('act', lambda e, cbt=cbt, pg=pg, j=j: e.activation(out=cbt[:, 0:N], in_=pg[:, 0:N], func=AF.Identity, scale=cw[:, 3 * j + 2:3 * j + 3], bias=cb[:, j:j + 1]),
                     ['ps%d' % bg, 'cw', 'cb', ck], [ck])
                P.op('dve', lambda e, cbt=cbt, pg=pg, j=j: e.scalar_tensor_tensor(out=cbt[:, 1:N], in0=pg[:, 0:N - 1], scalar=cw[:, 3 * j + 1:3 * j + 2], in1=cbt[:, 1:N], op0=ALU.mult, op1=ALU.add),
                     ['ps%d' % bg, 'cw', ck], [ck])
                P.op('dve', lambda e, cbt=cbt, pg=pg, j=j: e.scalar_tensor_tensor(out=cbt[:, 2:N], in0=pg[:, 0:N - 2], scalar=cw[:, 3 * j:3 * j + 1], in1=cbt[:, 2:N], op0=ALU.mult, op1=ALU.add),
                     ['ps%d' % bg, 'cw', ck], [ck])
                P.op('dve', lambda e, cbt=cbt, j=j: e.scalar_tensor_tensor(out=cbt[:, 0:1], in0=hist[:, j, 1:2], scalar=cw[:, 3 * j + 1:3 * j + 2], in1=cbt[:, 0:1], op0=ALU.mult, op1=ALU.add),
                     ['hist', 'cw', ck], [ck])
                P.op('dve', lambda e, cbt=cbt, j=j: e.scalar_tensor_tensor(out=cbt[:, 0:2], in0=hist[:, j, 0:2], scalar=cw[:, 3 * j:3 * j + 1], in1=cbt[:, 0:2], op0=ALU.mult, op1=ALU.add),
                     ['hist', 'cw', ck], [ck])
                P.op('act', lambda e, pg=pg, j=j: e.activation(out=hist[:, j, :], in_=pg[:, N - 2:N], func=AF.Copy), ['ps%d' % bg, 'hist', ck], ['hist'])
                sk = 'sgb%d' % ci
                sgt = sgb[ci]
                P.op('act', lambda e, sgt=sgt, cbt=cbt: e.activation(out=sgt[:, 0:N], in_=cbt[:, 0:N], func=AF.Silu), [ck, sk], [sk])
                P.op('dve', lambda e, sgt=sgt, bu=bu, j=j: e.tensor_tensor(out=actT[:, j, 0:N], in0=ps[:, bu, 0:N], in1=sgt[:, 0:N], op=ALU.mult),
                     ['ps%d' % bu, sk, 'actT'], ['actT'])

        def c_ffn_out_group(ti, s, n2):
            xt2 = xt2s[ti % 2]
            xk = 'xt2_%d' % (ti % 2)
            b = nextbank()
            for j in range(NJ):
                mm(ps[:, b, :], actT[:, j, s * 128:(s + 1) * 128], w_fo_t[:, j, n2 * 512:(n2 + 1) * 512], j == 0, j == NJ - 1,
                   ['actT', 'w_fo:%d' % wfo_piece(j)], ['ps%d' % b])
            P.op('dve', lambda e: e.tensor_tensor(out=xt2[:, s, n2 * 512:(n2 + 1) * 512], in0=ps[:, b, :], in1=xt2[:, s, n2 * 512:(n2 + 1) * 512], op=ALU.add),
                 ['ps%d' % b, xk + ':%d' % s], [xk + ':%d' % s])

        def c_final_s(ti, s):
            b0, nb = tiles[ti]
            xt2 = xt2s[ti % 2]
            xk = 'xt2_%d' % (ti % 2)
            P.op('act', lambda e: e.activation(out=junk[:], in_=xt2[:, s, :], func=AF.Square, accum_out=stat[:, 14:15]),
                 [xk + ':%d' % s], ['fss', 'junk'])
            P.op('act', lambda e: e.activation(out=stat[:, 15:16], in_=stat[:, 14:15], func=AF.Ln, scale=1.0 / D, bias=epsn[:]),
                 ['fss', 'epsn'], ['frs'])
            P.op('act', lambda e: e.activation(out=stat[:, 15:16], in_=stat[:, 15:16], func=AF.Exp, scale=-0.5), ['frs'], ['frs'])
            P.op('dve', lambda e: e.scalar_tensor_tensor(out=xt2[:, s, :], in0=xt2[:, s, :], scalar=stat[:, 15:16], in1=gfbc[:], op0=ALU.mult, op1=ALU.mult),
                 [xk + ':%d' % s, 'frs', 'gfbc'], [xk + ':%d' % s])
            P.op('dve', lambda e: e.memset(stat[:, 14:15], 0.0), (), ['fss'])
            o0 = (b0 - 16) * 128 + s * 128
            P.dma('sp', 'out%d' % (ti % 2), lambda e: e.dma_start(out=yout[o0:o0 + 128, :], in_=xt2[:, s, :]),
                  [xk + ':%d' % s], ['yout%d' % (ti % 2)])

        nt = len(tiles)
        c_load(0)
        c_load(1)
        c_outproj(0)
        c_outproj(1)
        c_norm(0, 0)
        for r in range(4):
            c_round(0, r)
        c_norm(1, 0)
        for r in range(4):
            c_round(1, r)
        c_norm(1, 1)
        for r in range(4, 8):
            c_round(1, r)
        for ti in range(1, nt):
            nxt = ti + 1 < nt
            if nxt:
                c_load(ti + 1)
            c_ffn_in(ti)
            if nxt:
                c_outproj(ti + 1)
                c_norm(ti + 1, 0)
            for g in range(8):
                c_ffn_out_group(ti, g // 2, g % 2)
                if nxt:
                    if g == 4:
                        c_norm(ti + 1, 1)
                    c_round(ti + 1, g)
                if g % 2 == 1:
                    c_final_s(ti, g // 2)
        P.finish('sp', ['yout0', 'yout1'])
        P.emit(ctx)
    return nc


def _bucket(n):
    n = np.maximum(n, 0)
    nf = np.maximum(n, 1).astype(np.float32)
    large = 16 + (np.log(nf / np.float32(16)) / np.float32(math.log(128 / 16)) * np.float32(16)).astype(np.int32)
    large = np.minimum(large, 31)
    return np.where(n < 16, n, large)


_NC_CACHE = {}


def kernel(x, norm_mix_g, w_in, pool_w, pool_scale, lambda_q1, lambda_k1, lambda_q2, lambda_k2,
           subln_g, rel_bias, w_out, norm_ffn_g, ffn_w_in, ffn_conv_w, ffn_conv_b, ffn_w_out, norm_final_g):
    f = lambda a: np.ascontiguousarray(np.asarray(a, dtype=np.float32))
    x = f(x)
    rel_bias = f(rel_bias)
    kk = np.arange(128)[:, None]
    qq = np.arange(128)[None, :]
    near_idx = _bucket(128 + qq - kk)
    diag_idx = _bucket(qq - kk)
    nearb = np.ascontiguousarray(np.transpose(rel_bias[near_idx], (0, 2, 1)).reshape(128, 512))
    diagb = np.ascontiguousarray(np.transpose(rel_bias[diag_idx], (0, 2, 1)).reshape(128, 512))
    maskd = np.where(kk <= qq, 0.0, NEGM).astype(np.float32)
    fw = f(ffn_w_in)[0]
    wg = fw[:, :DFF].reshape(8, 128, NJ, 128)
    wu = fw[:, DFF:].reshape(8, 128, NJ, 128)
    wgu = np.ascontiguousarray(np.concatenate([wg, wu], axis=3).transpose(2, 1, 0, 3).reshape(NJ, 128, 8 * 256))
    convw = np.ascontiguousarray(f(ffn_conv_w)[0].reshape(3, NJ, 128).transpose(2, 1, 0).reshape(128, NJ * 3))
    convb = np.ascontiguousarray(f(ffn_conv_b)[0].reshape(NJ, 128).T)
    shared = {
        "w_in": f(w_in)[0], "w_out": f(w_out)[0], "wgu": wgu, "w_fo": f(ffn_w_out)[0],
        "pool_w": f(pool_w)[0], "g1t": np.ascontiguousarray(f(norm_mix_g).reshape(8, 128).T), "g2": f(norm_ffn_g), "gf": f(norm_final_g).reshape(1, D),
        "subg": f(subln_g), "lamv": np.concatenate([f(lambda_q1), f(lambda_k1), f(lambda_q2), f(lambda_k2)], axis=1),
        "pscale": np.ascontiguousarray(f(pool_scale)[0].reshape(4, 128).T), "convw": convw, "convb": convb,
        "nearb": nearb, "diagb": diagb, "maskd": maskd, "b31": np.ascontiguousarray(rel_bias[31:32, :]),
        "identd": np.eye(128, dtype=np.float32),
    }
    tpos = np.arange(16)
    corrA = np.stack([w / np.minimum(tpos + 1, w) for w in (2, 4, 8, 16)], 0).astype(np.float32)
    in_maps = []
    for c in range(8):
        b, role = c // 2, c % 2
        m = dict(shared)
        if role == 0:
            m["xin"] = np.ascontiguousarray(np.concatenate([x[b, 2048:], x[b, :2048]], axis=0))
            m["cflag"] = np.full((128, 1), NEGM, np.float32)
            m["oflag"] = np.zeros((128, 1), np.float32)
            m["corr"] = np.ascontiguousarray(np.broadcast_to(corrA.reshape(1, 64), (128, 64)))
        else:
            m["xin"] = x[b]
            m["cflag"] = np.zeros((128, 1), np.float32)
            m["oflag"] = np.ones((128, 1), np.float32)
            m["corr"] = np.ones((128, 64), np.float32)
        in_maps.append(m)
    if "nc" not in _NC_CACHE:
        _NC_CACHE["nc"] = build_nc()
    res = run_bass_kernel_spmd(_NC_CACHE["nc"], in_maps, core_ids=list(range(8)))
    out = np.empty((4, S, D), np.float32)
    for c in range(8):
        b, role = c // 2, c % 2
        out[b, role * 2048:(role + 1) * 2048] = res.results[c]["yout"]
    return out
```

```python
import contextlib
import math
import numpy as np
import concourse.bass as bass
import concourse.mybir as mybir
from concourse.bass_utils import run_bass_kernel_spmd

F32 = mybir.dt.float32
BF16 = mybir.dt.bfloat16
ALU = mybir.AluOpType
AF = mybir.ActivationFunctionType

D = 1024
S = 4096
NB = 4
DFF = 2816
NJ = DFF // 128
NEGM = -30000.0
LAM_INIT = 0.8 - 0.6 * math.exp(0.0)


class Prog:
    ENGS = ['pe', 'act', 'dve', 'pool', 'sp']

    def __init__(self, nc):
        self.nc = nc
        self.streams = {e: [] for e in self.ENGS}
        self.count = {e: 0 for e in self.ENGS}
        self.seen = {e: {} for e in self.ENGS}
        self.lastw = {}
        self.readers = {}
        self.dma_count = {}

    def _collect(self, eng, reads, writes):
        need = {}

        def add(tok, kind):
            if tok is None:
                return
            k, v = tok
            if k == eng and eng == 'pe':
                return
            if need.get(k, 0) < v:
                need[k] = v
        for b in reads:
            add(self.lastw.get(b), 'raw')
        for b in writes:
            add(self.lastw.get(b), 'waw')
            for k, v in self.readers.get(b, {}).items():
                add((k, v), 'war')
        waits = []
        for k, v in need.items():
            if self.seen[eng].get(k, 0) >= v:
                continue
            self.seen[eng][k] = v
            waits.append((k, v))
        return waits

    def _update(self, tok, reads, writes):
        k, v = tok
        for b in reads:
            r = self.readers.setdefault(b, {})
            if r.get(k, 0) < v:
                r[k] = v
        for b in writes:
            self.lastw[b] = tok
            self.readers[b] = {}

    def op(self, eng, fn, reads=(), writes=()):
        waits = self._collect(eng, reads, writes)
        self.count[eng] += 1
        tok = (eng, self.count[eng])
        self.streams[eng].append((waits, fn, (eng, 1)))
        self._update(tok, reads, writes)

    def dma(self, q, key, fn, reads=(), writes=()):
        waits = self._collect(q, reads, writes)
        dk = 'dma_' + key
        self.dma_count[dk] = self.dma_count.get(dk, 0) + 16
        tok = (dk, self.dma_count[dk])
        self.streams[q].append((waits, fn, (dk, 16)))
        self._update(tok, reads, writes)

    def alias(self, new_keys, old_prefixes):
        olds = [k for k in set(list(self.lastw.keys()) + list(self.readers.keys()))
                if any(k == p or k.startswith(p + ':') for p in old_prefixes)]
        for new in new_keys:
            r = self.readers.setdefault(new, {})
            for o in olds:
                w = self.lastw.get(o)
                if w is not None and r.get(w[0], 0) < w[1]:
                    r[w[0]] = w[1]
                for k, v in self.readers.get(o, {}).items():
                    if r.get(k, 0) < v:
                        r[k] = v

    def finish(self, eng, bufs):
        waits = self._collect(eng, bufs, ())
        self.streams[eng].append((waits, None, None))

    def emit(self, ctx):
        nc = self.nc
        keys = list(self.ENGS) + sorted(self.dma_count.keys())
        sems = {k: ctx.enter_context(nc.semaphore('s_' + k)) for k in keys}
        block = ctx.enter_context(nc.Block())
        streams = self.streams

        def run(engname, e):
            for waits, fn, inc in streams[engname]:
                for k, v in waits:
                    e.wait_ge(sems[k], v)
                if fn is not None:
                    fn(e).then_inc(sems[inc[0]], inc[1])

        @block.tensor
        def _(e):
            run('pe', e)

        @block.scalar
        def _(e):
            run('act', e)

        @block.vector
        def _(e):
            run('dve', e)

        @block.gpsimd
        def _(e):
            run('pool', e)

        @block.sync
        def _(e):
            run('sp', e)


def build_nc():
    nc = bass.Bass("TRN2", target_bir_lowering=False)

    def din(name, shape):
        return nc.dram_tensor(name, list(shape), F32, kind="ExternalInput").ap()

    xin = din("xin", [S, D])
    w_in = din("w_in", [D, 2048])
    w_out = din("w_out", [D, D])
    wgu = din("wgu", [NJ, 128, 8 * 256])
    w_fo = din("w_fo", [DFF, D])
    pool_w = din("pool_w", [4, 128, 128])
    g1t = din("g1t", [128, 8])
    g2 = din("g2", [1, D])
    gf = din("gf", [1, D])
    subg = din("subg", [1, 128])
    lamv = din("lamv", [1, 256])
    pscale = din("pscale", [128, 4])
    convw = din("convw", [128, NJ * 3])
    convb = din("convb", [128, NJ])
    nearb = din("nearb", [128, 4 * 128])
    diagb = din("diagb", [128, 4 * 128])
    maskd = din("maskd", [128, 128])
    b31 = din("b31", [1, 4])
    cflag = din("cflag", [128, 1])
    oflag = din("oflag", [128, 1])
    corr = din("corr", [128, 4 * 16])
    identd = din("identd", [128, 128])
    yout = nc.dram_tensor("yout", [2048, D], F32, kind="ExternalOutput").ap()

    base = (nc._sbuf_addr_for_side(None) + 63) // 64 * 64
    top = nc._sbuf_addr_for_side('right')
    cur = [base]
    names = [0]
    offs = {}

    def alloc(shape, dt, name=None):
        nbytes = int(np.prod(shape[1:])) * (4 if dt == F32 else 2)
        names[0] += 1
        offs[name] = cur[0]
        t = nc.alloc_sbuf_tensor_at((name or 't') + str(names[0]), list(shape), dt, offset=cur[0])
        cur[0] += (nbytes + 63) // 64 * 64
        assert cur[0] <= top, ("SBUF overflow", cur[0], top)
        return t

    P = Prog(nc)
    ev = [0]
    force_act = [0]

    def evac(out, in_, reads, writes, scale=None):
        ev[0] += 1
        use_act = (ev[0] % 2 == 0)
        if force_act[0] > 0:
            force_act[0] -= 1
            use_act = True
        if use_act:
            if scale is None:
                P.op('act', lambda e: e.activation(out=out, in_=in_, func=AF.Copy), reads, writes)
            else:
                P.op('act', lambda e: e.activation(out=out, in_=in_, func=AF.Copy, scale=scale), reads, writes)
        else:
            if scale is None:
                P.op('dve', lambda e: e.tensor_copy(out=out, in_=in_), reads, writes)
            else:
                P.op('dve', lambda e: e.tensor_scalar(out=out, in0=in_, scalar1=scale, scalar2=None, op0=ALU.mult), reads, writes)

    def mm(out, lhsT, rhs, start, stop, reads, writes, skip=False):
        if skip:
            P.op('pe', lambda e: e.matmul(out, lhsT=lhsT, rhs=rhs, start=start, stop=stop, skip_group_check=True), reads, writes)
        else:
            P.op('pe', lambda e: e.matmul(out, lhsT=lhsT, rhs=rhs, start=start, stop=stop), reads, writes)

    with contextlib.ExitStack() as ctx:
        ps = ctx.enter_context(nc.psum_tensor("ps", [128, 7, 512], F32))
        pst = ctx.enter_context(nc.psum_tensor("pst", [128, 1024], BF16))

        ident = alloc([128, 128], BF16, 'ident')
        identf = alloc([128, 128], F32, 'identf')
        subgbc = alloc([128, 128], F32, 'subgbc')
        lamt = alloc([128, 256], F32, 'lamt')
        lams = alloc([128, 8], F32, 'lams')
        psc = alloc([128, 4], F32, 'psc')
        cw = alloc([128, NJ * 3], F32, 'cw')
        cb = alloc([128, NJ], F32, 'cb')
        spec = alloc([128, 4, 4, 128], F32, 'spec')
        mskd = alloc([128, 128], F32, 'mskd')
        b31t = alloc([128, 4], F32, 'b31t')
        kbias = alloc([128, 8], F32, 'kbias')
        cfl = alloc([128, 1], F32, 'cfl')
        ofl = alloc([128, 1], F32, 'ofl')
        corrt = alloc([128, 4, 16], F32, 'corrt')
        epsn = alloc([128, 1], F32, 'epsn')
        epss = alloc([128, 1], F32, 'epss')
        stat = alloc([128, 16], F32, 'stat')
        junk = alloc([128, D], BF16, 'junk')
        poolwt = alloc([128, 4, 128], BF16, 'poolwt')
        hist = alloc([128, NJ, 2], F32, 'hist')
        yT = alloc([128, 8, 17 * 128], BF16, 'yT')

        def ld(q, key, out, in_, wr):
            P.dma(q, key, lambda e: e.dma_start(out=out, in_=in_), (), wr)

        ld('sp', 'c0', identf[:], identd, ['identf'])
        P.op('dve', lambda e: e.tensor_copy(out=ident[:], in_=identf[:]), ['identf'], ['ident'])
        P.op('dve', lambda e: e.memset(epsn[:], 1e-6), (), ['epsn'])
        P.op('dve', lambda e: e.memset(epss[:], 1e-5), (), ['epss'])
        P.op('dve', lambda e: e.memset(hist[:].rearrange("p j t -> p (j t)"), 0.0), (), ['hist'])
        P.op('dve', lambda e: e.memset(stat[:], 0.0), (), ['ss0', 'ss1', 'ss2', 'ss3', 'rstd', 'cfl8', 'fss', 'frs'])
        g1T = alloc([128, 8], F32, 'g1T')
        ld('sp', 'c1', g1T[:], g1t, ['g1T'])
        persist_end = cur[0]
        def setup_late():
            ld('sp', 'c3', subgbc[:], subg.broadcast_to([128, 128]), ['subgbc'])
            ld('sp', 'c4', lamt[:], lamv.broadcast_to([128, 256]), ['lamt'])
            ld('sp', 'c5', psc[:], pscale, ['psc'])
            ld('sp', 'c6', cw[:], convw, ['cw'])
            ld('sp', 'c7', cb[:], convb, ['cb'])
            ld('sp', 'c8', spec[:, 0, :, :].rearrange("p h q -> p (h q)"), nearb, ['spec0'])
            ld('sp', 'c9', spec[:, 1, :, :].rearrange("p h q -> p (h q)"), diagb, ['spec1'])
            ld('sp', 'c10', mskd[:], maskd, ['mskd'])
            ld('sp', 'c11', b31t[:], b31.broadcast_to([128, 4]), ['b31t'])
            ld('sp', 'c12', cfl[:], cflag, ['cfl'])
            ld('sp', 'c13', ofl[:], oflag, ['ofl'])
            ld('sp', 'c14', corrt[:].rearrange("p g t -> p (g t)"), corr, ['corrt'])
            ld('pool', 'c15', poolwt[:], pool_w.rearrange("g c d -> c g d"), ['poolwt'])

            P.op('dve', lambda e: e.tensor_scalar(out=subgbc[:], in0=subgbc[:], scalar1=1.0 - LAM_INIT, scalar2=None, op0=ALU.mult), ['subgbc'], ['subgbc'])
            P.op('dve', lambda e: e.tensor_tensor(out=lamt[:, 0:64], in0=lamt[:, 0:64], in1=lamt[:, 64:128], op=ALU.mult), ['lamt'], ['lamt'])
            P.op('dve', lambda e: e.tensor_tensor(out=lamt[:, 128:192], in0=lamt[:, 128:192], in1=lamt[:, 192:256], op=ALU.mult), ['lamt'], ['lamt'])
            P.op('dve', lambda e: e.reduce_sum(out=lams[:, 0:1], in_=lamt[:, 0:64], axis=mybir.AxisListType.X), ['lamt'], ['lams'])
            P.op('dve', lambda e: e.reduce_sum(out=lams[:, 1:2], in_=lamt[:, 128:192], axis=mybir.AxisListType.X), ['lamt'], ['lams'])
            P.op('act', lambda e: e.activation(out=lams[:, 2:4], in_=lams[:, 0:2], func=AF.Exp), ['lams'], ['lams'])
            P.op('dve', lambda e: e.tensor_tensor(out=lams[:, 4:5], in0=lams[:, 3:4], in1=lams[:, 2:3], op=ALU.subtract), ['lams'], ['lams'])
            P.op('dve', lambda e: e.tensor_scalar(out=lams[:, 4:5], in0=lams[:, 4:5], scalar1=-LAM_INIT, scalar2=None, op0=ALU.add), ['lams'], ['lams'])
            P.op('dve', lambda e: e.tensor_copy(out=kbias[:, 0:4], in_=b31t[:]), ['b31t'], ['kbias'])
            P.op('dve', lambda e: e.tensor_scalar(out=kbias[:, 4:8], in0=b31t[:], scalar1=cfl[:, 0:1], scalar2=None, op0=ALU.add), ['b31t', 'cfl', 'kbias'], ['kbias'])
            for h in range(NB):
                P.op('dve', lambda e, h=h: e.tensor_scalar(out=spec[:, 0, h, :], in0=spec[:, 0, h, :], scalar1=b31t[:, h:h + 1], scalar2=8.0, op0=ALU.subtract, op1=ALU.mult), ['spec0', 'b31t'], ['spec0'])
                P.op('dve', lambda e, h=h: e.tensor_scalar(out=spec[:, 1, h, :], in0=spec[:, 1, h, :], scalar1=b31t[:, h:h + 1], scalar2=None, op0=ALU.subtract), ['spec1', 'b31t'], ['spec1'])
                P.op('dve', lambda e, h=h: e.tensor_tensor(out=spec[:, 1, h, :], in0=spec[:, 1, h, :], in1=mskd[:], op=ALU.add), ['spec1', 'mskd'], ['spec1'])
                P.op('dve', lambda e, h=h: e.tensor_scalar(out=spec[:, 1, h, :], in0=spec[:, 1, h, :], scalar1=8.0, scalar2=None, op0=ALU.mult), ['spec1'], ['spec1'])
            P.op('dve', lambda e: e.tensor_scalar(out=stat[:, 8:9], in0=cfl[:, 0:1], scalar1=8.0, scalar2=None, op0=ALU.mult), ['cfl'], ['cfl8'])
            for ty in range(2):
                P.op('dve', lambda e, ty=ty: e.tensor_scalar(out=spec[:, 2 + ty, :, :].rearrange("p h q -> p (h q)"), in0=spec[:, ty, :, :].rearrange("p h q -> p (h q)"), scalar1=stat[:, 8:9], scalar2=None, op0=ALU.add), ['spec%d' % ty, 'cfl8'], ['spec%d' % (2 + ty)])


            return lams[:, 4:5]

        KT = alloc([128, NB, S], BF16, 'KT')
        V = alloc([128, 32, NB, 130], BF16, 'V')
        QT = alloc([128, NB, 17 * 128], BF16, 'QT')
        ab_end = cur[0]
        w_in_t = alloc([128, 8, 2048], BF16, 'w_in_t')
        xt = alloc([128, 2, D], F32, 'xt')
        hb = alloc([128, 2, D], BF16, 'hb')
        hTs = [alloc([128, 8, 512], BF16, 'hT%d' % i) for i in range(2)]
        zt = alloc([128, 4, 528], F32, 'zt')
        zs = alloc([128, 2, 528], F32, 'zs')
        pooled = alloc([128, 4, 512], BF16, 'pooled')
        print('phaseA end', cur[0], 'top', top)
        a_end = cur[0]

        def load_w_in(c4):
            ld('pool', 'win%d' % c4, w_in_t[:, :, c4 * 512:(c4 + 1) * 512],
               w_in.rearrange("(k p) n -> p k n", p=128)[:, :, c4 * 512:(c4 + 1) * 512], ['w_in:%d' % c4])

        load_w_in(2)
        load_w_in(3)
        P.op('pool', lambda e: e.memset(V[:].rearrange("p a b c -> p (a b c)"), 1.0), (), ['V'])
        P.op('pool', lambda e: e.memset(zt[:].rearrange("p a b -> p (a b)"), 0.0), (), ['zt:%d' % g for g in range(4)] + ['zth:%d' % g for g in range(4)])

        bank = [0]

        def nextbank(nbanks=7):
            b = bank[0] % nbanks
            bank[0] += 1
            return b

        def norm_rows(src, ns, gain_key, gain, out_bf, src_keys, out_keys, eps_t, dim):
            for s in range(ns):
                P.op('act', lambda e, s=s: e.activation(out=junk[:, 0:dim], in_=src(s), func=AF.Square, accum_out=stat[:, s:s + 1]),
                     src_keys(s), ['ss%d' % s, 'junk'])
            P.op('act', lambda e: e.activation(out=stat[:, 4:4 + ns], in_=stat[:, 0:ns], func=AF.Ln, scale=1.0 / dim, bias=eps_t[:]),
                 ['ss%d' % s for s in range(ns)] + ['epsn', 'epss'], ['rstd'])
            P.op('act', lambda e: e.activation(out=stat[:, 4:4 + ns], in_=stat[:, 4:4 + ns], func=AF.Exp, scale=-0.5), ['rstd'], ['rstd'])
            for s in range(ns):
                if gain is None:
                    P.op('dve', lambda e, s=s: e.tensor_scalar(out=out_bf(s), in0=src(s), scalar1=stat[:, 4 + s:5 + s], scalar2=None, op0=ALU.mult),
                         src_keys(s) + ['rstd'], out_keys(s))
                else:
                    P.op('dve', lambda e, s=s: e.scalar_tensor_tensor(out=out_bf(s), in0=src(s), scalar=stat[:, 4 + s:5 + s], in1=gain, op0=ALU.mult, op1=ALU.mult),
                         src_keys(s) + ['rstd', gain_key], out_keys(s))
            P.op('dve', lambda e: e.memset(stat[:, 0:4], 0.0), (), ['ss%d' % s for s in range(4)])

        def tr_round(src_bf, ns, dstT, col0, kp, src_keys, dst_key, gscale=None):
            if kp % 2 == 1:
                return
            q = kp // 2
            for kk in range(4):
                k = q * 4 + kk
                for s in range(ns):
                    P.op('pe', lambda e, k=k, kk=kk, s=s: e.transpose(pst[:, kk * 256 + s * 128: kk * 256 + (s + 1) * 128], src_bf(s)[:, k * 128:(k + 1) * 128], ident[:]),
                         src_keys(s) + ['ident'], ['pst'])
            if gscale is None:
                evac(dstT[:, 4 * q:4 * q + 4, col0:col0 + ns * 128],
                     pst[:].rearrange("p (a b) -> p a b", a=4)[:, :, 0:ns * 128], ['pst'], [dst_key])
            else:
                for kk in range(4):
                    k = q * 4 + kk
                    evac(dstT[:, k, col0:col0 + ns * 128], pst[:, kk * 256:kk * 256 + ns * 128], ['pst', 'g1T'], [dst_key],
                         scale=gscale[:, k:k + 1])

        def a_norm(t, half):
            r0 = t * 512 + half * 256
            P.dma('sp', 'xt', lambda e: e.dma_start(out=xt[:], in_=xin[r0:r0 + 256, :].rearrange("(s p) d -> p s d", p=128)),
                  (), ['xt:%d' % s for s in range(2)])
            norm_rows(lambda s: xt[:, s, :], 2, None, None, lambda s: hb[:, s, :],
                      lambda s: ['xt:%d' % s], lambda s: ['hb:%d' % s], epsn, D)

        def a_round(t, r):
            half, kp = r // 4, r % 4
            tr_round(lambda s: hb[:, s, :], 2, hTs[t % 2], half * 256, kp, lambda s: ['hb:%d' % s], 'hT%d' % (t % 2), gscale=g1T)

        def a_groups(t):
            own = t >= 4
            full = t >= 3
            hT = hTs[t % 2]
            hk = 'hT%d' % (t % 2)
            grps = []

            def chunk(m):
                b = nextbank()
                n0 = 0 if (own or m >= 8) else (256 if m < 4 else 384)
                for k in range(8):
                    mm(ps[:, b, n0:512], w_in_t[:, k, m * 128:(m + 1) * 128], hT[:, k, n0:512], k == 0, k == 7,
                       ['w_in:%d' % (m // 4), hk], ['ps%d' % b])
                if m < 4:
                    P.op('act', lambda e: e.activation(out=zt[:, m, 16 + n0:528], in_=ps[:, b, n0:512], func=AF.Copy), ['ps%d' % b], ['zt:%d' % m])
                elif m < 8:
                    h = m - 4
                    if own:
                        c0 = (16 + 4 * (t - 4) - 15) * 128
                        evac(QT[:, h, c0:c0 + 512], ps[:, b, :], ['ps%d' % b], ['QT'])
                    else:
                        evac(QT[:, h, 0:128], ps[:, b, 384:512], ['ps%d' % b], ['QT'])
                else:
                    h = m - 8
                    evac(KT[:, h, t * 512:(t + 1) * 512], ps[:, b, :], ['ps%d' % b], ['KT'])

            def vgrp(s):
                b = nextbank()
                for k in range(8):
                    mm(ps[:, b, :], hT[:, k, s * 128:(s + 1) * 128], w_in_t[:, k, 1536:2048], k == 0, k == 7,
                       ['w_in:3', hk], ['ps%d' % b])
                evac(V[:, t * 4 + s, :, 0:128], ps[:, b, :].rearrange("p (h e) -> p h e", h=NB), ['ps%d' % b], ['V'])

            if full:
                for m in range(0, 4):
                    grps.append(lambda m=m: chunk(m))
            for m in range(8, 12):
                grps.append(lambda m=m: chunk(m))
            for s in range(4):
                grps.append(lambda s=s: vgrp(s))
            if full:
                for m in range(4, 8):
                    grps.append(lambda m=m: chunk(m))
            return grps

        def a_pool(t):
            own = t >= 4
            if t == 4:
                P.op('dve', lambda e: e.tensor_scalar(out=zt[:, :, 0:16], in0=zt[:, :, 0:16], scalar1=ofl[:, 0:1], scalar2=None, op0=ALU.mult),
                     ['zth:%d' % g for g in range(4)] + ['ofl'], ['zth:%d' % g for g in range(4)])
            for g in range(4):
                zk = 'zt:%d' % g
                z = zt[:, g, :]
                P.op('dve', lambda e, z=z: e.tensor_tensor(out=zs[:, 0, 1:528], in0=z[:, 1:528], in1=z[:, 0:527], op=ALU.add), [zk, 'zth:%d' % g, 'zs0'], ['zs0'])
                curi = 0
                sh = 2
                for step in range(g):
                    nxt = 1 - curi
                    P.op('dve', lambda e, curi=curi, nxt=nxt, sh=sh: e.tensor_tensor(out=zs[:, nxt, 2 * sh - 1:528], in0=zs[:, curi, 2 * sh - 1:528], in1=zs[:, curi, sh - 1:528 - sh], op=ALU.add),
                         ['zs%d' % curi, 'zs%d' % nxt], ['zs%d' % nxt])
                    curi = nxt
                    sh *= 2
                w = 2 ** (g + 1)
                if t == 4:
                    P.op('dve', lambda e, curi=curi, g=g: e.tensor_tensor(out=zs[:, curi, 16:32], in0=zs[:, curi, 16:32], in1=corrt[:, g, :], op=ALU.mult),
                         ['zs%d' % curi, 'corrt'], ['zs%d' % curi])
                pg_ = g
                pk_ = 'pooled:%d' % pg_
                P.op('dve', lambda e, curi=curi, z=z, w=w, pg_=pg_: e.scalar_tensor_tensor(out=pooled[:, pg_, :], in0=zs[:, curi, 16:528], scalar=1.0 / w, in1=z[:, 16:528], op0=ALU.mult, op1=ALU.subtract),
                     ['zs%d' % curi, zk, pk_], [pk_])
                P.op('dve', lambda e, z=z: e.tensor_copy(out=z[:, 0:16], in_=z[:, 512:528]), [zk], ['zth:%d' % g])
            force_act[0] = 14

        def a_pool_mm(t):
            own = t >= 4
            for g in range(4):
                pk_ = 'pooled:%d' % g
                b = nextbank()
                if own:
                    c0 = (16 + 4 * (t - 4) - 15) * 128
                    mm(ps[:, b, :], poolwt[:, g, :], pooled[:, g, :], True, True, ['poolwt', pk_], ['ps%d' % b])
                    evac(yT[:, g, c0:c0 + 512], ps[:, b, :], ['ps%d' % b, 'psc'], ['yT:%d' % g], scale=psc[:, g:g + 1])
                else:
                    mm(ps[:, b, 0:128], poolwt[:, g, :], pooled[:, g, 384:512], True, True, ['poolwt', pk_], ['ps%d' % b])
                    evac(yT[:, g, 0:128], ps[:, b, 0:128], ['ps%d' % b, 'psc'], ['yT:%d' % g], scale=psc[:, g:g + 1])

        a_norm(0, 0)
        for r in range(4):
            a_round(0, r)
        a_norm(0, 1)
        for r in range(4, 8):
            a_round(0, r)
        neglam = setup_late()
        for t in range(8):
            grps = a_groups(t)
            nxt = t + 1 < 8
            if nxt:
                a_norm(t + 1, 0)
            for i, gfn in enumerate(grps):
                gfn()
                if nxt:
                    if i == 0:
                        a_round(t + 1, 0)
                        a_round(t + 1, 1)
                    elif i == 1:
                        a_round(t + 1, 2)
                        a_round(t + 1, 3)
                        a_norm(t + 1, 1)
                    elif 4 <= i < 8:
                        a_round(t + 1, i)
                if t >= 3 and i == 7:
                    a_pool(t)
            if t == 0:
                load_w_in(0)
                load_w_in(1)
            if t >= 3:
                a_pool_mm(t)

        cur[0] = ab_end
        w_out_t = alloc([128, 8, D], BF16, 'w_out_t')
        ring = [alloc([128, 8, 256], BF16, 'ring%d' % i) for i in range(4)]
        bc_start = cur[0]
        Pt = [alloc([128, 2, 512], BF16, 'Pt%d' % i) for i in range(2)]
        ya = alloc([128, 4, 512], BF16, 'ya')
        accs = alloc([128, 3, 512], F32, 'accs')
        oall = alloc([128, 4, 128], F32, 'oall')
        sqall = alloc([128, 4, 128], F32, 'sqall')
        stb = alloc([128, 16], F32, 'stb')
        phaseA_bufs = ['w_in', 'xt', 'hb', 'hT0', 'hT1', 'zt', 'zth', 'zs0', 'zs1', 'pooled']
        phaseB_bufs = ['Pt0', 'Pt1', 'ya', 'accs', 'oall:0', 'oall:1', 'oall:2', 'oall:3', 'sqall:0', 'sqall:1', 'sqall:2', 'sqall:3', 'rs', 'sss', 'srs']
        P.alias(['w_out', 'Pt0', 'Pt1', 'ya:0', 'ya:1', 'ya:2', 'ya:3', 'accs', 'oall:0', 'oall:1', 'oall:2', 'oall:3', 'sqall:0', 'sqall:1', 'sqall:2', 'sqall:3', 'rs', 'sss', 'srs'] + ['ring%d' % i for i in range(4)], phaseA_bufs)
        ld('pool', 'wout', w_out_t[:], w_out.rearrange("(k p) n -> p k n", p=128), ['w_out'])

        tiles = [(15, 1)] + [(16 + 4 * i, 4) for i in range(4)]
        ring_seq = [(ti, j) for ti in range(1, len(tiles)) for j in range(NJ)]

        def issue_ring(n):
            if n >= len(ring_seq):
                return
            ti, j = ring_seq[n]
            ri = n % 4
            rk = 'ring%d' % ri
            P.dma('pool', rk, lambda e: e.dma_start(out=ring[ri][:].rearrange("p k n -> p (k n)"), in_=wgu[j]), (), [rk])

        for n in range(4):
            issue_ring(n)

        groups = [(15, 1)] + [(16 + 4 * i, 4) for i in range(4)]
        ACC_BANK0 = 4
        HQ = 32
        for i_ in range(2):
            P.op('dve', lambda e, i_=i_: e.memset(Pt[i_][:].rearrange("p a b -> p (a b)"), 0.0), (), ['Pt%d' % i_])
        iters = []
        for (g0, nq) in groups:
            for h in range(NB):
                for kb in range(g0 + nq):
                    iters.append((g0, nq, h, kb))

        def emit_qk(idx):
            g0, nq, h, kb = iters[idx]
            qc0 = (g0 - 15) * 128
            a = max(0, kb - g0)
            n0, n1 = a * 128, nq * 128
            if nq == 1:
                n0 = 128 - HQ
            sb = 2 * (idx % 2)
            skeys = ['ps%d' % sb, 'ps%d' % (sb + 1)]
            for c in range(2):
                mm(ps[:, sb + c, n0:n1], KT[c * 64:(c + 1) * 64, h, kb * 128:(kb + 1) * 128],
                   QT[c * 64:(c + 1) * 64, h, qc0 + n0:qc0 + n1], True, True, ['KT', 'QT'], [skeys[c]])
            ctxk = kb < 16
            for qi in range(a, nq):
                d = g0 + qi - kb
                if d > 1:
                    continue
                ty = (0 if d == 1 else 1) + (2 if ctxk else 0)
                q0 = n0 if nq == 1 else qi * 128
                for c in range(2):
                    P.op('dve', lambda e, c=c, qi=qi, ty=ty, q0=q0: e.tensor_tensor(
                        out=ps[:, sb + c, q0:(qi + 1) * 128], in0=ps[:, sb + c, q0:(qi + 1) * 128],
                        in1=spec[:, ty, h, q0 - qi * 128:128], op=ALU.add), [skeys[c], 'spec%d' % ty], [skeys[c]])

        def emit_exp(idx):
            g0, nq, h, kb = iters[idx]
            a = max(0, kb - g0)
            n0, n1 = a * 128, nq * 128
            if nq == 1:
                n0 = 128 - HQ
            si = idx % 2
            sb = 2 * si
            skeys = ['ps%d' % sb, 'ps%d' % (sb + 1)]
            ctxk = kb < 16
            bc = 4 + h if ctxk else h
            bcol = kbias[:, bc:bc + 1]
            pk = 'Pt%d' % si
            P.op('act', lambda e: e.activation(
                out=Pt[si][:, :, n0:n1], in_=ps[:, sb:sb + 2, n0:n1], func=AF.Exp, bias=bcol, scale=0.125),
                skeys + ['kbias'], [pk])

        def emit_av(idx):
            g0, nq, h, kb = iters[idx]
            a = max(0, kb - g0)
            si = idx % 2
            pk = 'Pt%d' % si
            for qi in range(a, nq):
                for c in range(2):
                    ai = qi * 2 + c
                    ab = ACC_BANK0 + ai // 3
                    ac = (ai % 3) * 160
                    first_in_bank = (ai % 3 == 0)
                    mm(ps[:, ab, ac:ac + 129], Pt[si][:, c, qi * 128:(qi + 1) * 128], V[:, kb, h, 0:129],
                       (kb == 0 and first_in_bank), kb == g0 + qi, [pk, 'V'], ['ps%d' % ab], skip=True)

        def emit_head_end(g0, nq, h):
            nbk = (2 * nq + 2) // 3
            for bk in range(nbk):
                P.op('dve', lambda e, bk=bk: e.tensor_copy(out=accs[:, bk, 0:480], in_=ps[:, ACC_BANK0 + bk, 0:480]),
                     ['ps%d' % (ACC_BANK0 + bk), 'accs'], ['accs'])

            def acc(ai):
                return accs[:, ai // 3, (ai % 3) * 160:(ai % 3) * 160 + 129]
            for ai in range(2 * nq):
                P.op('dve', lambda e, ai=ai: e.tensor_scalar(out=stb[:, ai:ai + 1], in0=acc(ai)[:, 128:129], scalar1=1e-30, scalar2=None, op0=ALU.max),
                     ['accs', 'rs'], ['rs'])
            P.op('dve', lambda e: e.reciprocal(out=stb[:, 0:2 * nq], in_=stb[:, 0:2 * nq]), ['rs'], ['rs'])
            for qi in range(nq):
                P.op('dve', lambda e, qi=qi: e.tensor_tensor(out=stb[:, 2 * qi + 1:2 * qi + 2], in0=stb[:, 2 * qi + 1:2 * qi + 2], in1=neglam, op=ALU.mult), ['rs', 'lams'], ['rs'])
            for qi in range(nq):
                P.op('dve', lambda e, qi=qi: e.tensor_scalar(out=sqall[:, qi, :], in0=acc(2 * qi + 1)[:, 0:128], scalar1=stb[:, 2 * qi + 1:2 * qi + 2], scalar2=None, op0=ALU.mult),
                     ['accs', 'rs', 'sqall:%d' % qi], ['sqall:%d' % qi])
                P.op('dve', lambda e, qi=qi: e.scalar_tensor_tensor(out=oall[:, qi, :], in0=acc(2 * qi)[:, 0:128], scalar=stb[:, 2 * qi:2 * qi + 1], in1=sqall[:, qi, :], op0=ALU.mult, op1=ALU.add),
                     ['accs', 'rs', 'sqall:%d' % qi, 'oall:%d' % qi], ['oall:%d' % qi])
            sqk = ['sqall:%d' % qi for qi in range(nq)]
            oak = ['oall:%d' % qi for qi in range(nq)]
            P.op('dve', lambda e: e.tensor_tensor(out=sqall[:, 0:nq, :], in0=oall[:, 0:nq, :], in1=oall[:, 0:nq, :], op=ALU.mult), oak + sqk, sqk)
            P.op('dve', lambda e: e.reduce_sum(out=stb[:, 8:8 + nq], in_=sqall[:, 0:nq, :], axis=mybir.AxisListType.X), sqk, ['sss'])

        def emit_head_end2(g0, nq, h):
            P.op('act', lambda e: e.activation(out=stb[:, 12:12 + nq], in_=stb[:, 8:8 + nq], func=AF.Ln, scale=1.0 / 128, bias=epss[:]), ['sss', 'epss'], ['srs'])
            P.op('act', lambda e: e.activation(out=stb[:, 12:12 + nq], in_=stb[:, 12:12 + nq], func=AF.Exp, scale=-0.5), ['srs'], ['srs'])
            for qi in range(nq):
                P.op('dve', lambda e, qi=qi: e.scalar_tensor_tensor(out=ya[:, qi, h * 128:(h + 1) * 128], in0=oall[:, qi, :], scalar=stb[:, 12 + qi:13 + qi], in1=subgbc[:], op0=ALU.mult, op1=ALU.mult),
                     ['oall:%d' % qi, 'srs', 'subgbc'], ['ya:%d' % qi])

        def emit_group_end(g0, nq):
            qc0 = (g0 - 15) * 128
            for qi in range(nq):
                for h in range(NB):
                    P.op('pe', lambda e, qi=qi, h=h: e.transpose(pst[:, h * 128:(h + 1) * 128], ya[:, qi, h * 128:(h + 1) * 128], ident[:]),
                         ['ya:%d' % qi, 'ident'], ['pst'])
                cc = qc0 + qi * 128
                evac(yT[:, 4:8, cc:cc + 128], pst[:, 0:512].rearrange("p (h q) -> p h q", h=NB), ['pst'], ['yT:att'])

        emit_qk(0)
        emit_qk(1)
        pending = []
        for idx in range(len(iters)):
            emit_exp(idx)
            if idx + 2 < len(iters):
                emit_qk(idx + 2)
            emit_av(idx)
            while pending and pending[0][0] <= idx:
                pending.pop(0)[1]()
            g0, nq, h, kb = iters[idx]
            if kb == g0 + nq - 1:
                emit_head_end(g0, nq, h)

                def part2(g0=g0, nq=nq, h=h):
                    emit_head_end2(g0, nq, h)
                    if h == NB - 1:
                        emit_group_end(g0, nq)
                pending.append((idx + 10, part2))
        while pending:
            pending.pop(0)[1]()

        cur[0] = persist_end
        w_fo_t = alloc([128, NJ, D], BF16, 'w_fo_t')
        actT = alloc([128, NJ, 512], BF16, 'actT')
        h2T = alloc([128, 8, 512], BF16, 'h2T')
        h2Th = alloc([128, 8, 128], BF16, 'h2Th')
        cbuf = [alloc([128, 512], F32, 'cbuf%d' % i) for i in range(2)]
        assert cur[0] <= ab_end, (cur[0], ab_end)
        cur[0] = bc_start
        xt2s = [alloc([128, 4, D], F32, 'xt2_%d' % i) for i in range(2)]
        h2 = alloc([128, 2, D], BF16, 'h2')
        sgb = [alloc([128, 512], F32, 'sgb%d' % i) for i in range(2)]
        save2 = cur[0]
        cur[0] = offs['spec']
        g2bc = alloc([128, D], F32, 'g2bc')
        gfbc = alloc([128, D], F32, 'gfbc')
        assert cur[0] <= offs['mskd'], (cur[0], offs['mskd'])
        cur[0] = save2
        newC = (['xt2_%d:%d' % (b, i) for b in range(2) for i in range(4)] + ['h2:0', 'h2:1', 'g2bc', 'gfbc', 'w_fo:0', 'w_fo:1', 'w_fo:2', 'w_fo:3', 'actT', 'h2T', 'h2Th',
                'cbuf0', 'cbuf1', 'sgb0', 'sgb1'])
        P.alias(newC, phaseA_bufs + phaseB_bufs + ['KT', 'V', 'QT', 'spec0', 'spec1', 'spec2', 'spec3'])
        ld('sp', 'g2', g2bc[:], g2.broadcast_to([128, D]), ['g2bc'])
        ld('sp', 'c2', gfbc[:], gf.broadcast_to([128, D]), ['gfbc'])

        def c_load(ti):
            b0, nb = tiles[ti]
            xt2 = xt2s[ti % 2]
            xk = 'xt2_%d' % (ti % 2)
            P.dma('sp', xk, lambda e: e.dma_start(out=xt2[:, 0:nb, :], in_=xin[b0 * 128:(b0 + nb) * 128, :].rearrange("(s p) d -> p s d", p=128)),
                  (), [xk + ':%d' % s for s in range(nb)])

        def c_outproj(ti):
            b0, nb = tiles[ti]
            xt2 = xt2s[ti % 2]
            xk = 'xt2_%d' % (ti % 2)
            yc0 = (b0 - 15) * 128
            for s in range(nb):
                for n2 in range(2):
                    b = nextbank()
                    for k in range(8):
                        yk = 'yT:%d' % k if k < 4 else 'yT:att'
                        mm(ps[:, b, :], yT[:, k, yc0 + s * 128: yc0 + (s + 1) * 128], w_out_t[:, k, n2 * 512:(n2 + 1) * 512], k == 0, k == 7,
                           [yk, 'w_out'], ['ps%d' % b])
                    P.op('dve', lambda e, s=s, n2=n2, b=b: e.tensor_tensor(out=xt2[:, s, n2 * 512:(n2 + 1) * 512], in0=ps[:, b, :], in1=xt2[:, s, n2 * 512:(n2 + 1) * 512], op=ALU.add),
                         ['ps%d' % b, xk + ':%d' % s], [xk + ':%d' % s])

        def c_norm(ti, half):
            b0, nb = tiles[ti]
            xt2 = xt2s[ti % 2]
            xk = 'xt2_%d' % (ti % 2)
            ns = min(2, nb - 2 * half)
            if ns <= 0:
                return
            norm_rows(lambda s: xt2[:, 2 * half + s, :], ns, 'g2bc', g2bc[:], lambda s: h2[:, s, :],
                      lambda s: [xk + ':%d' % (2 * half + s)], lambda s: ['h2:%d' % s], epsn, D)

        def c_round(ti, r):
            b0, nb = tiles[ti]
            half, kp = r // 4, r % 4
            ns = min(2, nb - 2 * half)
            if ns <= 0:
                return
            if ti == 0:
                tr_round(lambda s: h2[:, s, :], ns, h2Th, 0, kp, lambda s: ['h2:%d' % s], 'h2Th')
            else:
                tr_round(lambda s: h2[:, s, :], ns, h2T, half * 256, kp, lambda s: ['h2:%d' % s], 'h2T')

        ring_n = [0]

        WFO_PIECES = [(0, 6), (6, 12), (12, 17), (17, 22)]

        def wfo_piece(j):
            for i, (a_, b_) in enumerate(WFO_PIECES):
                if a_ <= j < b_:
                    return i

        def c_ffn_in(ti):
            b0, nb = tiles[ti]
            N = nb * 128
            for j in range(NJ):
                if ti == 1 and j in (1, 6, 11, 16):
                    pc = (1, 6, 11, 16).index(j)
                    a_, b_ = WFO_PIECES[pc]
                    ld('pool', 'wfo%d' % pc, w_fo_t[:, a_:b_, :],
                       w_fo.rearrange("(j p) n -> p j n", p=128)[:, a_:b_, :], ['w_fo:%d' % pc])
                n = ring_n[0]
                ring_n[0] += 1
                ri = n % 4
                rk = 'ring%d' % ri
                if ti == 1:
                    bh = nextbank()
                    for k in range(8):
                        mm(ps[:, bh, 0:2], ring[ri][:, k, 0:128], h2Th[:, k, 126:128], k == 0, k == 7, [rk, 'h2Th'], ['ps%d' % bh])
                    P.op('act', lambda e, bh=bh, j=j: e.activation(out=hist[:, j, :], in_=ps[:, bh, 0:2], func=AF.Copy, scale=ofl[:, 0:1]),
                         ['ps%d' % bh, 'hist', 'ofl'], ['hist'])
                bg = nextbank()
                for k in range(8):
                    mm(ps[:, bg, 0:N], ring[ri][:, k, 0:128], h2T[:, k, 0:N], k == 0, k == 7, [rk, 'h2T'], ['ps%d' % bg])
                bu = nextbank()
                for k in range(8):
                    mm(ps[:, bu, 0:N], ring[ri][:, k, 128:256], h2T[:, k, 0:N], k == 0, k == 7, [rk, 'h2T'], ['ps%d' % bu])
                issue_ring(n + 4)
                ci = j % 2
                ck = 'cbuf%d' % ci
                cbt = cbuf[ci]
                pg = ps[:, bg, :]
                P.op('act', lambda e, cbt=cbt, pg=pg, j=j: e.activation(out=cbt[:, 0:N], in_=pg[:, 0:N], func=AF.Identity, scale=cw[:, 3 * j + 2:3 * j + 3], bias=cb[:, j:j + 1]),
                     ['ps%d' % bg, 'cw', 'cb', ck], [ck])
                P.op('dve', lambda e, cbt=cbt, pg=pg, j=j: e.scalar_tensor_tensor(out=cbt[:, 1:N], in0=pg[:, 0:N - 1], scalar=cw[:, 3 * j + 1:3 * j + 2], in1=cbt[:, 1:N], op0=ALU.mult, op1=ALU.add),
                     ['ps%d' % bg, 'cw', ck], [ck])
                P.op('dve', lambda e, cbt=cbt, pg=pg, j=j: e.scalar_tensor_tensor(out=cbt[:, 2:N], in0=pg[:, 0:N - 2], scalar=cw[:, 3 * j:3 * j + 1], in1=cbt[:, 2:N], op0=ALU.mult, op1=ALU.add),
                     ['ps%d' % bg, 'cw', ck], [ck])
                P.op('dve', lambda e, cbt=cbt, j=j: e.scalar_tensor_tensor(out=cbt[:, 0:1], in0=hist[:, j, 1:2], scalar=cw[:, 3 * j + 1:3 * j + 2], in1=cbt[:, 0:1], op0=ALU.mult, op1=ALU.add),
                     ['hist', 'cw', ck], [ck])
                P.op('dve', lambda e, cbt=cbt, j=j: e.scalar_tensor_tensor(out=cbt[:, 0:2], in0=hist[:, j, 0:2], scalar=cw[:, 3 * j:3 * j + 1], in1=cbt[:, 0:2], op0=ALU.mult, op1=ALU.add),
                     ['hist', 'cw', ck], [ck])
                P.op('act', lambda e, pg=pg, j=j: e.activation(out=hist[:, j, :], in_=pg[:, N - 2:N], func=AF.Copy), ['ps%d' % bg, 'hist', ck], ['hist'])
                sk = 'sgb%d' % ci
                sgt = sgb[ci]
                P.op('act', lambda e, sgt=sgt, cbt=cbt: e.activation(out=sgt[:, 0:N], in_=cbt[:, 0:N], func=AF.Silu), [ck, sk], [sk])
                P.op('dve', lambda e, sgt=sgt, bu=bu, j=j: e.tensor_tensor(out=actT[:, j, 0:N], in0=ps[:, bu, 0:N], in1=sgt[:, 0:N], op=ALU.mult),
                     ['ps%d' % bu, sk, 'actT'], ['actT'])

        def c_ffn_out_group(ti, s, n2):
            xt2 = xt2s[ti % 2]
            xk = 'xt2_%d' % (ti % 2)
            b = nextbank()
            for j in range(NJ):
                mm(ps[:, b, :], actT[:, j, s * 128:(s + 1) * 128], w_fo_t[:, j, n2 * 512:(n2 + 1) * 512], j == 0, j == NJ - 1,
                   ['actT', 'w_fo:%d' % wfo_piece(j)], ['ps%d' % b])
            P.op('dve', lambda e: e.tensor_tensor(out=xt2[:, s, n2 * 512:(n2 + 1) * 512], in0=ps[:, b, :], in1=xt2[:, s, n2 * 512:(n2 + 1) * 512], op=ALU.add),
                 ['ps%d' % b, xk + ':%d' % s], [xk + ':%d' % s])

        def c_final_s(ti, s):
            b0, nb = tiles[ti]
            xt2 = xt2s[ti % 2]
            xk = 'xt2_%d' % (ti % 2)
            P.op('act', lambda e: e.activation(out=junk[:], in_=xt2[:, s, :], func=AF.Square, accum_out=stat[:, 14:15]),
                 [xk + ':%d' % s], ['fss', 'junk'])
            P.op('act', lambda e: e.activation(out=stat[:, 15:16], in_=stat[:, 14:15], func=AF.Ln, scale=1.0 / D, bias=epsn[:]),
                 ['fss', 'epsn'], ['frs'])
            P.op('act', lambda e: e.activation(out=stat[:, 15:16], in_=stat[:, 15:16], func=AF.Exp, scale=-0.5), ['frs'], ['frs'])
            P.op('dve', lambda e: e.scalar_tensor_tensor(out=xt2[:, s, :], in0=xt2[:, s, :], scalar=stat[:, 15:16], in1=gfbc[:], op0=ALU.mult, op1=ALU.mult),
                 [xk + ':%d' % s, 'frs', 'gfbc'], [xk + ':%d' % s])
            P.op('dve', lambda e: e.memset(stat[:, 14:15], 0.0), (), ['fss'])
            o0 = (b0 - 16) * 128 + s * 128
            P.dma('sp', 'out%d' % (ti % 2), lambda e: e.dma_start(out=yout[o0:o0 + 128, :], in_=xt2[:, s, :]),
                  [xk + ':%d' % s], ['yout%d' % (ti % 2)])

        nt = len(tiles)
        c_load(0)
        c_outproj(0)
        c_norm(0, 0)
        for r in range(4):
            c_round(0, r)
        c_load(1)
        c_outproj(1)
        c_norm(1, 0)
        for r in range(4):
            c_round(1, r)
        c_norm(1, 1)
        for r in range(4, 8):
            c_round(1, r)
        for ti in range(1, nt):
            nxt = ti + 1 < nt
            if nxt:
                c_load(ti + 1)
            c_ffn_in(ti)
            if nxt:
                c_outproj(ti + 1)
                c_norm(ti + 1, 0)
            for g in range(8):
                c_ffn_out_group(ti, g // 2, g % 2)
                if nxt:
                    if g == 4:
                        c_norm(ti + 1, 1)
                    c_round(ti + 1, g)
                if g % 2 == 1:
                    c_final_s(ti, g // 2)
        P.finish('sp', ['yout0', 'yout1'])
        P.emit(ctx)
    return nc


def _bucket(n):
    n = np.maximum(n, 0)
    nf = np.maximum(n, 1).astype(np.float32)
    large = 16 + (np.log(nf / np.float32(16)) / np.float32(math.log(128 / 16)) * np.float32(16)).astype(np.int32)
    large = np.minimum(large, 31)
    return np.where(n < 16, n, large)


_NC_CACHE = {}


def kernel(x, norm_mix_g, w_in, pool_w, pool_scale, lambda_q1, lambda_k1, lambda_q2, lambda_k2,
           subln_g, rel_bias, w_out, norm_ffn_g, ffn_w_in, ffn_conv_w, ffn_conv_b, ffn_w_out, norm_final_g):
    f = lambda a: np.ascontiguousarray(np.asarray(a, dtype=np.float32))
    x = f(x)
    rel_bias = f(rel_bias)
    kk = np.arange(128)[:, None]
    qq = np.arange(128)[None, :]
    near_idx = _bucket(128 + qq - kk)
    diag_idx = _bucket(qq - kk)
    nearb = np.ascontiguousarray(np.transpose(rel_bias[near_idx], (0, 2, 1)).reshape(128, 512))
    diagb = np.ascontiguousarray(np.transpose(rel_bias[diag_idx], (0, 2, 1)).reshape(128, 512))
    maskd = np.where(kk <= qq, 0.0, NEGM).astype(np.float32)
    fw = f(ffn_w_in)[0]
    wg = fw[:, :DFF].reshape(8, 128, NJ, 128)
    wu = fw[:, DFF:].reshape(8, 128, NJ, 128)
    wgu = np.ascontiguousarray(np.concatenate([wg, wu], axis=3).transpose(2, 1, 0, 3).reshape(NJ, 128, 8 * 256))
    convw = np.ascontiguousarray(f(ffn_conv_w)[0].reshape(3, NJ, 128).transpose(2, 1, 0).reshape(128, NJ * 3))
    convb = np.ascontiguousarray(f(ffn_conv_b)[0].reshape(NJ, 128).T)
    shared = {
        "w_in": f(w_in)[0], "w_out": f(w_out)[0], "wgu": wgu, "w_fo": f(ffn_w_out)[0],
        "pool_w": f(pool_w)[0], "g1t": np.ascontiguousarray(f(norm_mix_g).reshape(8, 128).T), "g2": f(norm_ffn_g), "gf": f(norm_final_g).reshape(1, D),
        "subg": f(subln_g), "lamv": np.concatenate([f(lambda_q1), f(lambda_k1), f(lambda_q2), f(lambda_k2)], axis=1),
        "pscale": np.ascontiguousarray(f(pool_scale)[0].reshape(4, 128).T), "convw": convw, "convb": convb,
        "nearb": nearb, "diagb": diagb, "maskd": maskd, "b31": np.ascontiguousarray(rel_bias[31:32, :]),
        "identd": np.eye(128, dtype=np.float32),
    }
    tpos = np.arange(16)
    corrA = np.stack([w / np.minimum(tpos + 1, w) for w in (2, 4, 8, 16)], 0).astype(np.float32)
    in_maps = []
    for c in range(8):
        b, role = c // 2, c % 2
        m = dict(shared)
        if role == 0:
            m["xin"] = np.ascontiguousarray(np.concatenate([x[b, 2048:], x[b, :2048]], axis=0))
            m["cflag"] = np.full((128, 1), NEGM, np.float32)
            m["oflag"] = np.zeros((128, 1), np.float32)
            m["corr"] = np.ascontiguousarray(np.broadcast_to(corrA.reshape(1, 64), (128, 64)))
        else:
            m["xin"] = x[b]
            m["cflag"] = np.zeros((128, 1), np.float32)
            m["oflag"] = np.ones((128, 1), np.float32)
            m["corr"] = np.ones((128, 64), np.float32)
        in_maps.append(m)
    if "nc" not in _NC_CACHE:
        _NC_CACHE["nc"] = build_nc()
    res = run_bass_kernel_spmd(_NC_CACHE["nc"], in_maps, core_ids=list(range(8)))
    out = np.empty((4, S, D), np.float32)
    for c in range(8):
        b, role = c // 2, c % 2
        out[b, role * 2048:(role + 1) * 2048] = res.results[c]["yout"]
    return out
```

```python
import contextlib
import math
import numpy as np
import concourse.bass as bass
import concourse.mybir as mybir
from concourse.bass_utils import run_bass_kernel_spmd

F32 = mybir.dt.float32
BF16 = mybir.dt.bfloat16
ALU = mybir.AluOpType
AF = mybir.ActivationFunctionType

D = 1024
S = 4096
NB = 4
DFF = 2816
NJ = DFF // 128
NEGM = -30000.0
LAM_INIT = 0.8 - 0.6 * math.exp(0.0)


class Prog:
    ENGS = ['pe', 'act', 'dve', 'pool', 'sp']

    def __init__(self, nc):
        self.nc = nc
        self.streams = {e: [] for e in self.ENGS}
        self.count = {e: 0 for e in self.ENGS}
        self.seen = {e: {} for e in self.ENGS}
        self.lastw = {}
        self.readers = {}
        self.dma_count = {}

    def _collect(self, eng, reads, writes):
        need = {}

        def add(tok, kind):
            if tok is None:
                return
            k, v = tok
            if k == eng and eng == 'pe':
                return
            if need.get(k, 0) < v:
                need[k] = v
        for b in reads:
            add(self.lastw.get(b), 'raw')
        for b in writes:
            add(self.lastw.get(b), 'waw')
            for k, v in self.readers.get(b, {}).items():
                add((k, v), 'war')
        waits = []
        for k, v in need.items():
            if self.seen[eng].get(k, 0) >= v:
                continue
            self.seen[eng][k] = v
            waits.append((k, v))
        return waits

    def _update(self, tok, reads, writes):
        k, v = tok
        for b in reads:
            r = self.readers.setdefault(b, {})
            if r.get(k, 0) < v:
                r[k] = v
        for b in writes:
            self.lastw[b] = tok
            self.readers[b] = {}

    def op(self, eng, fn, reads=(), writes=()):
        waits = self._collect(eng, reads, writes)
        self.count[eng] += 1
        tok = (eng, self.count[eng])
        self.streams[eng].append((waits, fn, (eng, 1)))
        self._update(tok, reads, writes)

    def dma(self, q, key, fn, reads=(), writes=()):
        waits = self._collect(q, reads, writes)
        dk = 'dma_' + key
        self.dma_count[dk] = self.dma_count.get(dk, 0) + 16
        tok = (dk, self.dma_count[dk])
        self.streams[q].append((waits, fn, (dk, 16)))
        self._update(tok, reads, writes)

    def alias(self, new_keys, old_prefixes):
        olds = [k for k in set(list(self.lastw.keys()) + list(self.readers.keys()))
                if any(k == p or k.startswith(p + ':') for p in old_prefixes)]
        for new in new_keys:
            r = self.readers.setdefault(new, {})
            for o in olds:
                w = self.lastw.get(o)
                if w is not None and r.get(w[0], 0) < w[1]:
                    r[w[0]] = w[1]
                for k, v in self.readers.get(o, {}).items():
                    if r.get(k, 0) < v:
                        r[k] = v

    def finish(self, eng, bufs):
        waits = self._collect(eng, bufs, ())
        self.streams[eng].append((waits, None, None))

    def emit(self, ctx):
        nc = self.nc
        keys = list(self.ENGS) + sorted(self.dma_count.keys())
        sems = {k: ctx.enter_context(nc.semaphore('s_' + k)) for k in keys}
        block = ctx.enter_context(nc.Block())
        streams = self.streams

        def run(engname, e):
            for waits, fn, inc in streams[engname]:
                for k, v in waits:
                    e.wait_ge(sems[k], v)
                if fn is not None:
                    fn(e).then_inc(sems[inc[0]], inc[1])

        @block.tensor
        def _(e):
            run('pe', e)

        @block.scalar
        def _(e):
            run('act', e)

        @block.vector
        def _(e):
            run('dve', e)

        @block.gpsimd
        def _(e):
            run('pool', e)

        @block.sync
        def _(e):
            run('sp', e)


def build_nc():
    nc = bass.Bass("TRN2", target_bir_lowering=False)

    def din(name, shape):
        return nc.dram_tensor(name, list(shape), F32, kind="ExternalInput").ap()

    xin = din("xin", [S, D])
    w_in = din("w_in", [D, 2048])
    w_out = din("w_out", [D, D])
    wgu = din("wgu", [NJ, 128, 8 * 256])
    w_fo = din("w_fo", [DFF, D])
    pool_w = din("pool_w", [4, 128, 128])
    g1t = din("g1t", [128, 8])
    g2 = din("g2", [1, D])
    gf = din("gf", [1, D])
    subg = din("subg", [1, 128])
    lamv = din("lamv", [1, 256])
    pscale = din("pscale", [128, 4])
    convw = din("convw", [128, NJ * 3])
    convb = din("convb", [128, NJ])
    nearb = din("nearb", [128, 4 * 128])
    diagb = din("diagb", [128, 4 * 128])
    maskd = din("maskd", [128, 128])
    b31 = din("b31", [1, 4])
    cflag = din("cflag", [128, 1])
    oflag = din("oflag", [128, 1])
    corr = din("corr", [128, 4 * 16])
    identd = din("identd", [128, 128])
    yout = nc.dram_tensor("yout", [2048, D], F32, kind="ExternalOutput").ap()

    base = (nc._sbuf_addr_for_side(None) + 63) // 64 * 64
    top = nc._sbuf_addr_for_side('right')
    cur = [base]
    names = [0]
    offs = {}

    def alloc(shape, dt, name=None):
        nbytes = int(np.prod(shape[1:])) * (4 if dt == F32 else 2)
        names[0] += 1
        offs[name] = cur[0]
        t = nc.alloc_sbuf_tensor_at((name or 't') + str(names[0]), list(shape), dt, offset=cur[0])
        cur[0] += (nbytes + 63) // 64 * 64
        assert cur[0] <= top, ("SBUF overflow", cur[0], top)
        return t

    P = Prog(nc)
    ev = [0]
    force_act = [0]

    def evac(out, in_, reads, writes, scale=None):
        ev[0] += 1
        use_act = (ev[0] % 2 == 0)
        if force_act[0] > 0:
            force_act[0] -= 1
            use_act = True
        if use_act:
            if scale is None:
                P.op('act', lambda e: e.activation(out=out, in_=in_, func=AF.Copy), reads, writes)
            else:
                P.op('act', lambda e: e.activation(out=out, in_=in_, func=AF.Copy, scale=scale), reads, writes)
        else:
            if scale is None:
                P.op('dve', lambda e: e.tensor_copy(out=out, in_=in_), reads, writes)
            else:
                P.op('dve', lambda e: e.tensor_scalar(out=out, in0=in_, scalar1=scale, scalar2=None, op0=ALU.mult), reads, writes)

    def mm(out, lhsT, rhs, start, stop, reads, writes, skip=False):
        if skip:
            P.op('pe', lambda e: e.matmul(out, lhsT=lhsT, rhs=rhs, start=start, stop=stop, skip_group_check=True), reads, writes)
        else:
            P.op('pe', lambda e: e.matmul(out, lhsT=lhsT, rhs=rhs, start=start, stop=stop), reads, writes)

    with contextlib.ExitStack() as ctx:
        ps = ctx.enter_context(nc.psum_tensor("ps", [128, 7, 512], F32))
        pst = ctx.enter_context(nc.psum_tensor("pst", [128, 1024], BF16))

        ident = alloc([128, 128], BF16, 'ident')
        identf = alloc([128, 128], F32, 'identf')
        subgbc = alloc([128, 128], F32, 'subgbc')
        lamt = alloc([128, 256], F32, 'lamt')
        lams = alloc([128, 8], F32, 'lams')
        psc = alloc([128, 4], F32, 'psc')
        cw = alloc([128, NJ * 3], F32, 'cw')
        cb = alloc([128, NJ], F32, 'cb')
        spec = alloc([128, 4, 4, 128], F32, 'spec')
        mskd = alloc([128, 128], F32, 'mskd')
        b31t = alloc([128, 4], F32, 'b31t')
        kbias = alloc([128, 8], F32, 'kbias')
        cfl = alloc([128, 1], F32, 'cfl')
        ofl = alloc([128, 1], F32, 'ofl')
        corrt = alloc([128, 4, 16], F32, 'corrt')
        epsn = alloc([128, 1], F32, 'epsn')
        epss = alloc([128, 1], F32, 'epss')
        stat = alloc([128, 16], F32, 'stat')
        junk = alloc([128, D], BF16, 'junk')
        poolwt = alloc([128, 4, 128], BF16, 'poolwt')
        hist = alloc([128, NJ, 2], F32, 'hist')
        yT = alloc([128, 8, 17 * 128], BF16, 'yT')

        def ld(q, key, out, in_, wr):
            P.dma(q, key, lambda e: e.dma_start(out=out, in_=in_), (), wr)

        ld('sp', 'c0', identf[:], identd, ['identf'])
        P.op('dve', lambda e: e.tensor_copy(out=ident[:], in_=identf[:]), ['identf'], ['ident'])
        P.op('dve', lambda e: e.memset(epsn[:], 1e-6), (), ['epsn'])
        P.op('dve', lambda e: e.memset(epss[:], 1e-5), (), ['epss'])
        P.op('dve', lambda e: e.memset(hist[:].rearrange("p j t -> p (j t)"), 0.0), (), ['hist'])
        P.op('dve', lambda e: e.memset(stat[:], 0.0), (), ['ss0', 'ss1', 'ss2', 'ss3', 'rstd', 'cfl8', 'fss', 'frs'])
        g1T = alloc([128, 8], F32, 'g1T')
        ld('sp', 'c1', g1T[:], g1t, ['g1T'])
        persist_end = cur[0]
        def setup_late():
            ld('sp', 'c3', subgbc[:], subg.broadcast_to([128, 128]), ['subgbc'])
            ld('sp', 'c4', lamt[:], lamv.broadcast_to([128, 256]), ['lamt'])
            ld('sp', 'c5', psc[:], pscale, ['psc'])
            ld('sp', 'c6', cw[:], convw, ['cw'])
            ld('sp', 'c7', cb[:], convb, ['cb'])
            ld('sp', 'c8', spec[:, 0, :, :].rearrange("p h q -> p (h q)"), nearb, ['spec0'])
            ld('sp', 'c9', spec[:, 1, :, :].rearrange("p h q -> p (h q)"), diagb, ['spec1'])
            ld('sp', 'c10', mskd[:], maskd, ['mskd'])
            ld('sp', 'c11', b31t[:], b31.broadcast_to([128, 4]), ['b31t'])
            ld('sp', 'c12', cfl[:], cflag, ['cfl'])
            ld('sp', 'c13', ofl[:], oflag, ['ofl'])
            ld('sp', 'c14', corrt[:].rearrange("p g t -> p (g t)"), corr, ['corrt'])
            ld('pool', 'c15', poolwt[:], pool_w.rearrange("g c d -> c g d"), ['poolwt'])

            P.op('dve', lambda e: e.tensor_scalar(out=subgbc[:], in0=subgbc[:], scalar1=1.0 - LAM_INIT, scalar2=None, op0=ALU.mult), ['subgbc'], ['subgbc'])
            P.op('dve', lambda e: e.tensor_tensor(out=lamt[:, 0:64], in0=lamt[:, 0:64], in1=lamt[:, 64:128], op=ALU.mult), ['lamt'], ['lamt'])
            P.op('dve', lambda e: e.tensor_tensor(out=lamt[:, 128:192], in0=lamt[:, 128:192], in1=lamt[:, 192:256], op=ALU.mult), ['lamt'], ['lamt'])
            P.op('dve', lambda e: e.reduce_sum(out=lams[:, 0:1], in_=lamt[:, 0:64], axis=mybir.AxisListType.X), ['lamt'], ['lams'])
            P.op('dve', lambda e: e.reduce_sum(out=lams[:, 1:2], in_=lamt[:, 128:192], axis=mybir.AxisListType.X), ['lamt'], ['lams'])
            P.op('act', lambda e: e.activation(out=lams[:, 2:4], in_=lams[:, 0:2], func=AF.Exp), ['lams'], ['lams'])
            P.op('dve', lambda e: e.tensor_tensor(out=lams[:, 4:5], in0=lams[:, 3:4], in1=lams[:, 2:3], op=ALU.subtract), ['lams'], ['lams'])
            P.op('dve', lambda e: e.tensor_scalar(out=lams[:, 4:5], in0=lams[:, 4:5], scalar1=-LAM_INIT, scalar2=None, op0=ALU.add), ['lams'], ['lams'])
            P.op('dve', lambda e: e.tensor_copy(out=kbias[:, 0:4], in_=b31t[:]), ['b31t'], ['kbias'])
            P.op('dve', lambda e: e.tensor_scalar(out=kbias[:, 4:8], in0=b31t[:], scalar1=cfl[:, 0:1], scalar2=None, op0=ALU.add), ['b31t', 'cfl', 'kbias'], ['kbias'])
            for h in range(NB):
                P.op('dve', lambda e, h=h: e.tensor_scalar(out=spec[:, 0, h, :], in0=spec[:, 0, h, :], scalar1=b31t[:, h:h + 1], scalar2=8.0, op0=ALU.subtract, op1=ALU.mult), ['spec0', 'b31t'], ['spec0'])
                P.op('dve', lambda e, h=h: e.tensor_scalar(out=spec[:, 1, h, :], in0=spec[:, 1, h, :], scalar1=b31t[:, h:h + 1], scalar2=None, op0=ALU.subtract), ['spec1', 'b31t'], ['spec1'])
                P.op('dve', lambda e, h=h: e.tensor_tensor(out=spec[:, 1, h, :], in0=spec[:, 1, h, :], in1=mskd[:], op=ALU.add), ['spec1', 'mskd'], ['spec1'])
                P.op('dve', lambda e, h=h: e.tensor_scalar(out=spec[:, 1, h, :], in0=spec[:, 1, h, :], scalar1=8.0, scalar2=None, op0=ALU.mult), ['spec1'], ['spec1'])
            P.op('dve', lambda e: e.tensor_scalar(out=stat[:, 8:9], in0=cfl[:, 0:1], scalar1=8.0, scalar2=None, op0=ALU.mult), ['cfl'], ['cfl8'])
            for ty in range(2):
                P.op('dve', lambda e, ty=ty: e.tensor_scalar(out=spec[:, 2 + ty, :, :].rearrange("p h q -> p (h q)"), in0=spec[:, ty, :, :].rearrange("p h q -> p (h q)"), scalar1=stat[:, 8:9], scalar2=None, op0=ALU.add), ['spec%d' % ty, 'cfl8'], ['spec%d' % (2 + ty)])


            return lams[:, 4:5]

        KT = alloc([128, NB, S], BF16, 'KT')
        V = alloc([128, 32, NB, 130], BF16, 'V')
        QT = alloc([128, NB, 17 * 128], BF16, 'QT')
        ab_end = cur[0]
        w_in_t = alloc([128, 8, 2048], BF16, 'w_in_t')
        xt = alloc([128, 2, D], F32, 'xt')
        hb = alloc([128, 2, D], BF16, 'hb')
        hTs = [alloc([128, 8, 512], BF16, 'hT%d' % i) for i in range(2)]
        zt = alloc([128, 4, 528], F32, 'zt')
        zs = alloc([128, 2, 528], F32, 'zs')
        pooled = alloc([128, 4, 512], BF16, 'pooled')
        print('phaseA end', cur[0], 'top', top)
        a_end = cur[0]

        def load_w_in(c4):
            ld('pool', 'win%d' % c4, w_in_t[:, :, c4 * 512:(c4 + 1) * 512],
               w_in.rearrange("(k p) n -> p k n", p=128)[:, :, c4 * 512:(c4 + 1) * 512], ['w_in:%d' % c4])
            for k in range(8):
                P.op('dve', lambda e, k=k: e.tensor_scalar(out=w_in_t[:, k, c4 * 512:(c4 + 1) * 512], in0=w_in_t[:, k, c4 * 512:(c4 + 1) * 512],
                                                          scalar1=g1T[:, k:k + 1], scalar2=None, op0=ALU.mult),
                     ['w_in:%d' % c4, 'g1T'], ['w_in:%d' % c4])

        load_w_in(2)
        load_w_in(3)
        P.op('pool', lambda e: e.memset(V[:].rearrange("p a b c -> p (a b c)"), 1.0), (), ['V'])
        P.op('pool', lambda e: e.memset(zt[:].rearrange("p a b -> p (a b)"), 0.0), (), ['zt:%d' % g for g in range(4)] + ['zth:%d' % g for g in range(4)])

        bank = [0]

        def nextbank(nbanks=7):
            b = bank[0] % nbanks
            bank[0] += 1
            return b

        def norm_rows(src, ns, gain_key, gain, out_bf, src_keys, out_keys, eps_t, dim):
            for s in range(ns):
                P.op('act', lambda e, s=s: e.activation(out=junk[:, 0:dim], in_=src(s), func=AF.Square, accum_out=stat[:, s:s + 1]),
                     src_keys(s), ['ss%d' % s, 'junk'])
            P.op('act', lambda e: e.activation(out=stat[:, 4:4 + ns], in_=stat[:, 0:ns], func=AF.Ln, scale=1.0 / dim, bias=eps_t[:]),
                 ['ss%d' % s for s in range(ns)] + ['epsn', 'epss'], ['rstd'])
            P.op('act', lambda e: e.activation(out=stat[:, 4:4 + ns], in_=stat[:, 4:4 + ns], func=AF.Exp, scale=-0.5), ['rstd'], ['rstd'])
            for s in range(ns):
                if gain is None:
                    P.op('dve', lambda e, s=s: e.tensor_scalar(out=out_bf(s), in0=src(s), scalar1=stat[:, 4 + s:5 + s], scalar2=None, op0=ALU.mult),
                         src_keys(s) + ['rstd'], out_keys(s))
                else:
                    P.op('dve', lambda e, s=s: e.scalar_tensor_tensor(out=out_bf(s), in0=src(s), scalar=stat[:, 4 + s:5 + s], in1=gain, op0=ALU.mult, op1=ALU.mult),
                         src_keys(s) + ['rstd', gain_key], out_keys(s))
            P.op('dve', lambda e: e.memset(stat[:, 0:4], 0.0), (), ['ss%d' % s for s in range(4)])

        def tr_round(src_bf, ns, dstT, col0, kp, src_keys, dst_key, gscale=None):
            if kp % 2 == 1:
                return
            q = kp // 2
            for kk in range(4):
                k = q * 4 + kk
                for s in range(ns):
                    P.op('pe', lambda e, k=k, kk=kk, s=s: e.transpose(pst[:, kk * 256 + s * 128: kk * 256 + (s + 1) * 128], src_bf(s)[:, k * 128:(k + 1) * 128], ident[:]),
                         src_keys(s) + ['ident'], ['pst'])
            if gscale is None:
                evac(dstT[:, 4 * q:4 * q + 4, col0:col0 + ns * 128],
                     pst[:].rearrange("p (a b) -> p a b", a=4)[:, :, 0:ns * 128], ['pst'], [dst_key])
            else:
                for kk in range(4):
                    k = q * 4 + kk
                    evac(dstT[:, k, col0:col0 + ns * 128], pst[:, kk * 256:kk * 256 + ns * 128], ['pst', 'g1T'], [dst_key],
                         scale=gscale[:, k:k + 1])

        def a_norm(t, half):
            r0 = t * 512 + half * 256
            P.dma('sp', 'xt', lambda e: e.dma_start(out=xt[:], in_=xin[r0:r0 + 256, :].rearrange("(s p) d -> p s d", p=128)),
                  (), ['xt:%d' % s for s in range(2)])
            norm_rows(lambda s: xt[:, s, :], 2, None, None, lambda s: hb[:, s, :],
                      lambda s: ['xt:%d' % s], lambda s: ['hb:%d' % s], epsn, D)

        def a_round(t, r):
            half, kp = r // 4, r % 4
            tr_round(lambda s: hb[:, s, :], 2, hTs[t % 2], half * 256, kp, lambda s: ['hb:%d' % s], 'hT%d' % (t % 2))

        def a_groups(t):
            own = t >= 4
            full = t >= 3
            hT = hTs[t % 2]
            hk = 'hT%d' % (t % 2)
            grps = []

            def chunk(m):
                b = nextbank()
                n0 = 0 if (own or m >= 8) else (256 if m < 4 else 384)
                for k in range(8):
                    mm(ps[:, b, n0:512], w_in_t[:, k, m * 128:(m + 1) * 128], hT[:, k, n0:512], k == 0, k == 7,
                       ['w_in:%d' % (m // 4), hk], ['ps%d' % b])
                if m < 4:
                    P.op('act', lambda e: e.activation(out=zt[:, m, 16 + n0:528], in_=ps[:, b, n0:512], func=AF.Copy), ['ps%d' % b], ['zt:%d' % m])
                elif m < 8:
                    h = m - 4
                    if own:
                        c0 = (16 + 4 * (t - 4) - 15) * 128
                        evac(QT[:, h, c0:c0 + 512], ps[:, b, :], ['ps%d' % b], ['QT'])
                    else:
                        evac(QT[:, h, 0:128], ps[:, b, 384:512], ['ps%d' % b], ['QT'])
                else:
                    h = m - 8
                    evac(KT[:, h, t * 512:(t + 1) * 512], ps[:, b, :], ['ps%d' % b], ['KT'])

            def vgrp(s):
                b = nextbank()
                for k in range(8):
                    mm(ps[:, b, :], hT[:, k, s * 128:(s + 1) * 128], w_in_t[:, k, 1536:2048], k == 0, k == 7,
                       ['w_in:3', hk], ['ps%d' % b])
                evac(V[:, t * 4 + s, :, 0:128], ps[:, b, :].rearrange("p (h e) -> p h e", h=NB), ['ps%d' % b], ['V'])

            if full:
                for m in range(0, 4):
                    grps.append(lambda m=m: chunk(m))
            for m in range(8, 12):
                grps.append(lambda m=m: chunk(m))
            for s in range(4):
                grps.append(lambda s=s: vgrp(s))
            if full:
                for m in range(4, 8):
                    grps.append(lambda m=m: chunk(m))
            return grps

        def a_pool(t):
            own = t >= 4
            if t == 4:
                P.op('dve', lambda e: e.tensor_scalar(out=zt[:, :, 0:16], in0=zt[:, :, 0:16], scalar1=ofl[:, 0:1], scalar2=None, op0=ALU.mult),
                     ['zth:%d' % g for g in range(4)] + ['ofl'], ['zth:%d' % g for g in range(4)])
            for g in range(4):
                zk = 'zt:%d' % g
                z = zt[:, g, :]
                P.op('dve', lambda e, z=z: e.tensor_tensor(out=zs[:, 0, 1:528], in0=z[:, 1:528], in1=z[:, 0:527], op=ALU.add), [zk, 'zth:%d' % g, 'zs0'], ['zs0'])
                curi = 0
                sh = 2
                for step in range(g):
                    nxt = 1 - curi
                    P.op('dve', lambda e, curi=curi, nxt=nxt, sh=sh: e.tensor_tensor(out=zs[:, nxt, 2 * sh - 1:528], in0=zs[:, curi, 2 * sh - 1:528], in1=zs[:, curi, sh - 1:528 - sh], op=ALU.add),
                         ['zs%d' % curi, 'zs%d' % nxt], ['zs%d' % nxt])
                    curi = nxt
                    sh *= 2
                w = 2 ** (g + 1)
                if t == 4:
                    P.op('dve', lambda e, curi=curi, g=g: e.tensor_tensor(out=zs[:, curi, 16:32], in0=zs[:, curi, 16:32], in1=corrt[:, g, :], op=ALU.mult),
                         ['zs%d' % curi, 'corrt'], ['zs%d' % curi])
                pg_ = g
                pk_ = 'pooled:%d' % pg_
                P.op('dve', lambda e, curi=curi, z=z, w=w, pg_=pg_: e.scalar_tensor_tensor(out=pooled[:, pg_, :], in0=zs[:, curi, 16:528], scalar=1.0 / w, in1=z[:, 16:528], op0=ALU.mult, op1=ALU.subtract),
                     ['zs%d' % curi, zk, pk_], [pk_])
                P.op('dve', lambda e, z=z: e.tensor_copy(out=z[:, 0:16], in_=z[:, 512:528]), [zk], ['zth:%d' % g])
            force_act[0] = 6

        def a_pool_mm(t):
            own = t >= 4
            for g in range(4):
                pk_ = 'pooled:%d' % g
                b = nextbank()
                if own:
                    c0 = (16 + 4 * (t - 4) - 15) * 128
                    mm(ps[:, b, :], poolwt[:, g, :], pooled[:, g, :], True, True, ['poolwt', pk_], ['ps%d' % b])
                    evac(yT[:, g, c0:c0 + 512], ps[:, b, :], ['ps%d' % b, 'psc'], ['yT:%d' % g], scale=psc[:, g:g + 1])
                else:
                    mm(ps[:, b, 0:128], poolwt[:, g, :], pooled[:, g, 384:512], True, True, ['poolwt', pk_], ['ps%d' % b])
                    evac(yT[:, g, 0:128], ps[:, b, 0:128], ['ps%d' % b, 'psc'], ['yT:%d' % g], scale=psc[:, g:g + 1])

        a_norm(0, 0)
        for r in range(4):
            a_round(0, r)
        a_norm(0, 1)
        for r in range(4, 8):
            a_round(0, r)
        neglam = setup_late()
        for t in range(8):
            grps = a_groups(t)
            nxt = t + 1 < 8
            if nxt:
                a_norm(t + 1, 0)
            for i, gfn in enumerate(grps):
                gfn()
                if nxt:
                    if i == 0:
                        a_round(t + 1, 0)
                        a_round(t + 1, 1)
                    elif i == 1:
                        a_round(t + 1, 2)
                        a_round(t + 1, 3)
                        a_norm(t + 1, 1)
                    elif 4 <= i < 8:
                        a_round(t + 1, i)
                if t >= 3 and i == 7:
                    a_pool(t)
            if t == 0:
                load_w_in(0)
                load_w_in(1)
            if t >= 3:
                a_pool_mm(t)

        cur[0] = ab_end
        w_out_t = alloc([128, 8, D], BF16, 'w_out_t')
        ring = [alloc([128, 8, 256], BF16, 'ring%d' % i) for i in range(4)]
        bc_start = cur[0]
        Pt = [alloc([128, 2, 512], BF16, 'Pt%d' % i) for i in range(2)]
        ya = alloc([128, 4, 512], BF16, 'ya')
        accs = alloc([128, 3, 512], F32, 'accs')
        oall = alloc([128, 4, 128], F32, 'oall')
        sqall = alloc([128, 4, 128], F32, 'sqall')
        stb = alloc([128, 16], F32, 'stb')
        phaseA_bufs = ['w_in', 'xt', 'hb', 'hT0', 'hT1', 'zt', 'zth', 'zs0', 'zs1', 'pooled']
        phaseB_bufs = ['Pt0', 'Pt1', 'ya', 'accs', 'oall:0', 'oall:1', 'oall:2', 'oall:3', 'sqall:0', 'sqall:1', 'sqall:2', 'sqall:3', 'rs', 'sss', 'srs']
        P.alias(['w_out', 'Pt0', 'Pt1', 'ya:0', 'ya:1', 'ya:2', 'ya:3', 'accs', 'oall:0', 'oall:1', 'oall:2', 'oall:3', 'sqall:0', 'sqall:1', 'sqall:2', 'sqall:3', 'rs', 'sss', 'srs'] + ['ring%d' % i for i in range(4)], phaseA_bufs)
        ld('pool', 'wout', w_out_t[:], w_out.rearrange("(k p) n -> p k n", p=128), ['w_out'])

        tiles = [(15, 1)] + [(16 + 4 * i, 4) for i in range(4)]
        ring_seq = [(ti, j) for ti in range(1, len(tiles)) for j in range(NJ)]

        def issue_ring(n):
            if n >= len(ring_seq):
                return
            ti, j = ring_seq[n]
            ri = n % 4
            rk = 'ring%d' % ri
            P.dma('pool', rk, lambda e: e.dma_start(out=ring[ri][:].rearrange("p k n -> p (k n)"), in_=wgu[j]), (), [rk])

        for n in range(4):
            issue_ring(n)

        groups = [(15, 1)] + [(16 + 4 * i, 4) for i in range(4)]
        ACC_BANK0 = 4
        HQ = 32
        for i_ in range(2):
            P.op('dve', lambda e, i_=i_: e.memset(Pt[i_][:].rearrange("p a b -> p (a b)"), 0.0), (), ['Pt%d' % i_])
        iters = []
        for (g0, nq) in groups:
            for h in range(NB):
                for kb in range(g0 + nq):
                    iters.append((g0, nq, h, kb))

        def emit_qk(idx):
            g0, nq, h, kb = iters[idx]
            qc0 = (g0 - 15) * 128
            a = max(0, kb - g0)
            n0, n1 = a * 128, nq * 128
            if nq == 1:
                n0 = 128 - HQ
            sb = 2 * (idx % 2)
            skeys = ['ps%d' % sb, 'ps%d' % (sb + 1)]
            for c in range(2):
                mm(ps[:, sb + c, n0:n1], KT[c * 64:(c + 1) * 64, h, kb * 128:(kb + 1) * 128],
                   QT[c * 64:(c + 1) * 64, h, qc0 + n0:qc0 + n1], True, True, ['KT', 'QT'], [skeys[c]])
            ctxk = kb < 16
            for qi in range(a, nq):
                d = g0 + qi - kb
                if d > 1:
                    continue
                ty = (0 if d == 1 else 1) + (2 if ctxk else 0)
                q0 = n0 if nq == 1 else qi * 128
                for c in range(2):
                    P.op('dve', lambda e, c=c, qi=qi, ty=ty, q0=q0: e.tensor_tensor(
                        out=ps[:, sb + c, q0:(qi + 1) * 128], in0=ps[:, sb + c, q0:(qi + 1) * 128],
                        in1=spec[:, ty, h, q0 - qi * 128:128], op=ALU.add), [skeys[c], 'spec%d' % ty], [skeys[c]])

        def emit_exp(idx):
            g0, nq, h, kb = iters[idx]
            a = max(0, kb - g0)
            n0, n1 = a * 128, nq * 128
            if nq == 1:
                n0 = 128 - HQ
            si = idx % 2
            sb = 2 * si
            skeys = ['ps%d' % sb, 'ps%d' % (sb + 1)]
            ctxk = kb < 16
            bc = 4 + h if ctxk else h
            bcol = kbias[:, bc:bc + 1]
            pk = 'Pt%d' % si
            P.op('act', lambda e: e.activation(
                out=Pt[si][:, :, n0:n1], in_=ps[:, sb:sb + 2, n0:n1], func=AF.Exp, bias=bcol, scale=0.125),
                skeys + ['kbias'], [pk])

        def emit_av(idx):
            g0, nq, h, kb = iters[idx]
            a = max(0, kb - g0)
            si = idx % 2
            pk = 'Pt%d' % si
            for qi in range(a, nq):
                for c in range(2):
                    ai = qi * 2 + c
                    ab = ACC_BANK0 + ai // 3
                    ac = (ai % 3) * 160
                    first_in_bank = (ai % 3 == 0)
                    mm(ps[:, ab, ac:ac + 129], Pt[si][:, c, qi * 128:(qi + 1) * 128], V[:, kb, h, 0:129],
                       (kb == 0 and first_in_bank), kb == g0 + qi, [pk, 'V'], ['ps%d' % ab], skip=True)

        def emit_head_end(g0, nq, h):
            nbk = (2 * nq + 2) // 3
            for bk in range(nbk):
                P.op('dve', lambda e, bk=bk: e.tensor_copy(out=accs[:, bk, 0:480], in_=ps[:, ACC_BANK0 + bk, 0:480]),
                     ['ps%d' % (ACC_BANK0 + bk), 'accs'], ['accs'])

            def acc(ai):
                return accs[:, ai // 3, (ai % 3) * 160:(ai % 3) * 160 + 129]
            for ai in range(2 * nq):
                P.op('dve', lambda e, ai=ai: e.tensor_scalar(out=stb[:, ai:ai + 1], in0=acc(ai)[:, 128:129], scalar1=1e-30, scalar2=None, op0=ALU.max),
                     ['accs', 'rs'], ['rs'])
            P.op('dve', lambda e: e.reciprocal(out=stb[:, 0:2 * nq], in_=stb[:, 0:2 * nq]), ['rs'], ['rs'])
            for qi in range(nq):
                P.op('dve', lambda e, qi=qi: e.tensor_tensor(out=stb[:, 2 * qi + 1:2 * qi + 2], in0=stb[:, 2 * qi + 1:2 * qi + 2], in1=neglam, op=ALU.mult), ['rs', 'lams'], ['rs'])
            for qi in range(nq):
                P.op('dve', lambda e, qi=qi: e.tensor_scalar(out=sqall[:, qi, :], in0=acc(2 * qi + 1)[:, 0:128], scalar1=stb[:, 2 * qi + 1:2 * qi + 2], scalar2=None, op0=ALU.mult),
                     ['accs', 'rs', 'sqall:%d' % qi], ['sqall:%d' % qi])
                P.op('dve', lambda e, qi=qi: e.scalar_tensor_tensor(out=oall[:, qi, :], in0=acc(2 * qi)[:, 0:128], scalar=stb[:, 2 * qi:2 * qi + 1], in1=sqall[:, qi, :], op0=ALU.mult, op1=ALU.add),
                     ['accs', 'rs', 'sqall:%d' % qi, 'oall:%d' % qi], ['oall:%d' % qi])
            sqk = ['sqall:%d' % qi for qi in range(nq)]
            oak = ['oall:%d' % qi for qi in range(nq)]
            P.op('dve', lambda e: e.tensor_tensor(out=sqall[:, 0:nq, :], in0=oall[:, 0:nq, :], in1=oall[:, 0:nq, :], op=ALU.mult), oak + sqk, sqk)
            P.op('dve', lambda e: e.reduce_sum(out=stb[:, 8:8 + nq], in_=sqall[:, 0:nq, :], axis=mybir.AxisListType.X), sqk, ['sss'])

        def emit_head_end2(g0, nq, h):
            P.op('act', lambda e: e.activation(out=stb[:, 12:12 + nq], in_=stb[:, 8:8 + nq], func=AF.Ln, scale=1.0 / 128, bias=epss[:]), ['sss', 'epss'], ['srs'])
            P.op('act', lambda e: e.activation(out=stb[:, 12:12 + nq], in_=stb[:, 12:12 + nq], func=AF.Exp, scale=-0.5), ['srs'], ['srs'])
            for qi in range(nq):
                P.op('dve', lambda e, qi=qi: e.scalar_tensor_tensor(out=ya[:, qi, h * 128:(h + 1) * 128], in0=oall[:, qi, :], scalar=stb[:, 12 + qi:13 + qi], in1=subgbc[:], op0=ALU.mult, op1=ALU.mult),
                     ['oall:%d' % qi, 'srs', 'subgbc'], ['ya:%d' % qi])

        def emit_group_end(g0, nq):
            qc0 = (g0 - 15) * 128
            for qi in range(nq):
                for h in range(NB):
                    P.op('pe', lambda e, qi=qi, h=h: e.transpose(pst[:, h * 128:(h + 1) * 128], ya[:, qi, h * 128:(h + 1) * 128], ident[:]),
                         ['ya:%d' % qi, 'ident'], ['pst'])
                cc = qc0 + qi * 128
                evac(yT[:, 4:8, cc:cc + 128], pst[:, 0:512].rearrange("p (h q) -> p h q", h=NB), ['pst'], ['yT:att'])

        emit_qk(0)
        emit_qk(1)
        pending = []
        for idx in range(len(iters)):
            emit_exp(idx)
            if idx + 2 < len(iters):
                emit_qk(idx + 2)
            emit_av(idx)
            while pending and pending[0][0] <= idx:
                pending.pop(0)[1]()
            g0, nq, h, kb = iters[idx]
            if kb == g0 + nq - 1:
                emit_head_end(g0, nq, h)

                def part2(g0=g0, nq=nq, h=h):
                    emit_head_end2(g0, nq, h)
                    if h == NB - 1:
                        emit_group_end(g0, nq)
                pending.append((idx + 10, part2))
        while pending:
            pending.pop(0)[1]()

        cur[0] = persist_end
        w_fo_t = alloc([128, NJ, D], BF16, 'w_fo_t')
        actT = alloc([128, NJ, 512], BF16, 'actT')
        h2T = alloc([128, 8, 512], BF16, 'h2T')
        h2Th = alloc([128, 8, 128], BF16, 'h2Th')
        cbuf = [alloc([128, 512], F32, 'cbuf%d' % i) for i in range(2)]
        assert cur[0] <= ab_end, (cur[0], ab_end)
        cur[0] = bc_start
        xt2s = [alloc([128, 4, D], F32, 'xt2_%d' % i) for i in range(2)]
        h2 = alloc([128, 2, D], BF16, 'h2')
        sgb = [alloc([128, 512], F32, 'sgb%d' % i) for i in range(2)]
        save2 = cur[0]
        cur[0] = offs['spec']
        g2bc = alloc([128, D], F32, 'g2bc')
        gfbc = alloc([128, D], F32, 'gfbc')
        assert cur[0] <= offs['mskd'], (cur[0], offs['mskd'])
        cur[0] = save2
        newC = (['xt2_%d:%d' % (b, i) for b in range(2) for i in range(4)] + ['h2:0', 'h2:1', 'g2bc', 'gfbc', 'w_fo:0', 'w_fo:1', 'w_fo:2', 'w_fo:3', 'actT', 'h2T', 'h2Th',
                'cbuf0', 'cbuf1', 'sgb0', 'sgb1'])
        P.alias(newC, phaseA_bufs + phaseB_bufs + ['KT', 'V', 'QT', 'spec0', 'spec1', 'spec2', 'spec3'])
        ld('sp', 'g2', g2bc[:], g2.broadcast_to([128, D]), ['g2bc'])
        ld('sp', 'c2', gfbc[:], gf.broadcast_to([128, D]), ['gfbc'])

        def c_load(ti):
            b0, nb = tiles[ti]
            xt2 = xt2s[ti % 2]
            xk = 'xt2_%d' % (ti % 2)
            P.dma('sp', xk, lambda e: e.dma_start(out=xt2[:, 0:nb, :], in_=xin[b0 * 128:(b0 + nb) * 128, :].rearrange("(s p) d -> p s d", p=128)),
                  (), [xk + ':%d' % s for s in range(nb)])

        def c_outproj(ti):
            b0, nb = tiles[ti]
            xt2 = xt2s[ti % 2]
            xk = 'xt2_%d' % (ti % 2)
            yc0 = (b0 - 15) * 128
            for s in range(nb):
                for n2 in range(2):
                    b = nextbank()
                    for k in range(8):
                        yk = 'yT:%d' % k if k < 4 else 'yT:att'
                        mm(ps[:, b, :], yT[:, k, yc0 + s * 128: yc0 + (s + 1) * 128], w_out_t[:, k, n2 * 512:(n2 + 1) * 512], k == 0, k == 7,
                           [yk, 'w_out'], ['ps%d' % b])
                    P.op('dve', lambda e, s=s, n2=n2, b=b: e.tensor_tensor(out=xt2[:, s, n2 * 512:(n2 + 1) * 512], in0=ps[:, b, :], in1=xt2[:, s, n2 * 512:(n2 + 1) * 512], op=ALU.add),
                         ['ps%d' % b, xk + ':%d' % s], [xk + ':%d' % s])

        def c_norm(ti, half):
            b0, nb = tiles[ti]
            xt2 = xt2s[ti % 2]
            xk = 'xt2_%d' % (ti % 2)
            ns = min(2, nb - 2 * half)
            if ns <= 0:
                return
            norm_rows(lambda s: xt2[:, 2 * half + s, :], ns, 'g2bc', g2bc[:], lambda s: h2[:, s, :],
                      lambda s: [xk + ':%d' % (2 * half + s)], lambda s: ['h2:%d' % s], epsn, D)

        def c_round(ti, r):
            b0, nb = tiles[ti]
            half, kp = r // 4, r % 4
            ns = min(2, nb - 2 * half)
            if ns <= 0:
                return
            if ti == 0:
                tr_round(lambda s: h2[:, s, :], ns, h2Th, 0, kp, lambda s: ['h2:%d' % s], 'h2Th')
            else:
                tr_round(lambda s: h2[:, s, :], ns, h2T, half * 256, kp, lambda s: ['h2:%d' % s], 'h2T')

        ring_n = [0]

        WFO_PIECES = [(0, 6), (6, 12), (12, 17), (17, 22)]

        def wfo_piece(j):
            for i, (a_, b_) in enumerate(WFO_PIECES):
                if a_ <= j < b_:
                    return i

        def c_ffn_in(ti):
            b0, nb = tiles[ti]
            N = nb * 128
            for j in range(NJ):
                if ti == 1 and j in (1, 6, 11, 16):
                    pc = (1, 6, 11, 16).index(j)
                    a_, b_ = WFO_PIECES[pc]
                    ld('pool', 'wfo%d' % pc, w_fo_t[:, a_:b_, :],
                       w_fo.rearrange("(j p) n -> p j n", p=128)[:, a_:b_, :], ['w_fo:%d' % pc])
                n = ring_n[0]
                ring_n[0] += 1
                ri = n % 4
                rk = 'ring%d' % ri
                if ti == 1:
                    bh = nextbank()
                    for k in range(8):
                        mm(ps[:, bh, 0:2], ring[ri][:, k, 0:128], h2Th[:, k, 126:128], k == 0, k == 7, [rk, 'h2Th'], ['ps%d' % bh])
                    P.op('act', lambda e, bh=bh, j=j: e.activation(out=hist[:, j, :], in_=ps[:, bh, 0:2], func=AF.Copy, scale=ofl[:, 0:1]),
                         ['ps%d' % bh, 'hist', 'ofl'], ['hist'])
                bg = nextbank()
                for k in range(8):
                    mm(ps[:, bg, 0:N], ring[ri][:, k, 0:128], h2T[:, k, 0:N], k == 0, k == 7, [rk, 'h2T'], ['ps%d' % bg])
                bu = nextbank()
                for k in range(8):
                    mm(ps[:, bu, 0:N], ring[ri][:, k, 128:256], h2T[:, k, 0:N], k == 0, k == 7, [rk, 'h2T'], ['ps%d' % bu])
                issue_ring(n + 4)
                ci = j % 2
                ck = 'cbuf%d' % ci
                cbt = cbuf[ci]
                pg = ps[:, bg, :]
                P.op('act', lambda e, cbt=cbt, pg=pg, j=j: e.activation(out=cbt[:, 0:N], in_=pg[:, 0:N], func=AF.Identity, scale=cw[:, 3 * j + 2:3 * j + 3], bias=cb[:, j:j + 1]),
                     ['ps%d' % bg, 'cw', 'cb', ck], [ck])
                P.op('dve', lambda e, cbt=cbt, pg=pg, j=j: e.scalar_tensor_tensor(out=cbt[:, 1:N], in0=pg[:, 0:N - 1], scalar=cw[:, 3 * j + 1:3 * j + 2], in1=cbt[:, 1:N], op0=ALU.mult, op1=ALU.add),
                     ['ps%d' % bg, 'cw', ck], [ck])
                P.op('dve', lambda e, cbt=cbt, pg=pg, j=j: e.scalar_tensor_tensor(out=cbt[:, 2:N], in0=pg[:, 0:N - 2], scalar=cw[:, 3 * j:3 * j + 1], in1=cbt[:, 2:N], op0=ALU.mult, op1=ALU.add),
                     ['ps%d' % bg, 'cw', ck], [ck])
                P.op('dve', lambda e, cbt=cbt, j=j: e.scalar_tensor_tensor(out=cbt[:, 0:1], in0=hist[:, j, 1:2], scalar=cw[:, 3 * j + 1:3 * j + 2], in1=cbt[:, 0:1], op0=ALU.mult, op1=ALU.add),
                     ['hist', 'cw', ck], [ck])
                P.op('dve', lambda e, cbt=cbt, j=j: e.scalar_tensor_tensor(out=cbt[:, 0:2], in0=hist[:, j, 0:2], scalar=cw[:, 3 * j:3 * j + 1], in1=cbt[:, 0:2], op0=ALU.mult, op1=ALU.add),
                     ['hist', 'cw', ck], [ck])
                P.op('act', lambda e, pg=pg, j=j: e.activation(out=hist[:, j, :], in_=pg[:, N - 2:N], func=AF.Copy), ['ps%d' % bg, 'hist', ck], ['hist'])
                sk = 'sgb%d' % ci
                sgt = sgb[ci]
                P.op('act', lambda e, sgt=sgt, cbt=cbt: e.activation(out=sgt[:, 0:N], in_=cbt[:, 0:N], func=AF.Silu), [ck, sk], [sk])
                P.op('dve', lambda e, sgt=sgt, bu=bu, j=j: e.tensor_tensor(out=actT[:, j, 0:N], in0=ps[:, bu, 0:N], in1=sgt[:, 0:N], op=ALU.mult),
                     ['ps%d' % bu, sk, 'actT'], ['actT'])

        def c_ffn_out_group(ti, s, n2):
            xt2 = xt2s[ti % 2]
            xk = 'xt2_%d' % (ti % 2)
            b = nextbank()
            for j in range(NJ):
                mm(ps[:, b, :], actT[:, j, s * 128:(s + 1) * 128], w_fo_t[:, j, n2 * 512:(n2 + 1) * 512], j == 0, j == NJ - 1,
                   ['actT', 'w_fo:%d' % wfo_piece(j)], ['ps%d' % b])
            P.op('dve', lambda e: e.tensor_tensor(out=xt2[:, s, n2 * 512:(n2 + 1) * 512], in0=ps[:, b, :], in1=xt2[:, s, n2 * 512:(n2 + 1) * 512], op=ALU.add),
                 ['ps%d' % b, xk + ':%d' % s], [xk + ':%d' % s])

        def c_final_s(ti, s):
            b0, nb = tiles[ti]
            xt2 = xt2s[ti % 2]
            xk = 'xt2_%d' % (ti % 2)
            P.op('act', lambda e: e.activation(out=junk[:], in_=xt2[:, s, :], func=AF.Square, accum_out=stat[:, 14:15]),
                 [xk + ':%d' % s], ['fss', 'junk'])
            P.op('act', lambda e: e.activation(out=stat[:, 15:16], in_=stat[:, 14:15], func=AF.Ln, scale=1.0 / D, bias=epsn[:]),
                 ['fss', 'epsn'], ['frs'])
            P.op('act', lambda e: e.activation(out=stat[:, 15:16], in_=stat[:, 15:16], func=AF.Exp, scale=-0.5), ['frs'], ['frs'])
            P.op('dve', lambda e: e.scalar_tensor_tensor(out=xt2[:, s, :], in0=xt2[:, s, :], scalar=stat[:, 15:16], in1=gfbc[:], op0=ALU.mult, op1=ALU.mult),
                 [xk + ':%d' % s, 'frs', 'gfbc'], [xk + ':%d' % s])
            P.op('dve', lambda e: e.memset(stat[:, 14:15], 0.0), (), ['fss'])
            o0 = (b0 - 16) * 128 + s * 128
            P.dma('sp', 'out%d' % (ti % 2), lambda e: e.dma_start(out=yout[o0:o0 + 128, :], in_=xt2[:, s, :]),
                  [xk + ':%d' % s], ['yout%d' % (ti % 2)])

        nt = len(tiles)
        c_load(0)
        c_outproj(0)
        c_norm(0, 0)
        for r in range(4):
            c_round(0, r)
        c_load(1)
        c_outproj(1)
        c_norm(1, 0)
        for r in range(4):
            c_round(1, r)
        c_norm(1, 1)
        for r in range(4, 8):
            c_round(1, r)
        for ti in range(1, nt):
            nxt = ti + 1 < nt
            if nxt:
                c_load(ti + 1)
            c_ffn_in(ti)
            if nxt:
                c_outproj(ti + 1)
                c_norm(ti + 1, 0)
            for g in range(8):
                c_ffn_out_group(ti, g // 2, g % 2)
                if nxt:
                    if g == 4:
                        c_norm(ti + 1, 1)
                    c_round(ti + 1, g)
                if g % 2 == 1:
                    c_final_s(ti, g // 2)
        P.finish('sp', ['yout0', 'yout1'])
        P.emit(ctx)
    return nc


def _bucket(n):
    n = np.maximum(n, 0)
    nf = np.maximum(n, 1).astype(np.float32)
    large = 16 + (np.log(nf / np.float32(16)) / np.float32(math.log(128 / 16)) * np.float32(16)).astype(np.int32)
    large = np.minimum(large, 31)
    return np.where(n < 16, n, large)


_NC_CACHE = {}


def kernel(x, norm_mix_g, w_in, pool_w, pool_scale, lambda_q1, lambda_k1, lambda_q2, lambda_k2,
           subln_g, rel_bias, w_out, norm_ffn_g, ffn_w_in, ffn_conv_w, ffn_conv_b, ffn_w_out, norm_final_g):
    f = lambda a: np.ascontiguousarray(np.asarray(a, dtype=np.float32))
    x = f(x)
    rel_bias = f(rel_bias)
    kk = np.arange(128)[:, None]
    qq = np.arange(128)[None, :]
    near_idx = _bucket(128 + qq - kk)
    diag_idx = _bucket(qq - kk)
    nearb = np.ascontiguousarray(np.transpose(rel_bias[near_idx], (0, 2, 1)).reshape(128, 512))
    diagb = np.ascontiguousarray(np.transpose(rel_bias[diag_idx], (0, 2, 1)).reshape(128, 512))
    maskd = np.where(kk <= qq, 0.0, NEGM).astype(np.float32)
    fw = f(ffn_w_in)[0]
    wg = fw[:, :DFF].reshape(8, 128, NJ, 128)
    wu = fw[:, DFF:].reshape(8, 128, NJ, 128)
    wgu = np.ascontiguousarray(np.concatenate([wg, wu], axis=3).transpose(2, 1, 0, 3).reshape(NJ, 128, 8 * 256))
    convw = np.ascontiguousarray(f(ffn_conv_w)[0].reshape(3, NJ, 128).transpose(2, 1, 0).reshape(128, NJ * 3))
    convb = np.ascontiguousarray(f(ffn_conv_b)[0].reshape(NJ, 128).T)
    shared = {
        "w_in": f(w_in)[0], "w_out": f(w_out)[0], "wgu": wgu, "w_fo": f(ffn_w_out)[0],
        "pool_w": f(pool_w)[0], "g1t": np.ascontiguousarray(f(norm_mix_g).reshape(8, 128).T), "g2": f(norm_ffn_g), "gf": f(norm_final_g).reshape(1, D),
        "subg": f(subln_g), "lamv": np.concatenate([f(lambda_q1), f(lambda_k1), f(lambda_q2), f(lambda_k2)], axis=1),
        "pscale": np.ascontiguousarray(f(pool_scale)[0].reshape(4, 128).T), "convw": convw, "convb": convb,
        "nearb": nearb, "diagb": diagb, "maskd": maskd, "b31": np.ascontiguousarray(rel_bias[31:32, :]),
        "identd": np.eye(128, dtype=np.float32),
    }
    tpos = np.arange(16)
    corrA = np.stack([w / np.minimum(tpos + 1, w) for w in (2, 4, 8, 16)], 0).astype(np.float32)
    in_maps = []
    for c in range(8):
        b, role = c // 2, c % 2
        m = dict(shared)
        if role == 0:
            m["xin"] = np.ascontiguousarray(np.concatenate([x[b, 2048:], x[b, :2048]], axis=0))
            m["cflag"] = np.full((128, 1), NEGM, np.float32)
            m["oflag"] = np.zeros((128, 1), np.float32)
            m["corr"] = np.ascontiguousarray(np.broadcast_to(corrA.reshape(1, 64), (128, 64)))
        else:
            m["xin"] = x[b]
            m["cflag"] = np.zeros((128, 1), np.float32)
            m["oflag"] = np.ones((128, 1), np.float32)
            m["corr"] = np.ones((128, 64), np.float32)
        in_maps.append(m)
    if "nc" not in _NC_CACHE:
        _NC_CACHE["nc"] = build_nc()
    res = run_bass_kernel_spmd(_NC_CACHE["nc"], in_maps, core_ids=list(range(8)))
    out = np.empty((4, S, D), np.float32)
    for c in range(8):
        b, role = c // 2, c % 2
        out[b, role * 2048:(role + 1) * 2048] = res.results[c]["yout"]
    return out
```

```python
import contextlib
import math
import numpy as np
import concourse.bass as bass
import concourse.mybir as mybir
from concourse.bass_utils import run_bass_kernel_spmd

F32 = mybir.dt.float32
BF16 = mybir.dt.bfloat16
ALU = mybir.AluOpType
AF = mybir.ActivationFunctionType

D = 1024
S = 4096
NB = 4
DFF = 2816
NJ = DFF // 128
NEGM = -30000.0
LAM_INIT = 0.8 - 0.6 * math.exp(0.0)


class Prog:
    ENGS = ['pe', 'act', 'dve', 'pool', 'sp']

    def __init__(self, nc):
        self.nc = nc
        self.streams = {e: [] for e in self.ENGS}
        self.count = {e: 0 for e in self.ENGS}
        self.seen = {e: {} for e in self.ENGS}
        self.lastw = {}
        self.readers = {}
        self.dma_count = {}

    def _collect(self, eng, reads, writes):
        need = {}

        def add(tok, kind):
            if tok is None:
                return
            k, v = tok
            if k == eng and eng == 'pe':
                return
            if need.get(k, 0) < v:
                need[k] = v
        for b in reads:
            add(self.lastw.get(b), 'raw')
        for b in writes:
            add(self.lastw.get(b), 'waw')
            for k, v in self.readers.get(b, {}).items():
                add((k, v), 'war')
        waits = []
        for k, v in need.items():
            if self.seen[eng].get(k, 0) >= v:
                continue
            self.seen[eng][k] = v
            waits.append((k, v))
        return waits

    def _update(self, tok, reads, writes):
        k, v = tok
        for b in reads:
            r = self.readers.setdefault(b, {})
            if r.get(k, 0) < v:
                r[k] = v
        for b in writes:
            self.lastw[b] = tok
            self.readers[b] = {}

    def op(self, eng, fn, reads=(), writes=()):
        waits = self._collect(eng, reads, writes)
        self.count[eng] += 1
        tok = (eng, self.count[eng])
        self.streams[eng].append((waits, fn, (eng, 1)))
        self._update(tok, reads, writes)

    def dma(self, q, key, fn, reads=(), writes=()):
        waits = self._collect(q, reads, writes)
        dk = 'dma_' + key
        self.dma_count[dk] = self.dma_count.get(dk, 0) + 16
        tok = (dk, self.dma_count[dk])
        self.streams[q].append((waits, fn, (dk, 16)))
        self._update(tok, reads, writes)

    def alias(self, new_keys, old_prefixes):
        olds = [k for k in set(list(self.lastw.keys()) + list(self.readers.keys()))
                if any(k == p or k.startswith(p + ':') for p in old_prefixes)]
        for new in new_keys:
            r = self.readers.setdefault(new, {})
            for o in olds:
                w = self.lastw.get(o)
                if w is not None and r.get(w[0], 0) < w[1]:
                    r[w[0]] = w[1]
                for k, v in self.readers.get(o, {}).items():
                    if r.get(k, 0) < v:
                        r[k] = v

    def finish(self, eng, bufs):
        waits = self._collect(eng, bufs, ())
        self.streams[eng].append((waits, None, None))

    def emit(self, ctx):
        nc = self.nc
        keys = list(self.ENGS) + sorted(self.dma_count.keys())
        sems = {k: ctx.enter_context(nc.semaphore('s_' + k)) for k in keys}
        block = ctx.enter_context(nc.Block())
        streams = self.streams

        def run(engname, e):
            for waits, fn, inc in streams[engname]:
                for k, v in waits:
                    e.wait_ge(sems[k], v)
                if fn is not None:
                    fn(e).then_inc(sems[inc[0]], inc[1])

        @block.tensor
        def _(e):
            run('pe', e)

        @block.scalar
        def _(e):
            run('act', e)

        @block.vector
        def _(e):
            run('dve', e)

        @block.gpsimd
        def _(e):
            run('pool', e)

        @block.sync
        def _(e):
            run('sp', e)


def build_nc():
    nc = bass.Bass("TRN2", target_bir_lowering=False)

    def din(name, shape):
        return nc.dram_tensor(name, list(shape), F32, kind="ExternalInput").ap()

    xin = din("xin", [S, D])
    w_in = din("w_in", [D, 2048])
    w_out = din("w_out", [D, D])
    wgu = din("wgu", [NJ, 128, 8 * 256])
    w_fo = din("w_fo", [DFF, D])
    pool_w = din("pool_w", [4, 128, 128])
    g1t = din("g1t", [128, 8])
    g2 = din("g2", [1, D])
    gf = din("gf", [1, D])
    subg = din("subg", [1, 128])
    lamv = din("lamv", [1, 256])
    pscale = din("pscale", [128, 4])
    convw = din("convw", [128, NJ * 3])
    convb = din("convb", [128, NJ])
    nearb = din("nearb", [128, 4 * 128])
    diagb = din("diagb", [128, 4 * 128])
    maskd = din("maskd", [128, 128])
    b31 = din("b31", [1, 4])
    cflag = din("cflag", [128, 1])
    oflag = din("oflag", [128, 1])
    corr = din("corr", [128, 4 * 16])
    identd = din("identd", [128, 128])
    yout = nc.dram_tensor("yout", [2048, D], F32, kind="ExternalOutput").ap()

    base = (nc._sbuf_addr_for_side(None) + 63) // 64 * 64
    top = nc._sbuf_addr_for_side('right')
    cur = [base]
    names = [0]
    offs = {}

    def alloc(shape, dt, name=None):
        nbytes = int(np.prod(shape[1:])) * (4 if dt == F32 else 2)
        names[0] += 1
        offs[name] = cur[0]
        t = nc.alloc_sbuf_tensor_at((name or 't') + str(names[0]), list(shape), dt, offset=cur[0])
        cur[0] += (nbytes + 63) // 64 * 64
        assert cur[0] <= top, ("SBUF overflow", cur[0], top)
        return t

    P = Prog(nc)
    ev = [0]
    force_act = [0]

    def evac(out, in_, reads, writes, scale=None):
        ev[0] += 1
        use_act = (ev[0] % 2 == 0)
        if force_act[0] > 0:
            force_act[0] -= 1
            use_act = True
        if use_act:
            if scale is None:
                P.op('act', lambda e: e.activation(out=out, in_=in_, func=AF.Copy), reads, writes)
            else:
                P.op('act', lambda e: e.activation(out=out, in_=in_, func=AF.Copy, scale=scale), reads, writes)
        else:
            if scale is None:
                P.op('dve', lambda e: e.tensor_copy(out=out, in_=in_), reads, writes)
            else:
                P.op('dve', lambda e: e.tensor_scalar(out=out, in0=in_, scalar1=scale, scalar2=None, op0=ALU.mult), reads, writes)

    def mm(out, lhsT, rhs, start, stop, reads, writes, skip=False):
        if skip:
            P.op('pe', lambda e: e.matmul(out, lhsT=lhsT, rhs=rhs, start=start, stop=stop, skip_group_check=True), reads, writes)
        else:
            P.op('pe', lambda e: e.matmul(out, lhsT=lhsT, rhs=rhs, start=start, stop=stop), reads, writes)

    with contextlib.ExitStack() as ctx:
        ps = ctx.enter_context(nc.psum_tensor("ps", [128, 7, 512], F32))
        pst = ctx.enter_context(nc.psum_tensor("pst", [128, 1024], BF16))

        ident = alloc([128, 128], BF16, 'ident')
        identf = alloc([128, 128], F32, 'identf')
        subgbc = alloc([128, 128], F32, 'subgbc')
        lamt = alloc([128, 256], F32, 'lamt')
        lams = alloc([128, 8], F32, 'lams')
        psc = alloc([128, 4], F32, 'psc')
        cw = alloc([128, NJ * 3], F32, 'cw')
        cb = alloc([128, NJ], F32, 'cb')
        spec = alloc([128, 4, 4, 128], F32, 'spec')
        mskd = alloc([128, 128], F32, 'mskd')
        b31t = alloc([128, 4], F32, 'b31t')
        kbias = alloc([128, 8], F32, 'kbias')
        cfl = alloc([128, 1], F32, 'cfl')
        ofl = alloc([128, 1], F32, 'ofl')
        corrt = alloc([128, 4, 16], F32, 'corrt')
        epsn = alloc([128, 1], F32, 'epsn')
        epss = alloc([128, 1], F32, 'epss')
        stat = alloc([128, 16], F32, 'stat')
        junk = alloc([128, D], BF16, 'junk')
        poolwt = alloc([128, 4, 128], BF16, 'poolwt')
        hist = alloc([128, NJ, 2], F32, 'hist')
        yT = alloc([128, 8, 17 * 128], BF16, 'yT')

        def ld(q, key, out, in_, wr):
            P.dma(q, key, lambda e: e.dma_start(out=out, in_=in_), (), wr)

        ld('sp', 'c0', identf[:], identd, ['identf'])
        P.op('dve', lambda e: e.tensor_copy(out=ident[:], in_=identf[:]), ['identf'], ['ident'])
        P.op('dve', lambda e: e.memset(epsn[:], 1e-6), (), ['epsn'])
        P.op('dve', lambda e: e.memset(epss[:], 1e-5), (), ['epss'])
        P.op('dve', lambda e: e.memset(hist[:].rearrange("p j t -> p (j t)"), 0.0), (), ['hist'])
        P.op('dve', lambda e: e.memset(stat[:], 0.0), (), ['ss0', 'ss1', 'ss2', 'ss3', 'rstd', 'cfl8', 'fss', 'frs'])
        g1T = alloc([128, 8], F32, 'g1T')
        ld('sp', 'c1', g1T[:], g1t, ['g1T'])
        persist_end = cur[0]
        def setup_late():
            ld('sp', 'c3', subgbc[:], subg.broadcast_to([128, 128]), ['subgbc'])
            ld('sp', 'c4', lamt[:], lamv.broadcast_to([128, 256]), ['lamt'])
            ld('sp', 'c5', psc[:], pscale, ['psc'])
            ld('sp', 'c6', cw[:], convw, ['cw'])
            ld('sp', 'c7', cb[:], convb, ['cb'])
            ld('sp', 'c8', spec[:, 0, :, :].rearrange("p h q -> p (h q)"), nearb, ['spec0'])
            ld('sp', 'c9', spec[:, 1, :, :].rearrange("p h q -> p (h q)"), diagb, ['spec1'])
            ld('sp', 'c10', mskd[:], maskd, ['mskd'])
            ld('sp', 'c11', b31t[:], b31.broadcast_to([128, 4]), ['b31t'])
            ld('sp', 'c12', cfl[:], cflag, ['cfl'])
            ld('sp', 'c13', ofl[:], oflag, ['ofl'])
            ld('sp', 'c14', corrt[:].rearrange("p g t -> p (g t)"), corr, ['corrt'])
            ld('pool', 'c15', poolwt[:], pool_w.rearrange("g c d -> c g d"), ['poolwt'])

            P.op('dve', lambda e: e.tensor_scalar(out=subgbc[:], in0=subgbc[:], scalar1=1.0 - LAM_INIT, scalar2=None, op0=ALU.mult), ['subgbc'], ['subgbc'])
            P.op('dve', lambda e: e.tensor_tensor(out=lamt[:, 0:64], in0=lamt[:, 0:64], in1=lamt[:, 64:128], op=ALU.mult), ['lamt'], ['lamt'])
            P.op('dve', lambda e: e.tensor_tensor(out=lamt[:, 128:192], in0=lamt[:, 128:192], in1=lamt[:, 192:256], op=ALU.mult), ['lamt'], ['lamt'])
            P.op('dve', lambda e: e.reduce_sum(out=lams[:, 0:1], in_=lamt[:, 0:64], axis=mybir.AxisListType.X), ['lamt'], ['lams'])
            P.op('dve', lambda e: e.reduce_sum(out=lams[:, 1:2], in_=lamt[:, 128:192], axis=mybir.AxisListType.X), ['lamt'], ['lams'])
            P.op('act', lambda e: e.activation(out=lams[:, 2:4], in_=lams[:, 0:2], func=AF.Exp), ['lams'], ['lams'])
            P.op('dve', lambda e: e.tensor_tensor(out=lams[:, 4:5], in0=lams[:, 3:4], in1=lams[:, 2:3], op=ALU.subtract), ['lams'], ['lams'])
            P.op('dve', lambda e: e.tensor_scalar(out=lams[:, 4:5], in0=lams[:, 4:5], scalar1=-LAM_INIT, scalar2=None, op0=ALU.add), ['lams'], ['lams'])
            P.op('dve', lambda e: e.tensor_copy(out=kbias[:, 0:4], in_=b31t[:]), ['b31t'], ['kbias'])
            P.op('dve', lambda e: e.tensor_scalar(out=kbias[:, 4:8], in0=b31t[:], scalar1=cfl[:, 0:1], scalar2=None, op0=ALU.add), ['b31t', 'cfl', 'kbias'], ['kbias'])
            for h in range(NB):
                P.op('dve', lambda e, h=h: e.tensor_scalar(out=spec[:, 0, h, :], in0=spec[:, 0, h, :], scalar1=b31t[:, h:h + 1], scalar2=8.0, op0=ALU.subtract, op1=ALU.mult), ['spec0', 'b31t'], ['spec0'])
                P.op('dve', lambda e, h=h: e.tensor_scalar(out=spec[:, 1, h, :], in0=spec[:, 1, h, :], scalar1=b31t[:, h:h + 1], scalar2=None, op0=ALU.subtract), ['spec1', 'b31t'], ['spec1'])
                P.op('dve', lambda e, h=h: e.tensor_tensor(out=spec[:, 1, h, :], in0=spec[:, 1, h, :], in1=mskd[:], op=ALU.add), ['spec1', 'mskd'], ['spec1'])
                P.op('dve', lambda e, h=h: e.tensor_scalar(out=spec[:, 1, h, :], in0=spec[:, 1, h, :], scalar1=8.0, scalar2=None, op0=ALU.mult), ['spec1'], ['spec1'])
            P.op('dve', lambda e: e.tensor_scalar(out=stat[:, 8:9], in0=cfl[:, 0:1], scalar1=8.0, scalar2=None, op0=ALU.mult), ['cfl'], ['cfl8'])
            for ty in range(2):
                P.op('dve', lambda e, ty=ty: e.tensor_scalar(out=spec[:, 2 + ty, :, :].rearrange("p h q -> p (h q)"), in0=spec[:, ty, :, :].rearrange("p h q -> p (h q)"), scalar1=stat[:, 8:9], scalar2=None, op0=ALU.add), ['spec%d' % ty, 'cfl8'], ['spec%d' % (2 + ty)])


            return lams[:, 4:5]

        KT = alloc([128, NB, S], BF16, 'KT')
        V = alloc([128, 32, NB, 130], BF16, 'V')
        QT = alloc([128, NB, 17 * 128], BF16, 'QT')
        ab_end = cur[0]
        w_in_t = alloc([128, 8, 2048], BF16, 'w_in_t')
        xt = alloc([128, 2, D], F32, 'xt')
        hb = alloc([128, 2, D], BF16, 'hb')
        hTs = [alloc([128, 8, 512], BF16, 'hT%d' % i) for i in range(2)]
        zt = alloc([128, 4, 528], F32, 'zt')
        zs = alloc([128, 2, 528], F32, 'zs')
        pooled = alloc([128, 4, 512], BF16, 'pooled')
        print('phaseA end', cur[0], 'top', top)
        a_end = cur[0]

        def load_w_in(c4):
            ld('pool', 'win%d' % c4, w_in_t[:, :, c4 * 512:(c4 + 1) * 512],
               w_in.rearrange("(k p) n -> p k n", p=128)[:, :, c4 * 512:(c4 + 1) * 512], ['w_in:%d' % c4])

        load_w_in(2)
        load_w_in(3)
        P.op('pool', lambda e: e.memset(V[:].rearrange("p a b c -> p (a b c)"), 1.0), (), ['V'])
        P.op('pool', lambda e: e.memset(zt[:].rearrange("p a b -> p (a b)"), 0.0), (), ['zt:%d' % g for g in range(4)] + ['zth:%d' % g for g in range(4)])

        bank = [0]

        def nextbank(nbanks=7):
            b = bank[0] % nbanks
            bank[0] += 1
            return b

        def norm_rows(src, ns, gain_key, gain, out_bf, src_keys, out_keys, eps_t, dim):
            for s in range(ns):
                P.op('act', lambda e, s=s: e.activation(out=junk[:, 0:dim], in_=src(s), func=AF.Square, accum_out=stat[:, s:s + 1]),
                     src_keys(s), ['ss%d' % s, 'junk'])
            P.op('act', lambda e: e.activation(out=stat[:, 4:4 + ns], in_=stat[:, 0:ns], func=AF.Ln, scale=1.0 / dim, bias=eps_t[:]),
                 ['ss%d' % s for s in range(ns)] + ['epsn', 'epss'], ['rstd'])
            P.op('act', lambda e: e.activation(out=stat[:, 4:4 + ns], in_=stat[:, 4:4 + ns], func=AF.Exp, scale=-0.5), ['rstd'], ['rstd'])
            for s in range(ns):
                if gain is None:
                    P.op('dve', lambda e, s=s: e.tensor_scalar(out=out_bf(s), in0=src(s), scalar1=stat[:, 4 + s:5 + s], scalar2=None, op0=ALU.mult),
                         src_keys(s) + ['rstd'], out_keys(s))
                else:
                    P.op('dve', lambda e, s=s: e.scalar_tensor_tensor(out=out_bf(s), in0=src(s), scalar=stat[:, 4 + s:5 + s], in1=gain, op0=ALU.mult, op1=ALU.mult),
                         src_keys(s) + ['rstd', gain_key], out_keys(s))
            P.op('dve', lambda e: e.memset(stat[:, 0:4], 0.0), (), ['ss%d' % s for s in range(4)])

        def tr_round(src_bf, ns, dstT, col0, kp, src_keys, dst_key, gscale=None):
            if kp % 2 == 1:
                return
            q = kp // 2
            for kk in range(4):
                k = q * 4 + kk
                for s in range(ns):
                    P.op('pe', lambda e, k=k, kk=kk, s=s: e.transpose(pst[:, kk * 256 + s * 128: kk * 256 + (s + 1) * 128], src_bf(s)[:, k * 128:(k + 1) * 128], ident[:]),
                         src_keys(s) + ['ident'], ['pst'])
            if gscale is None:
                evac(dstT[:, 4 * q:4 * q + 4, col0:col0 + ns * 128],
                     pst[:].rearrange("p (a b) -> p a b", a=4)[:, :, 0:ns * 128], ['pst'], [dst_key])
            else:
                for kk in range(4):
                    k = q * 4 + kk
                    evac(dstT[:, k, col0:col0 + ns * 128], pst[:, kk * 256:kk * 256 + ns * 128], ['pst', 'g1T'], [dst_key],
                         scale=gscale[:, k:k + 1])

        def a_norm(t, half):
            r0 = t * 512 + half * 256
            P.dma('sp', 'xt', lambda e: e.dma_start(out=xt[:], in_=xin[r0:r0 + 256, :].rearrange("(s p) d -> p s d", p=128)),
                  (), ['xt:%d' % s for s in range(2)])
            norm_rows(lambda s: xt[:, s, :], 2, None, None, lambda s: hb[:, s, :],
                      lambda s: ['xt:%d' % s], lambda s: ['hb:%d' % s], epsn, D)

        def a_round(t, r):
            half, kp = r // 4, r % 4
            tr_round(lambda s: hb[:, s, :], 2, hTs[t % 2], half * 256, kp, lambda s: ['hb:%d' % s], 'hT%d' % (t % 2), gscale=g1T)

        def a_groups(t):
            own = t >= 4
            full = t >= 3
            hT = hTs[t % 2]
            hk = 'hT%d' % (t % 2)
            grps = []

            def chunk(m):
                b = nextbank()
                n0 = 0 if (own or m >= 8) else (256 if m < 4 else 384)
                for k in range(8):
                    mm(ps[:, b, n0:512], w_in_t[:, k, m * 128:(m + 1) * 128], hT[:, k, n0:512], k == 0, k == 7,
                       ['w_in:%d' % (m // 4), hk], ['ps%d' % b])
                if m < 4:
                    P.op('act', lambda e: e.activation(out=zt[:, m, 16 + n0:528], in_=ps[:, b, n0:512], func=AF.Copy), ['ps%d' % b], ['zt:%d' % m])
                elif m < 8:
                    h = m - 4
                    if own:
                        c0 = (16 + 4 * (t - 4) - 15) * 128
                        evac(QT[:, h, c0:c0 + 512], ps[:, b, :], ['ps%d' % b], ['QT'])
                    else:
                        evac(QT[:, h, 0:128], ps[:, b, 384:512], ['ps%d' % b], ['QT'])
                else:
                    h = m - 8
                    evac(KT[:, h, t * 512:(t + 1) * 512], ps[:, b, :], ['ps%d' % b], ['KT'])

            def vgrp(s):
                b = nextbank()
                for k in range(8):
                    mm(ps[:, b, :], hT[:, k, s * 128:(s + 1) * 128], w_in_t[:, k, 1536:2048], k == 0, k == 7,
                       ['w_in:3', hk], ['ps%d' % b])
                evac(V[:, t * 4 + s, :, 0:128], ps[:, b, :].rearrange("p (h e) -> p h e", h=NB), ['ps%d' % b], ['V'])

            if full:
                for m in range(0, 4):
                    grps.append(lambda m=m: chunk(m))
            for m in range(8, 12):
                grps.append(lambda m=m: chunk(m))
            for s in range(4):
                grps.append(lambda s=s: vgrp(s))
            if full:
                for m in range(4, 8):
                    grps.append(lambda m=m: chunk(m))
            return grps

        def a_pool(t):
            own = t >= 4
            if t == 4:
                P.op('dve', lambda e: e.tensor_scalar(out=zt[:, :, 0:16], in0=zt[:, :, 0:16], scalar1=ofl[:, 0:1], scalar2=None, op0=ALU.mult),
                     ['zth:%d' % g for g in range(4)] + ['ofl'], ['zth:%d' % g for g in range(4)])
            for g in range(4):
                zk = 'zt:%d' % g
                z = zt[:, g, :]
                P.op('dve', lambda e, z=z: e.tensor_tensor(out=zs[:, 0, 1:528], in0=z[:, 1:528], in1=z[:, 0:527], op=ALU.add), [zk, 'zth:%d' % g, 'zs0'], ['zs0'])
                curi = 0
                sh = 2
                for step in range(g):
                    nxt = 1 - curi
                    P.op('dve', lambda e, curi=curi, nxt=nxt, sh=sh: e.tensor_tensor(out=zs[:, nxt, 2 * sh - 1:528], in0=zs[:, curi, 2 * sh - 1:528], in1=zs[:, curi, sh - 1:528 - sh], op=ALU.add),
                         ['zs%d' % curi, 'zs%d' % nxt], ['zs%d' % nxt])
                    curi = nxt
                    sh *= 2
                w = 2 ** (g + 1)
                if t == 4:
                    P.op('dve', lambda e, curi=curi, g=g: e.tensor_tensor(out=zs[:, curi, 16:32], in0=zs[:, curi, 16:32], in1=corrt[:, g, :], op=ALU.mult),
                         ['zs%d' % curi, 'corrt'], ['zs%d' % curi])
                pg_ = g
                pk_ = 'pooled:%d' % pg_
                P.op('dve', lambda e, curi=curi, z=z, w=w, pg_=pg_: e.scalar_tensor_tensor(out=pooled[:, pg_, :], in0=zs[:, curi, 16:528], scalar=1.0 / w, in1=z[:, 16:528], op0=ALU.mult, op1=ALU.subtract),
                     ['zs%d' % curi, zk, pk_], [pk_])
                P.op('dve', lambda e, z=z: e.tensor_copy(out=z[:, 0:16], in_=z[:, 512:528]), [zk], ['zth:%d' % g])
            force_act[0] = 6

        def a_pool_mm(t):
            own = t >= 4
            for g in range(4):
                pk_ = 'pooled:%d' % g
                b = nextbank()
                if own:
                    c0 = (16 + 4 * (t - 4) - 15) * 128
                    mm(ps[:, b, :], poolwt[:, g, :], pooled[:, g, :], True, True, ['poolwt', pk_], ['ps%d' % b])
                    evac(yT[:, g, c0:c0 + 512], ps[:, b, :], ['ps%d' % b, 'psc'], ['yT:%d' % g], scale=psc[:, g:g + 1])
                else:
                    mm(ps[:, b, 0:128], poolwt[:, g, :], pooled[:, g, 384:512], True, True, ['poolwt', pk_], ['ps%d' % b])
                    evac(yT[:, g, 0:128], ps[:, b, 0:128], ['ps%d' % b, 'psc'], ['yT:%d' % g], scale=psc[:, g:g + 1])

        a_norm(0, 0)
        for r in range(4):
            a_round(0, r)
        a_norm(0, 1)
        for r in range(4, 8):
            a_round(0, r)
        neglam = setup_late()
        for t in range(8):
            grps = a_groups(t)
            nxt = t + 1 < 8
            if nxt:
                a_norm(t + 1, 0)
            for i, gfn in enumerate(grps):
                gfn()
                if nxt:
                    if i == 0:
                        a_round(t + 1, 0)
                        a_round(t + 1, 1)
                    elif i == 1:
                        a_round(t + 1, 2)
                        a_round(t + 1, 3)
                        a_norm(t + 1, 1)
                    elif 4 <= i < 8:
                        a_round(t + 1, i)
                if t >= 3 and i == 7:
                    a_pool(t)
            if t == 0:
                load_w_in(0)
                load_w_in(1)
            if t >= 3:
                a_pool_mm(t)

        cur[0] = ab_end
        w_out_t = alloc([128, 8, D], BF16, 'w_out_t')
        ring = [alloc([128, 8, 256], BF16, 'ring%d' % i) for i in range(4)]
        bc_start = cur[0]
        Pt = [alloc([128, 2, 512], BF16, 'Pt%d' % i) for i in range(2)]
        ya = alloc([128, 4, 512], BF16, 'ya')
        accs = alloc([128, 3, 512], F32, 'accs')
        oall = alloc([128, 4, 128], F32, 'oall')
        sqall = alloc([128, 4, 128], F32, 'sqall')
        stb = alloc([128, 16], F32, 'stb')
        phaseA_bufs = ['w_in', 'xt', 'hb', 'hT0', 'hT1', 'zt', 'zth', 'zs0', 'zs1', 'pooled']
        phaseB_bufs = ['Pt0', 'Pt1', 'ya', 'accs', 'oall:0', 'oall:1', 'oall:2', 'oall:3', 'sqall:0', 'sqall:1', 'sqall:2', 'sqall:3', 'rs', 'sss', 'srs']
        P.alias(['w_out', 'Pt0', 'Pt1', 'ya:0', 'ya:1', 'ya:2', 'ya:3', 'accs', 'oall:0', 'oall:1', 'oall:2', 'oall:3', 'sqall:0', 'sqall:1', 'sqall:2', 'sqall:3', 'rs', 'sss', 'srs'] + ['ring%d' % i for i in range(4)], phaseA_bufs)
        ld('pool', 'wout', w_out_t[:], w_out.rearrange("(k p) n -> p k n", p=128), ['w_out'])

        tiles = [(15, 1)] + [(16 + 4 * i, 4) for i in range(4)]
        ring_seq = [(ti, j) for ti in range(1, len(tiles)) for j in range(NJ)]

        def issue_ring(n):
            if n >= len(ring_seq):
                return
            ti, j = ring_seq[n]
            ri = n % 4
            rk = 'ring%d' % ri
            P.dma('pool', rk, lambda e: e.dma_start(out=ring[ri][:].rearrange("p k n -> p (k n)"), in_=wgu[j]), (), [rk])

        for n in range(4):
            issue_ring(n)

        groups = [(15, 1)] + [(16 + 4 * i, 4) for i in range(4)]
        ACC_BANK0 = 4
        HQ = 32
        for i_ in range(2):
            P.op('dve', lambda e, i_=i_: e.memset(Pt[i_][:].rearrange("p a b -> p (a b)"), 0.0), (), ['Pt%d' % i_])
        iters = []
        for (g0, nq) in groups:
            for h in range(NB):
                for kb in range(g0 + nq):
                    iters.append((g0, nq, h, kb))

        def emit_qk(idx):
            g0, nq, h, kb = iters[idx]
            qc0 = (g0 - 15) * 128
            a = max(0, kb - g0)
            n0, n1 = a * 128, nq * 128
            if nq == 1:
                n0 = 128 - HQ
            sb = 2 * (idx % 2)
            skeys = ['ps%d' % sb, 'ps%d' % (sb + 1)]
            for c in range(2):
                mm(ps[:, sb + c, n0:n1], KT[c * 64:(c + 1) * 64, h, kb * 128:(kb + 1) * 128],
                   QT[c * 64:(c + 1) * 64, h, qc0 + n0:qc0 + n1], True, True, ['KT', 'QT'], [skeys[c]])
            ctxk = kb < 16
            for qi in range(a, nq):
                d = g0 + qi - kb
                if d > 1:
                    continue
                ty = (0 if d == 1 else 1) + (2 if ctxk else 0)
                q0 = n0 if nq == 1 else qi * 128
                for c in range(2):
                    P.op('dve', lambda e, c=c, qi=qi, ty=ty, q0=q0: e.tensor_tensor(
                        out=ps[:, sb + c, q0:(qi + 1) * 128], in0=ps[:, sb + c, q0:(qi + 1) * 128],
                        in1=spec[:, ty, h, q0 - qi * 128:128], op=ALU.add), [skeys[c], 'spec%d' % ty], [skeys[c]])

        def emit_exp(idx):
            g0, nq, h, kb = iters[idx]
            a = max(0, kb - g0)
            n0, n1 = a * 128, nq * 128
            if nq == 1:
                n0 = 128 - HQ
            si = idx % 2
            sb = 2 * si
            skeys = ['ps%d' % sb, 'ps%d' % (sb + 1)]
            ctxk = kb < 16
            bc = 4 + h if ctxk else h
            bcol = kbias[:, bc:bc + 1]
            pk = 'Pt%d' % si
            P.op('act', lambda e: e.activation(
                out=Pt[si][:, :, n0:n1], in_=ps[:, sb:sb + 2, n0:n1], func=AF.Exp, bias=bcol, scale=0.125),
                skeys + ['kbias'], [pk])

        def emit_av(idx):
            g0, nq, h, kb = iters[idx]
            a = max(0, kb - g0)
            si = idx % 2
            pk = 'Pt%d' % si
            for qi in range(a, nq):
                for c in range(2):
                    ai = qi * 2 + c
                    ab = ACC_BANK0 + ai // 3
                    ac = (ai % 3) * 160
                    first_in_bank = (ai % 3 == 0)
                    mm(ps[:, ab, ac:ac + 129], Pt[si][:, c, qi * 128:(qi + 1) * 128], V[:, kb, h, 0:129],
                       (kb == 0 and first_in_bank), kb == g0 + qi, [pk, 'V'], ['ps%d' % ab], skip=True)

        def emit_head_end(g0, nq, h):
            nbk = (2 * nq + 2) // 3
            for bk in range(nbk):
                P.op('dve', lambda e, bk=bk: e.tensor_copy(out=accs[:, bk, 0:480], in_=ps[:, ACC_BANK0 + bk, 0:480]),
                     ['ps%d' % (ACC_BANK0 + bk), 'accs'], ['accs'])

            def acc(ai):
                return accs[:, ai // 3, (ai % 3) * 160:(ai % 3) * 160 + 129]
            for ai in range(2 * nq):
                P.op('dve', lambda e, ai=ai: e.tensor_scalar(out=stb[:, ai:ai + 1], in0=acc(ai)[:, 128:129], scalar1=1e-30, scalar2=None, op0=ALU.max),
                     ['accs', 'rs'], ['rs'])
            P.op('dve', lambda e: e.reciprocal(out=stb[:, 0:2 * nq], in_=stb[:, 0:2 * nq]), ['rs'], ['rs'])
            for qi in range(nq):
                P.op('dve', lambda e, qi=qi: e.tensor_tensor(out=stb[:, 2 * qi + 1:2 * qi + 2], in0=stb[:, 2 * qi + 1:2 * qi + 2], in1=neglam, op=ALU.mult), ['rs', 'lams'], ['rs'])
            for qi in range(nq):
                P.op('dve', lambda e, qi=qi: e.tensor_scalar(out=sqall[:, qi, :], in0=acc(2 * qi + 1)[:, 0:128], scalar1=stb[:, 2 * qi + 1:2 * qi + 2], scalar2=None, op0=ALU.mult),
                     ['accs', 'rs', 'sqall:%d' % qi], ['sqall:%d' % qi])
                P.op('dve', lambda e, qi=qi: e.scalar_tensor_tensor(out=oall[:, qi, :], in0=acc(2 * qi)[:, 0:128], scalar=stb[:, 2 * qi:2 * qi + 1], in1=sqall[:, qi, :], op0=ALU.mult, op1=ALU.add),
                     ['accs', 'rs', 'sqall:%d' % qi, 'oall:%d' % qi], ['oall:%d' % qi])
            sqk = ['sqall:%d' % qi for qi in range(nq)]
            oak = ['oall:%d' % qi for qi in range(nq)]
            P.op('dve', lambda e: e.tensor_tensor(out=sqall[:, 0:nq, :], in0=oall[:, 0:nq, :], in1=oall[:, 0:nq, :], op=ALU.mult), oak + sqk, sqk)
            P.op('dve', lambda e: e.reduce_sum(out=stb[:, 8:8 + nq], in_=sqall[:, 0:nq, :], axis=mybir.AxisListType.X), sqk, ['sss'])

        def emit_head_end2(g0, nq, h):
            P.op('act', lambda e: e.activation(out=stb[:, 12:12 + nq], in_=stb[:, 8:8 + nq], func=AF.Ln, scale=1.0 / 128, bias=epss[:]), ['sss', 'epss'], ['srs'])
            P.op('act', lambda e: e.activation(out=stb[:, 12:12 + nq], in_=stb[:, 12:12 + nq], func=AF.Exp, scale=-0.5), ['srs'], ['srs'])
            for qi in range(nq):
                P.op('dve', lambda e, qi=qi: e.scalar_tensor_tensor(out=ya[:, qi, h * 128:(h + 1) * 128], in0=oall[:, qi, :], scalar=stb[:, 12 + qi:13 + qi], in1=subgbc[:], op0=ALU.mult, op1=ALU.mult),
                     ['oall:%d' % qi, 'srs', 'subgbc'], ['ya:%d' % qi])

        def emit_group_end(g0, nq):
            qc0 = (g0 - 15) * 128
            for q0 in range(0, nq, 2):
                nj = min(2, nq - q0)
                for j in range(nj):
                    for h in range(NB):
                        P.op('pe', lambda e, j=j, h=h, q0=q0: e.transpose(pst[:, h * 256 + j * 128:h * 256 + (j + 1) * 128], ya[:, q0 + j, h * 128:(h + 1) * 128], ident[:]),
                             ['ya:%d' % (q0 + j), 'ident'], ['pst'])
                cc = qc0 + q0 * 128
                P.op('dve', lambda e, cc=cc, nj=nj: e.tensor_copy(out=yT[:, 4:8, cc:cc + nj * 128],
                                                                 in_=pst[:].rearrange("p (h jq) -> p h jq", h=NB)[:, :, 0:nj * 128]),
                     ['pst'], ['yT:att'])

        emit_qk(0)
        emit_qk(1)
        pending = []
        for idx in range(len(iters)):
            emit_exp(idx)
            if idx + 2 < len(iters):
                emit_qk(idx + 2)
            emit_av(idx)
            while pending and pending[0][0] <= idx:
                pending.pop(0)[1]()
            g0, nq, h, kb = iters[idx]
            if kb == g0 + nq - 1:
                emit_head_end(g0, nq, h)

                def part2(g0=g0, nq=nq, h=h):
                    emit_head_end2(g0, nq, h)
                    if h == NB - 1:
                        emit_group_end(g0, nq)
                pending.append((idx + 10, part2))
        while pending:
            pending.pop(0)[1]()

        cur[0] = persist_end
        w_fo_t = alloc([128, NJ, D], BF16, 'w_fo_t')
        actT = alloc([128, NJ, 512], BF16, 'actT')
        h2T = alloc([128, 8, 512], BF16, 'h2T')
        h2Th = alloc([128, 8, 128], BF16, 'h2Th')
        cbuf = [alloc([128, 512], F32, 'cbuf%d' % i) for i in range(2)]
        assert cur[0] <= ab_end, (cur[0], ab_end)
        cur[0] = bc_start
        xt2s = [alloc([128, 4, D], F32, 'xt2_%d' % i) for i in range(2)]
        h2 = alloc([128, 2, D], BF16, 'h2')
        sgb = [alloc([128, 512], F32, 'sgb%d' % i) for i in range(2)]
        save2 = cur[0]
        cur[0] = offs['spec']
        g2bc = alloc([128, D], F32, 'g2bc')
        gfbc = alloc([128, D], F32, 'gfbc')
        assert cur[0] <= offs['mskd'], (cur[0], offs['mskd'])
        cur[0] = save2
        newC = (['xt2_%d:%d' % (b, i) for b in range(2) for i in range(4)] + ['h2:0', 'h2:1', 'g2bc', 'gfbc', 'w_fo:0', 'w_fo:1', 'w_fo:2', 'w_fo:3', 'actT', 'h2T', 'h2Th',
                'cbuf0', 'cbuf1', 'sgb0', 'sgb1'])
        P.alias(newC, phaseA_bufs + phaseB_bufs + ['KT', 'V', 'QT', 'spec0', 'spec1', 'spec2', 'spec3'])
        ld('sp', 'g2', g2bc[:], g2.broadcast_to([128, D]), ['g2bc'])
        ld('sp', 'c2', gfbc[:], gf.broadcast_to([128, D]), ['gfbc'])

        def c_load(ti):
            b0, nb = tiles[ti]
            xt2 = xt2s[ti % 2]
            xk = 'xt2_%d' % (ti % 2)
            P.dma('sp', xk, lambda e: e.dma_start(out=xt2[:, 0:nb, :], in_=xin[b0 * 128:(b0 + nb) * 128, :].rearrange("(s p) d -> p s d", p=128)),
                  (), [xk + ':%d' % s for s in range(nb)])

        def c_outproj(ti):
            b0, nb = tiles[ti]
            xt2 = xt2s[ti % 2]
            xk = 'xt2_%d' % (ti % 2)
            yc0 = (b0 - 15) * 128
            for s in range(nb):
                for n2 in range(2):
                    b = nextbank()
                    for k in range(8):
                        yk = 'yT:%d' % k if k < 4 else 'yT:att'
                        mm(ps[:, b, :], yT[:, k, yc0 + s * 128: yc0 + (s + 1) * 128], w_out_t[:, k, n2 * 512:(n2 + 1) * 512], k == 0, k == 7,
                           [yk, 'w_out'], ['ps%d' % b])
                    P.op('dve', lambda e, s=s, n2=n2, b=b: e.tensor_tensor(out=xt2[:, s, n2 * 512:(n2 + 1) * 512], in0=ps[:, b, :], in1=xt2[:, s, n2 * 512:(n2 + 1) * 512], op=ALU.add),
                         ['ps%d' % b, xk + ':%d' % s], [xk + ':%d' % s])

        def c_norm(ti, half):
            b0, nb = tiles[ti]
            xt2 = xt2s[ti % 2]
            xk = 'xt2_%d' % (ti % 2)
            ns = min(2, nb - 2 * half)
            if ns <= 0:
                return
            norm_rows(lambda s: xt2[:, 2 * half + s, :], ns, 'g2bc', g2bc[:], lambda s: h2[:, s, :],
                      lambda s: [xk + ':%d' % (2 * half + s)], lambda s: ['h2:%d' % s], epsn, D)

        def c_round(ti, r):
            b0, nb = tiles[ti]
            half, kp = r // 4, r % 4
            ns = min(2, nb - 2 * half)
            if ns <= 0:
                return
            if ti == 0:
                tr_round(lambda s: h2[:, s, :], ns, h2Th, 0, kp, lambda s: ['h2:%d' % s], 'h2Th')
            else:
                tr_round(lambda s: h2[:, s, :], ns, h2T, half * 256, kp, lambda s: ['h2:%d' % s], 'h2T')

        ring_n = [0]

        WFO_PIECES = [(0, 6), (6, 12), (12, 17), (17, 22)]

        def wfo_piece(j):
            for i, (a_, b_) in enumerate(WFO_PIECES):
                if a_ <= j < b_:
                    return i

        def c_ffn_in(ti):
            b0, nb = tiles[ti]
            N = nb * 128
            for j in range(NJ):
                if ti == 1 and j in (1, 6, 11, 16):
                    pc = (1, 6, 11, 16).index(j)
                    a_, b_ = WFO_PIECES[pc]
                    ld('pool', 'wfo%d' % pc, w_fo_t[:, a_:b_, :],
                       w_fo.rearrange("(j p) n -> p j n", p=128)[:, a_:b_, :], ['w_fo:%d' % pc])
                n = ring_n[0]
                ring_n[0] += 1
                ri = n % 4
                rk = 'ring%d' % ri
                if ti == 1:
                    bh = nextbank()
                    for k in range(8):
                        mm(ps[:, bh, 0:2], ring[ri][:, k, 0:128], h2Th[:, k, 126:128], k == 0, k == 7, [rk, 'h2Th'], ['ps%d' % bh])
                    P.op('act', lambda e, bh=bh, j=j: e.activation(out=hist[:, j, :], in_=ps[:, bh, 0:2], func=AF.Copy, scale=ofl[:, 0:1]),
                         ['ps%d' % bh, 'hist', 'ofl'], ['hist'])
                bg = nextbank()
                for k in range(8):
                    mm(ps[:, bg, 0:N], ring[ri][:, k, 0:128], h2T[:, k, 0:N], k == 0, k == 7, [rk, 'h2T'], ['ps%d' % bg])
                bu = nextbank()
                for k in range(8):
                    mm(ps[:, bu, 0:N], ring[ri][:, k, 128:256], h2T[:, k, 0:N], k == 0, k == 7, [rk, 'h2T'], ['ps%d' % bu])
                issue_ring(n + 4)
                ci = j % 2
                ck = 'cbuf%d' % ci
                cbt = cbuf[ci]
                pg = ps[:, bg, :]
                P.op('act', lambda e, cbt=cbt, pg=pg, j=j: e.activation(out=cbt[:, 0:N], in_=pg[:, 0:N], func=AF.Identity, scale=cw[:, 3 * j + 2:3 * j + 3], bias=cb[:, j:j + 1]),
                     ['ps%d' % bg, 'cw', 'cb', ck], [ck])
                P.op('dve', lambda e, cbt=cbt, pg=pg, j=j: e.scalar_tensor_tensor(out=cbt[:, 1:N], in0=pg[:, 0:N - 1], scalar=cw[:, 3 * j + 1:3 * j + 2], in1=cbt[:, 1:N], op0=ALU.mult, op1=ALU.add),
                     ['ps%d' % bg, 'cw', ck], [ck])
                P.op('dve', lambda e, cbt=cbt, pg=pg, j=j: e.scalar_tensor_tensor(out=cbt[:, 2:N], in0=pg[:, 0:N - 2], scalar=cw[:, 3 * j:3 * j + 1], in1=cbt[:, 2:N], op0=ALU.mult, op1=ALU.add),
                     ['ps%d' % bg, 'cw', ck], [ck])
                P.op('dve', lambda e, cbt=cbt, j=j: e.scalar_tensor_tensor(out=cbt[:, 0:1], in0=hist[:, j, 1:2], scalar=cw[:, 3 * j + 1:3 * j + 2], in1=cbt[:, 0:1], op0=ALU.mult, op1=ALU.add),
                     ['hist', 'cw', ck], [ck])
                P.op('dve', lambda e, cbt=cbt, j=j: e.scalar_tensor_tensor(out=cbt[:, 0:2], in0=hist[:, j, 0:2], scalar=cw[:, 3 * j:3 * j + 1], in1=cbt[:, 0:2], op0=ALU.mult, op1=ALU.add),
                     ['hist', 'cw', ck], [ck])
                P.op('act', lambda e, pg=pg, j=j: e.activation(out=hist[:, j, :], in_=pg[:, N - 2:N], func=AF.Copy), ['ps%d' % bg, 'hist', ck], ['hist'])
                sk = 'sgb%d' % ci
                sgt = sgb[ci]
                P.op('act', lambda e, sgt=sgt, cbt=cbt: e.activation(out=sgt[:, 0:N], in_=cbt[:, 0:N], func=AF.Silu), [ck, sk], [sk])
                P.op('dve', lambda e, sgt=sgt, bu=bu, j=j: e.tensor_tensor(out=actT[:, j, 0:N], in0=ps[:, bu, 0:N], in1=sgt[:, 0:N], op=ALU.mult),
                     ['ps%d' % bu, sk, 'actT'], ['actT'])

        def c_ffn_out_group(ti, s, n2):
            xt2 = xt2s[ti % 2]
            xk = 'xt2_%d' % (ti % 2)
            b = nextbank()
            for j in range(NJ):
                mm(ps[:, b, :], actT[:, j, s * 128:(s + 1) * 128], w_fo_t[:, j, n2 * 512:(n2 + 1) * 512], j == 0, j == NJ - 1,
                   ['actT', 'w_fo:%d' % wfo_piece(j)], ['ps%d' % b])
            P.op('dve', lambda e: e.tensor_tensor(out=xt2[:, s, n2 * 512:(n2 + 1) * 512], in0=ps[:, b, :], in1=xt2[:, s, n2 * 512:(n2 + 1) * 512], op=ALU.add),
                 ['ps%d' % b, xk + ':%d' % s], [xk + ':%d' % s])

        def c_final_s(ti, s):
            b0, nb = tiles[ti]
            xt2 = xt2s[ti % 2]
            xk = 'xt2_%d' % (ti % 2)
            P.op('act', lambda e: e.activation(out=junk[:], in_=xt2[:, s, :], func=AF.Square, accum_out=stat[:, 14:15]),
                 [xk + ':%d' % s], ['fss', 'junk'])
            P.op('act', lambda e: e.activation(out=stat[:, 15:16], in_=stat[:, 14:15], func=AF.Ln, scale=1.0 / D, bias=epsn[:]),
                 ['fss', 'epsn'], ['frs'])
            P.op('act', lambda e: e.activation(out=stat[:, 15:16], in_=stat[:, 15:16], func=AF.Exp, scale=-0.5), ['frs'], ['frs'])
            P.op('dve', lambda e: e.scalar_tensor_tensor(out=xt2[:, s, :], in0=xt2[:, s, :], scalar=stat[:, 15:16], in1=gfbc[:], op0=ALU.mult, op1=ALU.mult),
                 [xk + ':%d' % s, 'frs', 'gfbc'], [xk + ':%d' % s])
            P.op('dve', lambda e: e.memset(stat[:, 14:15], 0.0), (), ['fss'])
            o0 = (b0 - 16) * 128 + s * 128
            P.dma('sp', 'out%d' % (ti % 2), lambda e: e.dma_start(out=yout[o0:o0 + 128, :], in_=xt2[:, s, :]),
                  [xk + ':%d' % s], ['yout%d' % (ti % 2)])

        nt = len(tiles)
        c_load(0)
        c_outproj(0)
        c_norm(0, 0)
        for r in range(4):
            c_round(0, r)
        c_load(1)
        c_outproj(1)
        c_norm(1, 0)
        for r in range(4):
            c_round(1, r)
        c_norm(1, 1)
        for r in range(4, 8):
            c_round(1, r)
        for ti in range(1, nt):
            nxt = ti + 1 < nt
            if nxt:
                c_load(ti + 1)
            c_ffn_in(ti)
            if nxt:
                c_outproj(ti + 1)
                c_norm(ti + 1, 0)
            for g in range(8):
                c_ffn_out_group(ti, g // 2, g % 2)
                if nxt:
                    if g == 4:
                        c_norm(ti + 1, 1)
                    c_round(ti + 1, g)
                if g % 2 == 1:
                    c_final_s(ti, g // 2)
        P.finish('sp', ['yout0', 'yout1'])
        P.emit(ctx)
    return nc


def _bucket(n):
    n = np.maximum(n, 0)
    nf = np.maximum(n, 1).astype(np.float32)
    large = 16 + (np.log(nf / np.float32(16)) / np.float32(math.log(128 / 16)) * np.float32(16)).astype(np.int32)
    large = np.minimum(large, 31)
    return np.where(n < 16, n, large)


_NC_CACHE = {}


def kernel(x, norm_mix_g, w_in, pool_w, pool_scale, lambda_q1, lambda_k1, lambda_q2, lambda_k2,
           subln_g, rel_bias, w_out, norm_ffn_g, ffn_w_in, ffn_conv_w, ffn_conv_b, ffn_w_out, norm_final_g):
    f = lambda a: np.ascontiguousarray(np.asarray(a, dtype=np.float32))
    x = f(x)
    rel_bias = f(rel_bias)
    kk = np.arange(128)[:, None]
    qq = np.arange(128)[None, :]
    near_idx = _bucket(128 + qq - kk)
    diag_idx = _bucket(qq - kk)
    nearb = np.ascontiguousarray(np.transpose(rel_bias[near_idx], (0, 2, 1)).reshape(128, 512))
    diagb = np.ascontiguousarray(np.transpose(rel_bias[diag_idx], (0, 2, 1)).reshape(128, 512))
    maskd = np.where(kk <= qq, 0.0, NEGM).astype(np.float32)
    fw = f(ffn_w_in)[0]
    wg = fw[:, :DFF].reshape(8, 128, NJ, 128)
    wu = fw[:, DFF:].reshape(8, 128, NJ, 128)
    wgu = np.ascontiguousarray(np.concatenate([wg, wu], axis=3).transpose(2, 1, 0, 3).reshape(NJ, 128, 8 * 256))
    convw = np.ascontiguousarray(f(ffn_conv_w)[0].reshape(3, NJ, 128).transpose(2, 1, 0).reshape(128, NJ * 3))
    convb = np.ascontiguousarray(f(ffn_conv_b)[0].reshape(NJ, 128).T)
    shared = {
        "w_in": f(w_in)[0], "w_out": f(w_out)[0], "wgu": wgu, "w_fo": f(ffn_w_out)[0],
        "pool_w": f(pool_w)[0], "g1t": np.ascontiguousarray(f(norm_mix_g).reshape(8, 128).T), "g2": f(norm_ffn_g), "gf": f(norm_final_g).reshape(1, D),
        "subg": f(subln_g), "lamv": np.concatenate([f(lambda_q1), f(lambda_k1), f(lambda_q2), f(lambda_k2)], axis=1),
        "pscale": np.ascontiguousarray(f(pool_scale)[0].reshape(4, 128).T), "convw": convw, "convb": convb,
        "nearb": nearb, "diagb": diagb, "maskd": maskd, "b31": np.ascontiguousarray(rel_bias[31:32, :]),
        "identd": np.eye(128, dtype=np.float32),
    }
    tpos = np.arange(16)
    corrA = np.stack([w / np.minimum(tpos + 1, w) for w in (2, 4, 8, 16)], 0).astype(np.float32)
    in_maps = []
    for c in range(8):
        b, role = c // 2, c % 2
        m = dict(shared)
        if role == 0:
            m["xin"] = np.ascontiguousarray(np.concatenate([x[b, 2048:], x[b, :2048]], axis=0))
            m["cflag"] = np.full((128, 1), NEGM, np.float32)
            m["oflag"] = np.zeros((128, 1), np.float32)
            m["corr"] = np.ascontiguousarray(np.broadcast_to(corrA.reshape(1, 64), (128, 64)))
        else:
            m["xin"] = x[b]
            m["cflag"] = np.zeros((128, 1), np.float32)
            m["oflag"] = np.ones((128, 1), np.float32)
            m["corr"] = np.ones((128, 64), np.float32)
        in_maps.append(m)
    if "nc" not in _NC_CACHE:
        _NC_CACHE["nc"] = build_nc()
    res = run_bass_kernel_spmd(_NC_CACHE["nc"], in_maps, core_ids=list(range(8)))
    out = np.empty((4, S, D), np.float32)
    for c in range(8):
        b, role = c // 2, c % 2
        out[b, role * 2048:(role + 1) * 2048] = res.results[c]["yout"]
    return out
```
